# Optimizing a Trainium2 kernel written in Bass

```python
import math
import numpy as np
import jax
import jax.numpy as jnp
from jax import lax

D_MODEL = 1024
BATCH = 4
SEQ = 8192
DEPTH = 2

GRID_W = 64
CTX_LEN = 256
WIN_HEADS = 8
WIN_KV_HEADS = 2
WIN_HEAD_DIM = 64
WINDOW = 128
BLOCK = 128
HG_HEADS = 4
HG_KEY_DIM = 64
HG_VAL_DIM = 64
CHUNK = 64
DIFF_HEADS = 4
DIFF_QK_DIM = 32
DIFF_V_DIM = 64
MIX_WIDTH = WIN_HEADS * WIN_HEAD_DIM + HG_HEADS * HG_VAL_DIM + DIFF_HEADS * DIFF_V_DIM
D_FF = 11 * D_MODEL // 4
CONV_WIDTH = 3
ROPE_BASE = 10000.0
EPS = 1e-6
SPLIT_SIZES = (
    WIN_HEADS * WIN_HEAD_DIM, WIN_KV_HEADS * WIN_HEAD_DIM, WIN_KV_HEADS * WIN_HEAD_DIM,
    HG_HEADS * HG_KEY_DIM, HG_HEADS * HG_KEY_DIM,
    HG_HEADS * HG_KEY_DIM, HG_HEADS * HG_KEY_DIM,
    HG_HEADS * HG_VAL_DIM, HG_HEADS * HG_VAL_DIM,
    DIFF_HEADS * 2 * DIFF_QK_DIM, DIFF_HEADS * 2 * DIFF_QK_DIM, DIFF_HEADS * DIFF_V_DIM,
)
IN_WIDTH = sum(SPLIT_SIZES)

kernel_name = 'hymba_style_hybrid_diffusion_trunk'

F32 = jnp.float32


def rms_norm(x, g):
    xf = x.astype(F32)
    y = xf * lax.rsqrt(jnp.mean(xf * xf, axis=-1, keepdims=True) + EPS)
    return (y * g.astype(F32)).astype(x.dtype)


def split_columns(p):
    idx = np.cumsum(SPLIT_SIZES)[:-1].tolist()
    return jnp.split(p, idx, axis=-1)


def adaln(vec, w, b):
    m = jax.nn.silu(vec) @ w + b
    return [t.reshape(-1, 1, D_MODEL) for t in jnp.split(m, 6, axis=-1)]


def axial_rope_tables(L, dim):
    rows = L // GRID_W
    row = jnp.repeat(jnp.arange(rows, dtype=F32), GRID_W)
    col = jnp.tile(jnp.arange(GRID_W, dtype=F32), rows)
    axis_dim = dim // 2
    n = axis_dim // 2
    inv = jnp.power(ROPE_BASE, -jnp.arange(n, dtype=F32) * 2.0 / axis_dim)
    ar = row[:, None] * inv[None, :]
    ac = col[:, None] * inv[None, :]
    return (jnp.cos(ar), jnp.sin(ar), jnp.cos(ac), jnp.sin(ac))


def _rotate_half(x, cos, sin):
    x1, x2 = jnp.split(x, 2, axis=-1)
    return jnp.concatenate([x1 * cos - x2 * sin, x1 * sin + x2 * cos], axis=-1)


def apply_axial_rope(x, tables):
    cos_r, sin_r, cos_c, sin_c = tables
    shape = (x.shape[1],) + (1,) * (x.ndim - 3) + (cos_r.shape[-1],)
    xr, xcol = jnp.split(x.astype(F32), 2, axis=-1)
    out = jnp.concatenate([
        _rotate_half(xr, cos_r.reshape(shape), sin_r.reshape(shape)),
        _rotate_half(xcol, cos_c.reshape(shape), sin_c.reshape(shape))], axis=-1)
    return out.astype(x.dtype)


def softmax_with_sink(s, sink):
    m = jnp.maximum(jnp.max(s, axis=-1, keepdims=True), sink)
    e = jnp.exp(s - m)
    return e / (jnp.sum(e, axis=-1, keepdims=True) + jnp.exp(sink - m))


def window_gqa_latent(q, k, v, kc, vc, sink):
    B, L, H, d = q.shape
    G = H // WIN_KV_HEADS
    nb = L // BLOCK
    qg = q.reshape(B, L, WIN_KV_HEADS, G, d)
    pad = ((0, 0), (BLOCK, BLOCK), (0, 0), (0, 0))
    kp = jnp.pad(k, pad)
    vp = jnp.pad(v, pad)
    sink = sink.astype(F32).reshape(1, WIN_KV_HEADS, G, 1, 1)
    scale = d ** -0.5

    def block(j):
        start = j * BLOCK
        qb = lax.dynamic_slice_in_dim(qg, start, BLOCK, axis=1)
        kb = lax.dynamic_slice_in_dim(kp, start, 3 * BLOCK, axis=1)
        vb = lax.dynamic_slice_in_dim(vp, start, 3 * BLOCK, axis=1)
        qpos = start + jnp.arange(BLOCK)
        kpos = start - BLOCK + jnp.arange(3 * BLOCK)
        band = (jnp.abs(qpos[:, None] - kpos[None, :]) <= WINDOW) & (kpos >= 0)[None, :] & (kpos < L)[None, :]
        s_loc = jnp.einsum('bqkgd,bskd->bkgqs', qb, kb).astype(F32) * scale
        s_loc = jnp.where(band, s_loc, -jnp.inf)
        s_ctx = jnp.einsum('bqkgd,bskd->bkgqs', qb, kc).astype(F32) * scale
        p = softmax_with_sink(jnp.concatenate([s_loc, s_ctx], axis=-1), sink).astype(v.dtype)
        o = (jnp.einsum('bkgqs,bskd->bqkgd', p[..., :3 * BLOCK], vb)
             + jnp.einsum('bkgqs,bskd->bqkgd', p[..., 3 * BLOCK:], vc))
        return o.reshape(B, BLOCK, H * d)

    o = lax.map(block, jnp.arange(nb))
    return o.transpose(1, 0, 2, 3).reshape(B, L, H * d)


def gqa_context(qc, kc, vc, sink):
    B, Lc, H, d = qc.shape
    G = H // WIN_KV_HEADS
    qg = qc.reshape(B, Lc, WIN_KV_HEADS, G, d)
    s = jnp.einsum('bqkgd,bskd->bkgqs', qg, kc).astype(F32) * d ** -0.5
    p = softmax_with_sink(s, sink.astype(F32).reshape(1, WIN_KV_HEADS, G, 1, 1)).astype(vc.dtype)
    return jnp.einsum('bkgqs,bskd->bqkgd', p, vc).reshape(B, Lc, H * d)


def hgrn2_lower_bounds(raw):
    p = jax.nn.softmax(raw.astype(F32), axis=0)
    return jnp.cumsum(p, axis=0) - p[0]


def hgrn2_gates(pq, pf, lb):
    B, L, _ = pq.shape
    q = pq.reshape(B, L, HG_HEADS, HG_KEY_DIM).astype(F32)
    lb = lb.reshape(HG_HEADS, HG_KEY_DIM)
    f = lb + (1.0 - lb) * jax.nn.sigmoid(pf.reshape(B, L, HG_HEADS, HG_KEY_DIM).astype(F32))
    return q, jnp.log(f), 1.0 - f


def hgrn2_chunk_scan(q, logf, k, v, s0, with_output):
    B, L, H, dk = q.shape
    dv = v.shape[-1]
    n = L // CHUNK

    def chunks(t):
        return t.reshape(B, n, CHUNK, H, t.shape[-1]).transpose(1, 0, 3, 2, 4)

    tri = jnp.tril(jnp.ones((CHUNK, CHUNK), dtype=bool))[None, None, :, :, None]

    def step(S, xs):
        qc, lfc, kc, vc = xs
        b = jnp.cumsum(lfc, axis=2)
        b_end = b[:, :, -1]
        S_new = (jnp.exp(b_end)[..., None] * S
                 + jnp.einsum('bhsk,bhsv->bhkv', kc * jnp.exp(b_end[:, :, None, :] - b), vc))
        if not with_output:
            return S_new, None
        rel = jnp.where(tri, b[:, :, :, None, :] - b[:, :, None, :, :], -jnp.inf)
        A = jnp.einsum('bhtk,bhsk,bhtsk->bhts', qc, kc, jnp.exp(rel))
        o = (jnp.einsum('bhts,bhsv->bhtv', A, vc)
             + jnp.einsum('bhtk,bhkv->bhtv', qc * jnp.exp(b), S))
        return S_new, o

    S, o = lax.scan(step, s0, (chunks(q), chunks(logf), chunks(k), chunks(v)))
    if not with_output:
        return S, None
    return S, o.transpose(1, 0, 3, 2, 4).reshape(B, L, H, dv)


def maybe_flip(t, rev):
    return jnp.flip(t, axis=1) if rev else t


def hgrn2_bidirectional(pl, pc, lb, need_ctx):
    B, L, _ = pl[4].shape
    Lc = pc[4].shape[1]
    il = pl[4].reshape(B, L, HG_HEADS, HG_VAL_DIM).astype(F32)
    ic = pc[4].reshape(B, Lc, HG_HEADS, HG_VAL_DIM).astype(F32)
    o_l = 0.0
    o_c = 0.0
    for d in range(2):
        rev = d == 1
        ql, lfl, kl = hgrn2_gates(pl[2 * d], pl[2 * d + 1], lb[d])
        qc, lfc, kc = hgrn2_gates(pc[2 * d], pc[2 * d + 1], lb[d])
        s0 = jnp.zeros((B, HG_HEADS, HG_KEY_DIM, HG_VAL_DIM), F32)
        s_ctx, oc = hgrn2_chunk_scan(maybe_flip(qc, rev), maybe_flip(lfc, rev), maybe_flip(kc, rev),
                                     maybe_flip(ic, rev), s0, need_ctx)
        _, ol = hgrn2_chunk_scan(maybe_flip(ql, rev), maybe_flip(lfl, rev), maybe_flip(kl, rev),
                                 maybe_flip(il, rev), s_ctx, True)
        o_l = o_l + maybe_flip(ol, rev)
        if need_ctx:
            o_c = o_c + maybe_flip(oc, rev)
    return o_l, (o_c if need_ctx else None)


def hgrn2_readout(o, g, og):
    B, L = o.shape[:2]
    gate = jax.nn.silu(g.reshape(B, L, HG_HEADS, HG_VAL_DIM).astype(F32))
    return (rms_norm(o, og) * gate).reshape(B, L, HG_HEADS * HG_VAL_DIM)


def diff_attn(q, k, v, lam):
    s = jnp.einsum('bqhcd,bshcd->bhcqs', q, k).astype(F32) * DIFF_QK_DIM ** -0.5
    p = jax.nn.softmax(s, axis=-1)
    a = (p[:, :, 0] - lam * p[:, :, 1]).astype(v.dtype)
    return jnp.einsum('bhqs,bshd->bqhd', a, v)


def diff_attn_latent(q, k_all, v_all, lam):
    B, L = q.shape[:2]
    nb = L // BLOCK

    def block(j):
        qb = lax.dynamic_slice_in_dim(q, j * BLOCK, BLOCK, axis=1)
        return diff_attn(qb, k_all, v_all, lam)

    o = lax.map(block, jnp.arange(nb))
    return o.transpose(1, 0, 2, 3, 4).reshape(B, L, DIFF_HEADS, DIFF_V_DIM)


def token_mixers(hl, hc, w_in, win_qg, win_kg, win_sink, lb, hg_og, diff_qg, diff_kg, diff_lam, diff_og,
                 lam_init, rope_a, rope_d, need_ctx):
    B, L, _ = hl.shape
    Lc = hc.shape[1]
    pl = split_columns(hl @ w_in)
    pc = split_columns(hc @ w_in)

    def a_qkv(p, n):
        q = rms_norm(p[0].reshape(B, n, WIN_HEADS, WIN_HEAD_DIM), win_qg)
        k = rms_norm(p[1].reshape(B, n, WIN_KV_HEADS, WIN_HEAD_DIM), win_kg)
        v = p[2].reshape(B, n, WIN_KV_HEADS, WIN_HEAD_DIM)
        return q, k, v

    qa, ka, va = a_qkv(pl, L)
    qa = apply_axial_rope(qa, rope_a)
    ka = apply_axial_rope(ka, rope_a)
    qac, kac, vac = a_qkv(pc, Lc)
    oa_l = window_gqa_latent(qa, ka, va, kac, vac, win_sink)

    ob_l, ob_c = hgrn2_bidirectional(pl[3:8], pc[3:8], lb, need_ctx)
    ob_l = hgrn2_readout(ob_l, pl[8], hg_og).astype(hl.dtype)

    lam_f = diff_lam.astype(F32)
    lam = jnp.exp(jnp.sum(lam_f[0] * lam_f[1])) - jnp.exp(jnp.sum(lam_f[2] * lam_f[3])) + lam_init

    def d_qkv(p, n):
        q = rms_norm(p[9].reshape(B, n, DIFF_HEADS, 2, DIFF_QK_DIM), diff_qg)
        k = rms_norm(p[10].reshape(B, n, DIFF_HEADS, 2, DIFF_QK_DIM), diff_kg)
        v = p[11].reshape(B, n, DIFF_HEADS, DIFF_V_DIM)
        return q, k, v

    qd, kd, vd = d_qkv(pl, L)
    qd = apply_axial_rope(qd, rope_d)
    kd = apply_axial_rope(kd, rope_d)
    qdc, kdc, vdc = d_qkv(pc, Lc)
    od_l = diff_attn_latent(qd, jnp.concatenate([kd, kdc], axis=1), jnp.concatenate([vd, vdc], axis=1), lam)
    od_l = (rms_norm(od_l, diff_og) * (1.0 - lam_init)).reshape(B, L, DIFF_HEADS * DIFF_V_DIM)

    out_l = jnp.concatenate([oa_l, ob_l, od_l.astype(hl.dtype)], axis=-1)
    if not need_ctx:
        return out_l, None

    oa_c = gqa_context(qac, kac, vac, win_sink)
    ob_c = hgrn2_readout(ob_c, pc[8], hg_og).astype(hc.dtype)
    od_c = (rms_norm(diff_attn(qdc, kdc, vdc, lam), diff_og) * (1.0 - lam_init)).reshape(B, Lc, -1)
    out_c = jnp.concatenate([oa_c, ob_c, od_c.astype(hc.dtype)], axis=-1)
    return out_l, out_c


def conv_ffn(h, w_up, conv_w, conv_b, w_down):
    u = h @ w_up
    L = u.shape[1]
    r = CONV_WIDTH // 2
    up = jnp.pad(u, ((0, 0), (r, r), (0, 0)))
    y = conv_b + up[:, 0:L] * conv_w[0]
    for j in range(1, CONV_WIDTH):
        y = y + up[:, j:j + L] * conv_w[j]
    a, val = jnp.split(y, 2, axis=-1)
    return (jax.nn.silu(a) * val) @ w_down


def setup_inputs(seed: int = 0) -> dict:
    key = jax.random.key(seed)
    ks = jax.random.split(key, 24)

    def nrm(k, shape, s):
        return jax.random.normal(k, shape, F32) * s

    D = D_MODEL
    return {
        'x': nrm(ks[0], (BATCH, SEQ, D), 1.0),
        'c': nrm(ks[1], (BATCH, D), 1.0),
        'ctx': nrm(ks[2], (BATCH, CTX_LEN, D), 1.0),
        'c_ctx': nrm(ks[3], (D,), 1.0),
        'w_mod': nrm(ks[4], (DEPTH, D, 6 * D), 0.5 * D ** -0.5),
        'b_mod': nrm(ks[5], (DEPTH, 6 * D), 0.02),
        'norm1_g': 1.0 + nrm(ks[6], (DEPTH, D), 0.05),
        'norm2_g': 1.0 + nrm(ks[7], (DEPTH, D), 0.05),
        'w_in': nrm(ks[8], (DEPTH, D, IN_WIDTH), D ** -0.5),
        'win_qnorm_g': 1.0 + nrm(ks[9], (DEPTH, WIN_HEAD_DIM), 0.05),
        'win_knorm_g': 1.0 + nrm(ks[10], (DEPTH, WIN_HEAD_DIM), 0.05),
        'win_sink': nrm(ks[11], (DEPTH, WIN_HEADS), 0.5),
        'hg_lower': nrm(ks[12], (DEPTH, 2, HG_HEADS * HG_KEY_DIM), 1.0),
        'hg_onorm_g': 1.0 + nrm(ks[13], (DEPTH, HG_VAL_DIM), 0.05),
        'diff_qnorm_g': 1.0 + nrm(ks[14], (DEPTH, DIFF_QK_DIM), 0.05),
        'diff_knorm_g': 1.0 + nrm(ks[15], (DEPTH, DIFF_QK_DIM), 0.05),
        'diff_lambda': nrm(ks[16], (DEPTH, 4, DIFF_QK_DIM), 0.1),
        'diff_onorm_g': 1.0 + nrm(ks[17], (DEPTH, DIFF_V_DIM), 0.05),
        'w_out': nrm(ks[18], (DEPTH, MIX_WIDTH, D), MIX_WIDTH ** -0.5),
        'w_up': nrm(ks[19], (DEPTH, D, 2 * D_FF), D ** -0.5),
        'conv_w': nrm(ks[20], (DEPTH, CONV_WIDTH, 2 * D_FF), CONV_WIDTH ** -0.5),
        'conv_b': nrm(ks[21], (DEPTH, 2 * D_FF), 0.02),
        'w_down': nrm(ks[22], (DEPTH, D_FF, D), D_FF ** -0.5),
    }


def reference(x, c, ctx, c_ctx, w_mod, b_mod, norm1_g, norm2_g, w_in, win_qnorm_g, win_knorm_g, win_sink,
              hg_lower, hg_onorm_g, diff_qnorm_g, diff_knorm_g, diff_lambda, diff_onorm_g, w_out,
              w_up, conv_w, conv_b, w_down):
    L = x.shape[1]
    rope_a = axial_rope_tables(L, WIN_HEAD_DIM)
    rope_d = axial_rope_tables(L, DIFF_QK_DIM)
    lower = hgrn2_lower_bounds(hg_lower)
    xl, xc = x, ctx
    for l in range(DEPTH):
        need_ctx = l < DEPTH - 1
        lam_init = 0.8 - 0.6 * math.exp(-0.3 * l)
        sh1, sc1, g1, sh2, sc2, g2 = adaln(c, w_mod[l], b_mod[l])
        csh1, csc1, cg1, csh2, csc2, cg2 = adaln(c_ctx, w_mod[l], b_mod[l])
        hl = rms_norm(xl, norm1_g[l]) * (1.0 + sc1) + sh1
        hc = rms_norm(xc, norm1_g[l]) * (1.0 + csc1) + csh1
        ol, oc = token_mixers(hl, hc, w_in[l], win_qnorm_g[l], win_knorm_g[l], win_sink[l], lower[l],
                              hg_onorm_g[l], diff_qnorm_g[l], diff_knorm_g[l], diff_lambda[l], diff_onorm_g[l],
                              lam_init, rope_a, rope_d, need_ctx)
        xl = xl + g1 * (ol @ w_out[l])
        hl = rms_norm(xl, norm2_g[l]) * (1.0 + sc2) + sh2
        xl = xl + g2 * conv_ffn(hl, w_up[l], conv_w[l], conv_b[l], w_down[l])
        if need_ctx:
            xc = xc + cg1 * (oc @ w_out[l])
            hc = rms_norm(xc, norm2_g[l]) * (1.0 + csc2) + csh2
            xc = xc + cg2 * conv_ffn(hc, w_up[l], conv_w[l], conv_b[l], w_down[l])
    return xl
```

```python
import contextlib
import numpy as np
import concourse.bass as bass
import concourse.mybir as mybir
from concourse.bass_utils import run_bass_kernel_spmd

F32 = mybir.dt.float32
BF16 = mybir.dt.bfloat16
AF = mybir.ActivationFunctionType
ALU = mybir.AluOpType
AX = mybir.AxisListType

import os
DBG = os.environ.get('KDBG', '')
ENGS = ("pe", "act", "dve", "pool", "sp")
SAME_ENGINE_SYNC = "nosesync" not in DBG


class Buf:
    __slots__ = ("name", "w", "r")

    def __init__(self, name=""):
        self.name = name
        self.w = None
        self.r = {}


class T:
    def __init__(self, t, name=""):
        self.t = t
        self.b = Buf(name)

    def __getitem__(self, idx):
        return self.t[idx]


class Prog:
    def __init__(self, nc, nring=8):
        self.nc = nc
        self.stack = contextlib.ExitStack()
        self.ops = {e: [] for e in ENGS}
        self.cnt = {e: 0 for e in ENGS}
        self.seen = {e: {} for e in ENGS}
        self.dma_tot = {}
        self.ring = {e: 0 for e in ENGS}
        self.nring = nring
        self.sems = {}
        self.nalloc = 0
        self.bind = {}
        self.scopes = []
        for e in ENGS:
            self.sems[("e", e)] = self.stack.enter_context(nc.semaphore("s_" + e))
        for q in ("sp", "act", "pool"):
            for i in range(nring):
                self.sems[("d", q, i)] = self.stack.enter_context(nc.semaphore("d_%s%d" % (q, i)))

    def _stk(self):
        return self.scopes[-1] if self.scopes else self.stack

    def sb(self, name, shape, dt):
        self.nalloc += 1
        return T(self._stk().enter_context(self.nc.sbuf_tensor("%s_%d" % (name, self.nalloc), list(shape), dt)), name)

    def ps(self, name, shape=(128, 512), dt=F32):
        self.nalloc += 1
        return T(self._stk().enter_context(self.nc.psum_tensor("%s_%d" % (name, self.nalloc), list(shape), dt)), name)

    def scratch(self, name, shape, dt):
        return self.nc.dram_tensor(name, list(shape), dt, kind="Internal")

    def barrier(self):
        for e in ENGS:
            for f in ENGS:
                if f != e and self.cnt[f]:
                    self._wait(e, ("e", f), self.cnt[f])
            for key, tot in self.dma_tot.items():
                self._wait(e, key, tot)

    @contextlib.contextmanager
    def scope(self):
        st = contextlib.ExitStack()
        self.scopes.append(st)
        try:
            yield
        finally:
            self.barrier()
            self.scopes.pop()
            st.close()
            if hasattr(self, "_eps"):
                del self._eps

    def dram(self, name, shape, dt, kind):
        if name in self.bind:
            v = self.bind[name]
            assert tuple(v.shape) == tuple(shape), (name, tuple(v.shape), tuple(shape))
            return T(v, name)
        if not hasattr(self.nc, "k_io"):
            self.nc.k_io = []
        self.nc.k_io.append((name, tuple(shape), dt, kind))
        return T(self.nc.dram_tensor(name, list(shape), dt, kind=kind), name)

    def _wait(self, eng, key, val):
        if key == ("e", eng) and (eng == "pe" or not SAME_ENGINE_SYNC):
            return
        if self.seen[eng].get(key, 0) >= val:
            return
        self.seen[eng][key] = val
        sem = self.sems[key]
        self.ops[eng].append(lambda E, sem=sem, val=val: E.wait_ge(sem, val))

    def _deps(self, eng, reads, writes):
        for t in reads:
            b = t.b
            if b.w is not None:
                self._wait(eng, *b.w)
        for t in writes:
            b = t.b
            if b.w is not None:
                self._wait(eng, *b.w)
            for k, v in b.r.items():
                self._wait(eng, k, v)

    def _mark(self, key, val, reads, writes):
        for t in reads:
            t.b.r[key] = val
        for t in writes:
            t.b.w = (key, val)
            t.b.r = {}

    def op(self, eng, fn, reads=(), writes=()):
        self._deps(eng, reads, writes)
        self.cnt[eng] += 1
        key = ("e", eng)
        sem = self.sems[key]
        self.ops[eng].append(lambda E, fn=fn, sem=sem: fn(E).then_inc(sem, 1))
        self._mark(key, self.cnt[eng], reads, writes)

    def dma(self, q, out_ap, in_ap, reads=(), writes=(), slow=False):
        self._deps(q, reads, writes)
        idx = self.ring[q]
        self.ring[q] = (idx + 1) % self.nring
        key = ("d", q, idx)
        prev = self.dma_tot.get(key, 0)
        if prev:
            self._wait(q, key, prev)
        tot = prev + 16
        self.dma_tot[key] = tot
        sem = self.sems[key]
        kw = dict(allow_slow_non_contiguous=True) if slow else {}
        self.ops[q].append(lambda E, o=out_ap, i=in_ap, sem=sem, kw=kw: E.dma_start(out=o, in_=i, **kw).then_inc(sem, 16))
        self._mark(key, tot, reads, writes)

    def finish(self):
        for key, tot in self.dma_tot.items():
            self._wait(key[1], key, tot)
        nc = self.nc
        ops = self.ops
        with nc.Block() as block:
            @block.tensor
            def _(E):
                for f in ops["pe"]:
                    f(E)

            @block.scalar
            def _(E):
                for f in ops["act"]:
                    f(E)

            @block.vector
            def _(E):
                for f in ops["dve"]:
                    f(E)

            @block.gpsimd
            def _(E):
                for f in ops["pool"]:
                    f(E)

            @block.sync
            def _(E):
                for f in ops["sp"]:
                    f(E)
        self.stack.close()

    def make_eps(self, eps):
        if hasattr(self, "_eps"):
            return
        self._eps = self.sb("epsc", [128, 1], F32)
        self.memset("pool", self._eps[:, :], eps, [self._eps])

    def mm(self, out, lhsT, rhs, start, stop, reads, writes):
        self.op("pe", lambda E: E.matmul(out, lhsT, rhs, start=start, stop=stop), reads, writes)

    def act(self, out, in_, func, reads, writes, bias=None, scale=None, eng="act"):
        kw = {}
        if bias is not None:
            kw["bias"] = bias
        if scale is not None:
            kw["scale"] = scale
        self.op(eng, lambda E: E.activation(out, in_, func, **kw), reads, writes)

    def tt(self, eng, out, in0, in1, op, reads, writes):
        self.op(eng, lambda E: E.tensor_tensor(out, in0, in1, op), reads, writes)

    def ts(self, eng, out, in0, s1, s2, op0, op1, reads, writes):
        if s2 is None:
            self.op(eng, lambda E: E.tensor_scalar(out, in0, s1, 0.0, op0, ALU.add), reads, writes)
        else:
            self.op(eng, lambda E: E.tensor_scalar(out, in0, s1, s2, op0, op1), reads, writes)

    def stt(self, eng, out, in0, scalar, in1, op0, op1, reads, writes):
        eng = "dve"
        self.op(eng, lambda E: E.scalar_tensor_tensor(out, in0, scalar, in1, op0, op1), reads, writes)

    def rsqrt(self, out, in_, eps, tmp, reads, writes):
        n = out.shape[-1]
        np_ = out.shape[0]
        eps_t = self._eps
        self.op("act", lambda E: E.activation(tmp[0:np_, 0:n], in_, AF.Ln, bias=eps_t[0:np_, 0:1]), list(reads) + [eps_t], [tmp])
        self.op("act", lambda E: E.activation(out, tmp[0:np_, 0:n], AF.Exp, scale=-0.5), [tmp], writes)

    def eps_ap(self, eps):
        return self._eps[:, 0:1]

    def recip(self, out, in_, reads, writes):
        self.op("act", lambda E: E.activation(out, in_, AF.Ln), reads, writes)
        self.op("act", lambda E: E.activation(out, out, AF.Exp, scale=-1.0), writes, writes)

    def copy(self, eng, out, in_, reads, writes):
        if eng == "act":
            self.op(eng, lambda E: E.copy(out, in_), reads, writes)
        else:
            self.op(eng, lambda E: E.tensor_copy(out, in_), reads, writes)

    def memset(self, eng, out, val, writes):
        self.op(eng, lambda E: E.memset(out, val), (), writes)


D = 1024
DFF = 2816
NCORES = 8


def emit_ffn(p, T_list, TW=510):
    KC = D // 128
    FC = DFF // 128
    nseg = len(T_list)
    segs = []
    for i, Tn in enumerate(T_list):
        segs.append(dict(T=Tn, h2T=p.dram("h2T%d" % i, [D, Tn + 2], BF16, "ExternalInput"),
                         xmT=p.dram("xmT%d" % i, [D, Tn], F32, "ExternalInput"),
                         outT=p.dram("outT%d" % i, [D, Tn], F32, "ExternalOutput")))
    g2 = p.dram("g2", [128, nseg, KC], F32, "ExternalInput")
    w_up = p.dram("w_up", [D, 2 * DFF], F32, "ExternalInput")
    w_down = p.dram("w_down", [DFF, D], F32, "ExternalInput")
    cw = p.dram("cw", [128, 3, 2 * FC], F32, "ExternalInput")
    cb = p.dram("cb", [128, 2 * FC], F32, "ExternalInput")

    wup_sb = [p.sb("wup%d" % k, [128, 2 * DFF], BF16) for k in range(KC)]
    wdn_sb = [p.sb("wdn%d" % i, [128, D], BF16) for i in range(FC)]
    g2_sb = p.sb("g2", [128, nseg, KC], F32)
    cw_sb = p.sb("cw", [128, 3, 2 * FC], F32)
    cb_sb = p.sb("cb", [128, 2 * FC], F32)
    p.dma("sp", g2_sb[:], g2[:], [g2], [g2_sb])
    p.dma("sp", cw_sb[:], cw[:], [cw], [cw_sb])
    p.dma("sp", cb_sb[:], cb[:], [cb], [cb_sb])
    with p.scope():
        stg = [p.sb("stg%d" % i, [128, 1408], F32) for i in range(3)]
        si = 0
        ceng = ["dve", "pool", "act"]
        for k in range(KC):
            for j in range(4):
                s = stg[si % 3]
                p.dma("sp", s[:, 0:1408], w_up[k * 128:(k + 1) * 128, j * 1408:(j + 1) * 1408], [w_up], [s])
                p.copy(ceng[si % 3], wup_sb[k][:, j * 1408:(j + 1) * 1408], s[:, 0:1408], [s], [wup_sb[k]])
                si += 1
        for i in range(FC):
            s = stg[si % 3]
            p.dma("sp", s[:, 0:D], w_down[i * 128:(i + 1) * 128, :], [w_down], [s])
            p.copy(ceng[si % 3], wdn_sb[i][:, :], s[:, 0:D], [s], [wdn_sb[i]])
            si += 1

    NB = 2
    WB = TW + 2
    h_sb = [p.sb("h%d" % i, [128, KC, WB], BF16) for i in range(NB)]
    xm_sb = [p.sb("xm%d" % i, [128, TW], F32) for i in range(3)]
    gT = [p.sb("gT%d" % i, [128, TW], BF16) for i in range(FC)]
    pa = [p.ps("pa%d" % i) for i in range(2)]
    pv = [p.ps("pv%d" % i) for i in range(2)]
    pd = [p.ps("pd%d" % i) for i in range(2)]
    ya = [p.sb("ya%d" % i, [128, TW], F32) for i in range(2)]
    yv = [p.sb("yv%d" % i, [128, TW], F32) for i in range(2)]
    sa = [p.sb("sa%d" % i, [128, TW], F32) for i in range(2)]
    ot = [p.sb("ot%d" % i, [128, TW], F32) for i in range(2)]
    tiles = [(sgi, t0, min(TW, sg["T"] - t0)) for sgi, sg in enumerate(segs) for t0 in range(0, sg["T"], TW)]

    def load(ti):
        sgi, t0, w = tiles[ti]
        sg = segs[sgi]
        hb = h_sb[ti % NB]
        p.dma("sp", hb[:, :, 0:w + 2], sg["h2T"][:, t0:t0 + w + 2].rearrange("(k p) n -> p k n", p=128), [sg["h2T"]], [hb])

    load(0)
    cc = 0
    xi = 0
    for ti, (sgi, t0, w) in enumerate(tiles):
        sg = segs[sgi]
        if ti + 1 < len(tiles):
            load(ti + 1)
        hb = h_sb[ti % NB]
        for i in range(FC):
            A = pa[cc % 2]
            V = pv[cc % 2]
            for k in range(KC):
                p.mm(A[:, 0:w + 2], wup_sb[k][:, i * 128:(i + 1) * 128], hb[:, k, 0:w + 2], k == 0, k == KC - 1,
                     [wup_sb[k], hb], [A])
            for k in range(KC):
                p.mm(V[:, 0:w + 2], wup_sb[k][:, DFF + i * 128:DFF + (i + 1) * 128], hb[:, k, 0:w + 2], k == 0,
                     k == KC - 1, [wup_sb[k], hb], [V])
            a_t = ya[cc % 2]
            v_t = yv[cc % 2]
            s_t = sa[cc % 2]
            p.act(a_t[:, 0:w], A[:, 0:w], AF.Identity, [A, cw_sb, cb_sb], [a_t], bias=cb_sb[:, i:i + 1],
                  scale=cw_sb[:, 0, i:i + 1])
            p.stt("dve", a_t[:, 0:w], A[:, 1:w + 1], cw_sb[:, 1, i:i + 1], a_t[:, 0:w], ALU.mult, ALU.add,
                  [A, a_t, cw_sb], [a_t])
            p.stt("dve", a_t[:, 0:w], A[:, 2:w + 2], cw_sb[:, 2, i:i + 1], a_t[:, 0:w], ALU.mult, ALU.add,
                  [A, a_t, cw_sb], [a_t])
            p.act(v_t[:, 0:w], V[:, 0:w], AF.Identity, [V, cw_sb, cb_sb], [v_t], bias=cb_sb[:, FC + i:FC + i + 1],
                  scale=cw_sb[:, 0, FC + i:FC + i + 1])
            p.stt("dve", v_t[:, 0:w], V[:, 1:w + 1], cw_sb[:, 1, FC + i:FC + i + 1], v_t[:, 0:w], ALU.mult, ALU.add,
                  [V, v_t, cw_sb], [v_t])
            p.stt("dve", v_t[:, 0:w], V[:, 2:w + 2], cw_sb[:, 2, FC + i:FC + i + 1], v_t[:, 0:w], ALU.mult, ALU.add,
                  [V, v_t, cw_sb], [v_t])
            p.act(s_t[:, 0:w], a_t[:, 0:w], AF.Silu, [a_t], [s_t])
            p.tt("pool", gT[i][:, 0:w], s_t[:, 0:w], v_t[:, 0:w], ALU.mult, [s_t, v_t], [gT[i]])
            cc += 1
        for m in range(KC):
            xb = xm_sb[xi % 3]
            xi += 1
            p.dma("sp", xb[:, 0:w], sg["xmT"][m * 128:(m + 1) * 128, t0:t0 + w], [sg["xmT"]], [xb])
            Dp = pd[m % 2]
            for i in range(FC):
                p.mm(Dp[:, 0:w], wdn_sb[i][:, m * 128:(m + 1) * 128], gT[i][:, 0:w], i == 0, i == FC - 1,
                     [wdn_sb[i], gT[i]], [Dp])
            o = ot[m % 2]
            p.stt("dve", o[:, 0:w], Dp[:, 0:w], g2_sb[:, sgi, m:m + 1], xb[:, 0:w], ALU.mult, ALU.add,
                  [Dp, g2_sb, xb], [o])
            p.dma("pool", sg["outT"][m * 128:(m + 1) * 128, t0:t0 + w], o[:, 0:w], [o], [])


def build_ffn(T_list, TW=510):
    nc = bass.Bass("TRN2", target_bir_lowering=False)
    p = Prog(nc)
    emit_ffn(p, T_list, TW)
    p.finish()
    return nc


EPS = 1e-6
NIN = 3072
FM_CHUNKS = (
    [("qA", 128 * i, "qkA", i) for i in range(4)]
    + [("kA", 512, "qkA", 0)]
    + [("qB0", 768 + 128 * i, "plain", i) for i in range(2)]
    + [("kB0", 1024 + 128 * i, "fgate", i) for i in range(2)]
    + [("qB1", 1280 + 128 * i, "plain", i) for i in range(2)]
    + [("kB1", 1536 + 128 * i, "fgate", i) for i in range(2)]
    + [("gB", 2048 + 128 * i, "silu", i) for i in range(2)]
    + [("qC", 2304 + 128 * i, "qkC", i) for i in range(2)]
    + [("kC", 2560 + 128 * i, "qkC", i) for i in range(2)]
)
TM_GROUPS = [("vA", 640, 128, "plain"), ("vB", 1792, 256, "plain"), ("vC", 2816, 256, "plain"),
             ("f0", 1024, 256, "f"), ("f1", 1536, 256, "f")]
FM_ROWS = {"qA": 512, "kA": 128, "qB0": 256, "kB0": 256, "qB1": 256, "kB1": 256, "gB": 256, "qC": 256, "kC": 256}


def bcast_rows(t, nrow, ncol, off=0):
    h = t.t
    if isinstance(h, bass.AP):
        return bass.AP(h.tensor, h.offset + off, [[0, nrow], [1, ncol]])
    return bass.AP(h, off, [[0, nrow], [1, ncol]])


def p1_declare(p, sfx, Tn):
    o = {}
    for nm, rows in FM_ROWS.items():
        o[nm] = p.dram(nm + sfx, [rows, Tn], BF16, "ExternalOutput")
    o["vA"] = p.dram("vA" + sfx, [Tn, 128], BF16, "ExternalOutput")
    o["vB"] = p.dram("vB" + sfx, [Tn, 256], BF16, "ExternalOutput")
    o["vC"] = p.dram("vC" + sfx, [Tn, 256], BF16, "ExternalOutput")
    o["LF"] = p.dram("LF" + sfx, [2, Tn, 256], F32, "ExternalOutput")
    o["KB"] = p.dram("KB" + sfx, [2, Tn, 256], BF16, "ExternalOutput")
    return o


def emit_p1(p, TL, TC, extra=None):
    KC = 8
    xT = p.dram("xT", [D, TL], F32, "ExternalInput")
    xcT = p.dram("xcT", [D, TC], F32, "ExternalInput")
    mod = p.dram("mod", [128, 4, KC], F32, "ExternalInput")
    w_in = p.dram("w_in", [D, NIN], F32, "ExternalInput")
    gains = p.dram("gains", [128, 4], F32, "ExternalInput")
    lbT = p.dram("lbT", [128, 4], F32, "ExternalInput")
    lbrow = p.dram("lbrow", [2, 256], F32, "ExternalInput")
    ropeT = p.dram("ropeT", [4, 128, TL], F32, "ExternalInput")
    cmat = p.dram("cmat", [3, 128, 128], F32, "ExternalInput")
    rmat = p.dram("rmat", [2, 128, 128], BF16, "ExternalInput")
    outs_l = p1_declare(p, "", TL)
    outs_c = p1_declare(p, "_c", TC)

    w_sb = [p.sb("w%d" % k, [128, NIN], BF16) for k in range(KC)]
    mod_sb = p.sb("mod", [128, 4, KC], F32)
    gains_sb = p.sb("gains", [128, 4], F32)
    oml_sb = p.sb("oml", [128, 4], F32)
    lb_bc = p.sb("lb_bc", [128, 2, 256], F32)
    oml_bc = p.sb("oml_bc", [128, 2, 256], F32)
    cmat_sb = p.sb("cmat", [128, 3, 128], F32)
    rmat_sb = p.sb("rmat", [128, 2, 128], BF16)
    p.dma("sp", mod_sb[:], mod[:], [mod], [mod_sb])
    p.dma("sp", gains_sb[:], gains[:], [gains], [gains_sb])
    p.dma("sp", oml_sb[:], lbT[:], [lbT], [oml_sb])
    p.dma("sp", cmat_sb[:], cmat[:].rearrange("c p n -> p c n"), [cmat], [cmat_sb])
    p.dma("sp", rmat_sb[:], rmat[:].rearrange("c p n -> p c n"), [rmat], [rmat_sb])
    for d in range(2):
        p.dma("sp", lb_bc[:, d, :], bcast_rows(lbrow, 128, 256, d * 256), [lbrow], [lb_bc])
    p.ts("dve", oml_bc[:], lb_bc[:], -1.0, 1.0, ALU.mult, ALU.add, [lb_bc], [oml_bc])

    stg = [p.sb("stg%d" % i, [128, 1536], F32) for i in range(3)]
    ceng = ["dve", "pool", "act"]
    si = 0
    for k in range(KC):
        for j in range(2):
            s = stg[si % 3]
            p.dma("sp", s[:, :], w_in[k * 128:(k + 1) * 128, j * 1536:(j + 1) * 1536], [w_in], [s])
            p.copy(ceng[si % 3], w_sb[k][:, j * 1536:(j + 1) * 1536], s[:, :], [s], [w_sb[k]])
            si += 1

    TW = 512
    NB = 2
    x_sb = [p.sb("x%d" % i, [128, KC, TW], F32) for i in range(NB)]
    rope_sb = [p.sb("rope%d" % i, [128, 4, TW], F32) for i in range(NB)]
    sq_sb = [p.sb("sq%d" % i, [128, TW], F32) for i in range(2)]
    rstd_sb = p.sb("rstd", [128, TW], F32)
    hx_sb = [p.sb("hx%d" % i, [128, TW], F32) for i in range(2)]
    hT = [p.sb("hT%d" % k, [128, TW], BF16) for k in range(KC)]
    pss = p.ps("pss")
    pfm = [p.ps("pfm%d" % i) for i in range(2)]
    pms = p.ps("pms")
    prot = p.ps("prot")
    ptm = [p.ps("ptm%d" % i) for i in range(2)]
    NS = 3
    sqh = [p.sb("sqh%d" % i, [128, TW], F32) for i in range(NS)]
    rs2 = [p.sb("rs2%d" % i, [128, TW], F32) for i in range(NS)]
    qg = [p.sb("qg%d" % i, [128, TW], BF16) for i in range(NS)]
    t1 = [p.sb("t1%d" % i, [128, TW], F32) for i in range(NS)]
    t2 = [p.sb("t2%d" % i, [128, TW], F32) for i in range(NS)]
    ofm = [p.sb("ofm%d" % i, [128, TW], BF16) for i in range(NS)]
    sg = [p.sb("sg%d" % i, [128, TW], F32) for i in range(NS)]
    otm = [p.sb("otm%d" % i, [128, 256], BF16) for i in range(NS)]
    ftm = [p.sb("ftm%d" % i, [128, 256], F32) for i in range(NS)]
    lftm = [p.sb("lftm%d" % i, [128, 256], F32) for i in range(NS)]
    cnt = {"fm": 0, "tm": 0, "s": 0}
    rtmp = p.sb("rtmp", [128, TW], F32)
    rtmp2 = [p.sb("rtmp2%d" % i, [128, TW], F32) for i in range(NS)]
    p.make_eps(EPS)

    def run(src, Tn, outs, mi, rope):
        tw = min(TW, Tn)
        nt = Tn // tw

        def load(t):
            xb = x_sb[t % NB]
            p.dma("sp", xb[:, :, 0:tw], src[:, t * tw:(t + 1) * tw].rearrange("(k p) n -> p k n", p=128), [src], [xb])
            if rope:
                rb = rope_sb[t % NB]
                p.dma("sp", rb[:, :, 0:tw], ropeT[:, :, t * tw:(t + 1) * tw].rearrange("c p n -> p c n"), [ropeT], [rb])

        load(0)
        for t in range(nt):
            if t + 1 < nt:
                load(t + 1)
            xb = x_sb[t % NB]
            rb = rope_sb[t % NB]
            c0 = t * tw
            for k in range(KC):
                s = sq_sb[k % 2]
                p.act(s[:, 0:tw], xb[:, k, 0:tw], AF.Square, [xb], [s])
                p.mm(pss[:, 0:tw], cmat_sb[:, 0, :], s[:, 0:tw], k == 0, k == KC - 1, [cmat_sb, s], [pss])
            p.rsqrt(rstd_sb[:, 0:tw], pss[:, 0:tw], EPS, rtmp, [pss], [rstd_sb])
            for k in range(KC):
                hx = hx_sb[k % 2]
                p.stt("dve", hx[:, 0:tw], xb[:, k, 0:tw], mod_sb[:, mi, k:k + 1], rstd_sb[:, 0:tw], ALU.mult, ALU.mult,
                      [xb, mod_sb, rstd_sb], [hx])
                p.act(hT[k][:, 0:tw], hx[:, 0:tw], AF.Identity, [hx, mod_sb], [hT[k]], bias=mod_sb[:, mi + 1, k:k + 1])
            for (nm, col0, kind, ci) in FM_CHUNKS:
                P = pfm[cnt["fm"] % 2]
                cnt["fm"] += 1
                for k in range(KC):
                    p.mm(P[:, 0:tw], w_sb[k][:, col0:col0 + 128], hT[k][:, 0:tw], k == 0, k == KC - 1, [w_sb[k], hT[k]], [P])
                si_ = cnt["s"] % NS
                cnt["s"] += 1
                o = ofm[si_]
                if kind == "plain":
                    p.copy("act", o[:, 0:tw], P[:, 0:tw], [P], [o])
                elif kind == "fgate":
                    dirn = 0 if nm == "kB0" else 1
                    p.act(sg[si_][:, 0:tw], P[:, 0:tw], AF.Sigmoid, [P], [sg[si_]], scale=-1.0)
                    p.ts("pool", o[:, 0:tw], sg[si_][:, 0:tw], oml_sb[:, dirn * 2 + ci:dirn * 2 + ci + 1], None, ALU.mult, None,
                         [sg[si_], oml_sb], [o])
                elif kind == "silu":
                    p.act(sg[si_][:, 0:tw], P[:, 0:tw], AF.Sigmoid, [P], [sg[si_]])
                    p.tt("dve", o[:, 0:tw], P[:, 0:tw], sg[si_][:, 0:tw], ALU.mult, [P, sg[si_]], [o])
                else:
                    isA = kind == "qkA"
                    gi = (0 if nm == "qA" else 1) if isA else (2 if nm == "qC" else 3)
                    gm = 1 if isA else 2
                    p.act(sqh[si_][:, 0:tw], P[:, 0:tw], AF.Square, [P], [sqh[si_]])
                    p.mm(pms[:, 0:tw], cmat_sb[:, gm, :], sqh[si_][:, 0:tw], True, True, [cmat_sb, sqh[si_]], [pms])
                    p.rsqrt(rs2[si_][:, 0:tw], pms[:, 0:tw], EPS, rtmp2[si_], [pms], [rs2[si_]])
                    if not rope:
                        p.stt("dve", o[:, 0:tw], P[:, 0:tw], gains_sb[:, gi:gi + 1], rs2[si_][:, 0:tw], ALU.mult, ALU.mult,
                              [P, gains_sb, rs2[si_]], [o])
                    else:
                        q_ = qg[si_]
                        p.stt("dve", q_[:, 0:tw], P[:, 0:tw], gains_sb[:, gi:gi + 1], rs2[si_][:, 0:tw], ALU.mult, ALU.mult,
                              [P, gains_sb, rs2[si_]], [q_])
                        ri = 0 if isA else 1
                        p.mm(prot[:, 0:tw], rmat_sb[:, ri, :], q_[:, 0:tw], True, True, [rmat_sb, q_], [prot])
                        p.tt("pool", t1[si_][:, 0:tw], q_[:, 0:tw], rb[:, 2 * ri, 0:tw], ALU.mult, [q_, rb], [t1[si_]])
                        p.tt("dve", t2[si_][:, 0:tw], prot[:, 0:tw], rb[:, 2 * ri + 1, 0:tw], ALU.mult, [prot, rb], [t2[si_]])
                        p.tt("pool", o[:, 0:tw], t1[si_][:, 0:tw], t2[si_][:, 0:tw], ALU.add, [t1[si_], t2[si_]], [o])
                p.dma("pool", outs[nm][ci * 128:(ci + 1) * 128, c0:c0 + tw], o[:, 0:tw], [o], [])
            for s4 in range(tw // 128):
                r0 = c0 + s4 * 128
                for (nm, col0, ncols, kind) in TM_GROUPS:
                    P = ptm[cnt["tm"] % 2]
                    cnt["tm"] += 1
                    for k in range(KC):
                        p.mm(P[:, 0:ncols], hT[k][:, s4 * 128:(s4 + 1) * 128], w_sb[k][:, col0:col0 + ncols], k == 0, k == KC - 1,
                             [w_sb[k], hT[k]], [P])
                    si_ = cnt["s"] % NS
                    cnt["s"] += 1
                    if kind == "plain":
                        o = otm[si_]
                        p.copy("act", o[:, 0:ncols], P[:, 0:ncols], [P], [o])
                        p.dma("pool", outs[nm][r0:r0 + 128, :], o[:, 0:ncols], [o], [])
                    else:
                        dirn = 0 if nm == "f0" else 1
                        f_ = ftm[si_]
                        p.act(f_[:, :], P[:, 0:256], AF.Sigmoid, [P], [f_])
                        p.tt("dve", f_[:, :], f_[:, :], oml_bc[:, dirn, :], ALU.mult, [f_, oml_bc], [f_])
                        p.tt("pool", f_[:, :], f_[:, :], lb_bc[:, dirn, :], ALU.add, [f_, lb_bc], [f_])
                        lf = lftm[si_]
                        p.act(lf[:, :], f_[:, :], AF.Ln, [f_], [lf])
                        o = otm[si_]
                        p.ts("pool", o[:, :], f_[:, :], -1.0, 1.0, ALU.mult, ALU.add, [f_], [o])
                        p.dma("pool", outs["LF"][dirn, r0:r0 + 128, :], lf[:, :], [lf], [])
                        p.dma("pool", outs["KB"][dirn, r0:r0 + 128, :], o[:, :], [o], [])

    run(xT, TL, outs_l, 0, True)
    run(xcT, TC, outs_c, 2, False)


def build_p1(TL, TC):
    nc = bass.Bass("TRN2", target_bir_lowering=False)
    p = Prog(nc)
    emit_p1(p, TL, TC)
    p.finish()
    return nc


ROPE_BASE = 10000.0
GRID_W = 64


def rope_tables(pos):
    pos = np.asarray(pos)
    row = (pos // GRID_W).astype(np.float32)
    col = (pos % GRID_W).astype(np.float32)
    out = []
    for dim in (64, 32):
        axis_dim = dim // 2
        n = axis_dim // 2
        inv = np.power(np.float32(ROPE_BASE), (-np.arange(n, dtype=np.float32) * np.float32(2.0) / np.float32(axis_dim))).astype(np.float32)
        d = np.arange(128) % dim
        half = d // axis_dim
        i = d % n
        ang = np.where(half[:, None] == 0, row[None, :], col[None, :]).astype(np.float32) * inv[i][:, None]
        out.append(np.cos(ang).astype(np.float32))
        out.append(np.sin(ang).astype(np.float32))
    return np.stack(out, 0)


def const_mats():
    cm = np.zeros((3, 128, 128), np.float32)
    cm[0] = 1.0 / 1024
    for g in range(2):
        cm[1, g * 64:(g + 1) * 64, g * 64:(g + 1) * 64] = 1.0 / 64
    for g in range(4):
        cm[2, g * 32:(g + 1) * 32, g * 32:(g + 1) * 32] = 1.0 / 32
    rm = np.zeros((2, 128, 128), np.float32)
    for ri, axis_dim in enumerate((32, 16)):
        hf = axis_dim // 2
        for m in range(128):
            if (m % axis_dim) < hf:
                rm[ri, m + hf, m] = -1.0
            else:
                rm[ri, m - hf, m] = 1.0
    import ml_dtypes
    return cm, rm.astype(ml_dtypes.bfloat16)


def p2_attn(p, L, LC, need_ctx, lam_init, g=None, nat=None):
    NCH = (L + LC) // 128
    NLB = L // 128
    qC = p.dram("qC", [128, L], BF16, "ExternalInput")
    qC_c = p.dram("qC_c", [128, LC], BF16, "ExternalInput")
    if nat is None:
        kC = p.dram("kC", [128, L + LC], BF16, "ExternalInput")
        vC = p.dram("vCp", [128, NCH, 2, 128], BF16, "ExternalInput")
        kA = p.dram("kA", [64, L + LC], BF16, "ExternalInput")
        vA = p.dram("vAp", [128, NCH, 128], BF16, "ExternalInput")
    qA = p.dram("qA", [256, L], BF16, "ExternalInput")
    qA_c = p.dram("qA_c", [256, LC], BF16, "ExternalInput")
    scal = p.dram("scal", [128, 8], F32, "ExternalInput")
    masks = p.dram("masks", [2, 128, 128], BF16, "ExternalInput")
    g64 = p.dram("g64", [64, 64], F32, "ExternalInput")
    oC = p.dram("oC", [128, L], BF16, "ExternalOutput")
    oA = p.dram("oA", [256, L], BF16, "ExternalOutput")
    oC_c = p.dram("oC_c", [128, LC], BF16, "ExternalOutput")
    oA_c = p.dram("oA_c", [256, LC], BF16, "ExternalOutput")
    if 'dbg' in DBG:
        dbg = p.dram("dbg", [2, 4, 64, 512], F32, "ExternalOutput")
        dbg2 = p.dram("dbg2", [2, 2, 128, 512], F32, "ExternalOutput")
        dbgz = [p.sb("dbgz%d" % i, [128, 512], F32) for i in range(2)]

    scal_sb = p.sb("scal", [128, 8], F32)
    sc2 = p.sb("sc2", [128, 8], F32)
    mask_sb = p.sb("mask", [128, 2, 128], BF16)
    g64_sb = p.sb("g64", [64, 64], F32)
    p.dma("sp", scal_sb[:], scal[:], [scal], [scal_sb])
    if nat is not None:
        p.dma("sp", scal_sb[:, 0:1], nat["lamcol"], [scal_sb], [scal_sb], slow=True)
    p.dma("sp", mask_sb[:], masks[:].rearrange("c p n -> p c n"), [masks], [mask_sb])
    p.dma("sp", g64_sb[:], g64[:], [g64], [g64_sb])
    p.ts("dve", sc2[:, 0:1], scal_sb[:, 0:1], -1.0, None, ALU.mult, None, [scal_sb], [sc2])
    p.ts("dve", sc2[:, 1:2], scal_sb[:, 1:2], float(1.0 - lam_init), None, ALU.mult, None, [scal_sb], [sc2])
    p.act(sc2[:, 2:6], scal_sb[:, 2:6], AF.Exp, [scal_sb], [sc2])

    kC_sb = p.sb("kC", [64, 2, L + LC], BF16)
    qC_t = [p.sb("qCt%d" % i, [64, 2, 512], BF16) for i in range(2)]
    vC_sb = p.sb("vC", [128, NCH, 2, 128], BF16)
    kA_sb = p.sb("kA", [64, L + LC], BF16)
    qA_t = [p.sb("qAt%d" % i, [64, 4, 512], BF16) for i in range(2)]
    vA_sb = p.sb("vA", [128, NCH, 128], BF16)
    NSPL = 4
    Ls = (L + LC) // NSPL
    if nat is None:
        for i in range(NSPL):
            for h in range(2):
                p.dma("sp", kC_sb[:, h, i * Ls:(i + 1) * Ls], kC[h * 64:(h + 1) * 64, i * Ls:(i + 1) * Ls], [kC], [kC_sb])
            p.dma("sp", kA_sb[:, i * Ls:(i + 1) * Ls], kA[:, i * Ls:(i + 1) * Ls], [kA], [kA_sb])
    else:
        Lp = L // NSPL
        for h in range(2):
            rows = slice(g * 128 + h * 64, g * 128 + (h + 1) * 64)
            for i in range(NSPL):
                p.dma("sp", kC_sb[:, h, i * Lp:(i + 1) * Lp], nat["kC"][rows, LC + i * Lp:LC + (i + 1) * Lp], [], [kC_sb])
            p.dma("sp", kC_sb[:, h, L:L + LC], nat["kC"][rows, 0:LC], [], [kC_sb])
        for i in range(NSPL):
            p.dma("sp", kA_sb[:, i * Lp:(i + 1) * Lp], nat["kA"][g * 64:(g + 1) * 64, LC + i * Lp:LC + (i + 1) * Lp], [], [kA_sb])
        p.dma("sp", kA_sb[:, L:L + LC], nat["kA"][g * 64:(g + 1) * 64, 0:LC], [], [kA_sb])
    def load_qC(t):
        buf = qC_t[t % 2]
        for h in range(2):
            p.dma("sp", buf[:, h, :], qC[h * 64:(h + 1) * 64, t * 512:(t + 1) * 512], [qC], [buf])
        return buf

    def load_qA(t):
        buf = qA_t[t % 2]
        for hh in range(4):
            p.dma("sp", buf[:, hh, :], qA[hh * 64:(hh + 1) * 64, t * 512:(t + 1) * 512], [qA], [buf])
        return buf
    CS = NCH // 2
    if nat is None:
        for c0 in range(0, NCH, CS):
            p.dma("sp", vC_sb[:, c0:c0 + CS, :, :], vC[:, c0:c0 + CS, :, :], [vC], [vC_sb])
            p.dma("sp", vA_sb[:, c0:c0 + CS, :], vA[:, c0:c0 + CS, :], [vA], [vA_sb])
    else:
        p.memset("pool", vC_sb[:, :, :, 64:128], 1.0, [vC_sb])
        p.memset("pool", vA_sb[:, :, 64:128], 1.0, [vA_sb])
        for j in range(NCH):
            tk = LC + j * 128 if j < NLB else (j - NLB) * 128
            for h in range(2):
                cs = (2 * g + h) * 64
                p.dma("sp", vC_sb[:, j, h, 0:64], nat["vC"][tk:tk + 128, cs:cs + 64], [], [vC_sb])
            p.dma("sp", vA_sb[:, j, 0:64], nat["vA"][tk:tk + 128, g * 64:(g + 1) * 64], [], [vA_sb])
    qCc_sb = p.sb("qCc", [64, 2, LC], BF16)
    qAc_sb = p.sb("qAc", [64, 4, LC], BF16)
    if need_ctx:
        for h in range(2):
            p.dma("sp", qCc_sb[:, h, :], qC_c[h * 64:(h + 1) * 64, :], [qC_c], [qCc_sb])
        for hh in range(4):
            p.dma("sp", qAc_sb[:, hh, :], qA_c[hh * 64:(hh + 1) * 64, :], [qA_c], [qAc_sb])

    NPS = 2
    ps = [p.ps("ps%d" % i, (128, 1024)) for i in range(NPS)]
    po = [p.ps("po%d" % i) for i in range(4)]
    NE = 4
    E = [p.sb("E%d" % i, [128, 1024], BF16) for i in range(NE)]
    rz = [p.sb("rz%d" % i, [64, 512], F32) for i in range(2)]
    on = [p.sb("on%d" % i, [64, 512], F32) for i in range(2)]
    o_t = p.sb("o_t", [64, 512], F32)
    sq_t = p.sb("sq_t", [64, 512], F32)
    rs_t = p.sb("rs_t", [64, 512], F32)
    rtmp = p.sb("rtmpa", [64, 512], F32)
    ob = [p.sb("ob%d" % i, [64, 512], BF16) for i in range(2)]
    st = {"e": 0, "ps": 0, "po": 0, "ob": 0, "qm": 0}
    cm_sb = p.sb("cmsel", [64, 2], F32)
    p.memset("pool", cm_sb[:, :], 0.0, [cm_sb])
    p.memset("pool", cm_sb[0:32, 0:1], 1.0, [cm_sb])
    p.memset("pool", cm_sb[32:64, 1:2], 1.0, [cm_sb])
    qm_t = [[p.sb("qm%d_%d" % (i, c), [64, 2, 512], BF16) for c in range(2)] for i in range(2)]
    SC_C = 32 ** -0.5
    SC_A = 64 ** -0.5

    def diff_tile(q_sb, q0, N, chunks, out_dram, o0):
        qm = qm_t[st["qm"] % 2]
        st["qm"] += 1
        for c in range(2):
            p.ts("pool" if c == 0 else "dve", qm[c][:, :, 0:N], q_sb[:, :, q0:q0 + N], cm_sb[:, c:c + 1], None, ALU.mult, None,
                 [q_sb, cm_sb], [qm[c]])
        for h in ((1, 0, 1) if 'swap' in DBG else range(2)):
            POs = [po[(st["po"] + c) % 4] for c in range(2)]
            st["po"] += 2
            for c in range(2):
                r0 = 32 * c
                PO = POs[c]
                npair = len(chunks) // 2

                def pv(pi, e):
                    for u in range(2):
                        j = chunks[2 * pi + u]
                        p.mm(PO[:, 0:N], vC_sb[:, j, h, :], e[:, u * 512:u * 512 + N], pi == 0 and u == 0,
                             pi == npair - 1 and u == 1, [vC_sb, e], [PO])

                pend = []
                for pi in range(npair):
                    PS = ps[st["ps"] % NPS]
                    st["ps"] += 1
                    e = E[st["e"] % NE]
                    st["e"] += 1
                    for u in range(2):
                        j = chunks[2 * pi + u]
                        p.mm(PS[:, u * 512:u * 512 + N], kC_sb[:, h, j * 128:(j + 1) * 128], qm[c][:, h, 0:N],
                             True, True, [kC_sb, qm[c]], [PS])
                    if len(pend) == 1:
                        pv(*pend.pop(0))
                    p.act(e[:, :].rearrange("p (u n) -> p u n", u=2)[:, :, 0:N], PS[:, :].rearrange("p (u n) -> p u n", u=2)[:, :, 0:N],
                          AF.Exp, [PS], [e], scale=SC_C)
                    pend.append((pi, e))
                for pe_ in pend:
                    pv(*pe_)
            for c in range(2):
                p.recip(rz[c][:, 0:N], POs[c][64:128, 0:N], [POs[c]], [rz[c]])
                p.tt("dve", on[c][:, 0:N], POs[c][0:64, 0:N], rz[c][:, 0:N], ALU.mult, [POs[c], rz[c]], [on[c]])
            if 'dbg' in DBG and out_dram is oC_c:
                for c in range(2):
                    p.copy("dve", dbgz[c][:, 0:N], POs[c][:, 0:N], [POs[c]], [dbgz[c]])
                    p.dma("sp", dbg2[h, c, :, 0:N], dbgz[c][:, 0:N], [dbgz[c]], [])
                for c in range(2):
                    p.dma("sp", dbg[h, c, :, 0:N], rz[c][:, 0:N], [rz[c]], [])
                    p.dma("sp", dbg[h, 2 + c, :, 0:N], on[c][:, 0:N], [on[c]], [])
            p.stt("pool", o_t[:, 0:N], on[1][:, 0:N], sc2[0:64, 0:1], on[0][:, 0:N], ALU.mult, ALU.add,
                  [on[0], on[1], sc2], [o_t])
            p.tt("pool", sq_t[:, 0:N], o_t[:, 0:N], o_t[:, 0:N], ALU.mult, [o_t], [sq_t])
            pm = POs[0]
            p.mm(pm[0:64, 0:N], g64_sb[:, :], sq_t[:, 0:N], True, True, [g64_sb, sq_t], [pm])
            p.rsqrt(rs_t[:, 0:N], pm[0:64, 0:N], EPS, rtmp, [pm], [rs_t])
            o_b = ob[st["ob"] % 2]
            st["ob"] += 1
            p.stt("dve", o_b[:, 0:N], o_t[:, 0:N], sc2[0:64, 1:2], rs_t[:, 0:N], ALU.mult, ALU.mult, [o_t, sc2, rs_t], [o_b])
            p.dma("pool", out_dram[h * 64:(h + 1) * 64, o0:o0 + N], o_b[:, 0:N], [o_b], [])

    den = [p.sb("den%d" % i, [64, 512], F32) for i in range(2)]
    sinkL = p.sb("sinkL", [1, 128], BF16)
    esrow = p.sb("esrow", [1, 512], BF16)
    p.memset("pool", sinkL[:, 0:64], 0.0, [sinkL])
    p.memset("pool", sinkL[:, 64:128], 1.0, [sinkL])
    for h in range(4):
        p.copy("dve", esrow[0:1, h * 128:(h + 1) * 128], sc2[0:1, 2 + h:3 + h].to_broadcast([1, 128]), [sc2], [esrow])

    def win_pv(PO, first, last, kc, e, fin):
        p.mm(PO[:, :], vA_sb[:, kc, :], e[:, 0:512], first, False, [vA_sb, e], [PO])
        if not last:
            return
        p.mm(PO[:, :], sinkL[:, :], esrow[:, :], False, True, [sinkL, esrow], [PO])
        out_dram, o0 = fin
        dn = den[st["ob"] % 2]
        p.recip(dn[:, :], PO[64:128, :], [PO], [dn])
        o_b = ob[st["ob"] % 2]
        st["ob"] += 1
        p.tt("dve", o_b[:, :], PO[0:64, :], dn[:, :], ALU.mult, [PO, dn], [o_b])
        p.dma("pool", out_dram[:, o0:o0 + 128].rearrange("(h v) q -> v h q", v=64),
              o_b[:, :].rearrange("v (h q) -> v h q", h=4), [o_b], [])

    def win_blocks(blocks):
        pend = None
        for (q_sb, q0, chunks, out_dram, o0) in blocks:
            PO = po[st["po"] % 4]
            st["po"] += 1
            n = len(chunks)
            for ji, (kc, mt) in enumerate(chunks):
                PS = ps[st["ps"] % NPS]
                st["ps"] += 1
                e = E[st["e"] % NE]
                st["e"] += 1
                for h in range(4):
                    p.mm(PS[:, h * 128:(h + 1) * 128], kA_sb[:, kc * 128:(kc + 1) * 128],
                         q_sb[:, h, q0:q0 + 128], True, True, [kA_sb, q_sb], [PS])
                if pend is not None:
                    win_pv(*pend)
                p.act(e[:, 0:512], PS[:, 0:512], AF.Exp, [PS], [e], scale=SC_A)
                if mt is not None:
                    e3 = e[:, 0:512].rearrange("p (h q) -> p h q", h=4)
                    m3 = mask_sb[:, mt, :].unsqueeze(1).to_broadcast([128, 4, 128])
                    p.tt("dve", e3, e3, m3, ALU.mult, [e, mask_sb], [e])
                pend = (PO, ji == 0, ji == n - 1, kc, e, (out_dram, o0))
        if pend is not None:
            win_pv(*pend)

    ctx_ch = list(range(NLB, NCH))
    all_ch = list(range(NCH))
    return dict(diff_tile=diff_tile, win_blocks=win_blocks, ctx_ch=ctx_ch, all_ch=all_ch, NLB=NLB, NCH=NCH,
                load_qC=load_qC, load_qA=load_qA, qCc_sb=qCc_sb, qAc_sb=qAc_sb, oC=oC, oA=oA, oC_c=oC_c, oA_c=oA_c)


def emit_p2_attn(p, L, LC, need_ctx, lam_init, do_A=True, do_C=True, g=None, nat=None):
    p.make_eps(EPS)
    a = p2_attn(p, L, LC, need_ctx, lam_init, g, nat)
    NLB = a["NLB"]
    blocks = []
    if do_A:
        qbuf = None
        for j in range(NLB):
            if j % 4 == 0:
                qbuf = a["load_qA"](j // 4)
            ch = []
            if j > 0:
                ch.append((j - 1, 0))
            ch.append((j, None))
            if j + 1 < NLB:
                ch.append((j + 1, 1))
            ch += [(c, None) for c in a["ctx_ch"]]
            blocks.append((qbuf, (j % 4) * 128, ch, a["oA"], j * 128))
            if j % 4 == 3 or j == NLB - 1:
                a["win_blocks"](blocks)
                blocks = []
        if need_ctx:
            for j in range(LC // 128):
                blocks.append((a["qAc_sb"], j * 128, [(c, None) for c in a["ctx_ch"]], a["oA_c"], j * 128))
            a["win_blocks"](blocks)
    TQ = 512
    for t in range(L // TQ if (do_C and 'ctxonly' not in DBG) else 0):
        a["diff_tile"](a["load_qC"](t), 0, TQ, a["all_ch"], a["oC"], t * TQ)
    if need_ctx and do_C:
        a["diff_tile"](a["qCc_sb"], 0, LC, a["ctx_ch"], a["oC_c"], 0)


def build_p2_attn(L, LC, need_ctx, lam_init, do_A=True, do_C=True):
    nc = bass.Bass("TRN2", target_bir_lowering=False)
    p = Prog(nc)
    emit_p2_attn(p, L, LC, need_ctx, lam_init, do_A, do_C)
    p.finish()
    return nc


def attn_masks():
    import ml_dtypes
    i = np.arange(128)[:, None]
    j = np.arange(128)[None, :]
    return np.stack([(j <= i), (i <= j)], 0).astype(np.float32).astype(ml_dtypes.bfloat16)


def v_layout(v, heads):
    import ml_dtypes
    S = v.shape[0]
    out = np.ones((128, S // 128, heads, 128), ml_dtypes.bfloat16)
    out[:, :, :, 0:64] = v.reshape(S // 128, 128, heads, 64).transpose(1, 0, 2, 3)
    return out


def p2_hgrn(p, L, LC, need_ctx, g=None, nat=None):
    Tt = LC + L
    CH = 32
    qB = p.dram("qB", [2, 2, 64, Tt], BF16, "ExternalInput")
    kB = p.dram("kB", [2, 2, 64, Tt], BF16, "ExternalInput")
    if nat is None:
        LFb = p.dram("LFb", [2, 2, 128, Tt // 128, 64], F32, "ExternalInput")
        LFc = p.dram("LFc", [2, 2, 32, Tt // 32, 64], F32, "ExternalInput")
        KBc = p.dram("KBc", [2, 2, 32, Tt // 32, 64], BF16, "ExternalInput")
        vBc = p.dram("vBc", [2, 32, Tt // 32, 64], BF16, "ExternalInput")
    gB = p.dram("gBf", [2, 64, Tt], BF16, "ExternalInput")
    ogB = p.dram("ogB", [64, 1], F32, "ExternalInput")
    ucum = p.dram("ucum", [2, 128, 128], F32, "ExternalInput")
    lst = p.dram("lst", [2, 32, 32], F32, "ExternalInput")
    tri = p.dram("tri", [2, 32, 32], F32, "ExternalInput")
    g64 = p.dram("g64b", [64, 64], F32, "ExternalInput")
    oB = p.dram("oB", [128, L], BF16, "ExternalOutput")
    oB_c = p.dram("oB_c", [128, LC], BF16, "ExternalOutput")

    og_sb = p.sb("ogB", [64, 1], F32)
    ucum_sb = p.sb("ucum", [128, 2, 128], F32)
    lst_sb = p.sb("lst", [32, 2, 32], F32)
    tri_sb = p.sb("tri", [32, 2, 32], F32)
    g64_sb = p.sb("g64b", [64, 64], F32)
    p.dma("sp", og_sb[:], ogB[:], [ogB], [og_sb])
    p.dma("sp", ucum_sb[:], ucum[:].rearrange("c p n -> p c n"), [ucum], [ucum_sb])
    p.dma("sp", lst_sb[:], lst[:].rearrange("c p n -> p c n"), [lst], [lst_sb])
    p.dma("sp", tri_sb[:], tri[:].rearrange("c p n -> p c n"), [tri], [tri_sb])
    p.dma("sp", g64_sb[:], g64[:], [g64], [g64_sb])

    o_acc = [p.sb("oacc%d" % h, [64, Tt], F32) for h in range(2)]
    NSC = 4
    S = [p.sb("S%d" % i, [64, 64], F32) for i in range(NSC)]
    S_bf = [p.sb("Sbf%d" % i, [64, 64], BF16) for i in range(NSC)]
    for i in range(NSC):
        p.memset("pool", S[i][:, :], 0.0, [S[i]])
        p.memset("pool", S_bf[i][:, :], 0.0, [S_bf[i]])
    NPB = 2
    def mk(nm, shape, dt):
        return [[p.sb("%s%d_%d" % (nm, i, j), shape, dt) for j in range(NPB)] for i in range(NSC)]
    qe = mk("qe", [64, 512], BF16)
    ke = mk("ke", [64, 512], BF16)
    qb = mk("qb", [64, 512], BF16)
    kend = mk("kend", [32, 1024], BF16)
    ebend = mk("ebend", [64, 16], F32)
    Vc = mk("Vc", [32, 1024], BF16)
    ATm = mk("ATm", [32, 512], BF16)
    NT_ = 4
    LFt_sb = [p.sb("LFt%d" % i, [128, 4, 64], F32) for i in range(NT_)]
    LFc_sb = [p.sb("LFc%d" % i, [32, 1024], F32) for i in range(NT_)]
    Kc_sb = [p.sb("Kc%d" % i, [32, 1024], BF16) for i in range(NT_)]
    qT_sb = [p.sb("qTb%d" % i, [64, 512], BF16) for i in range(NT_)]
    kT_sb = [p.sb("kTb%d" % i, [64, 512], BF16) for i in range(NT_)]
    bT_sb = [p.sb("bT%d" % i, [64, 512], F32) for i in range(2)]
    d1_sb = [p.sb("d1%d" % i, [64, 512], F32) for i in range(2)]
    E_sb = [p.sb("Eb%d" % i, [64, 512], F32) for i in range(3)]
    ec_sb = [p.sb("ec%d" % i, [32, 512], F32) for i in range(2)]
    gT_sb = [p.sb("gTb%d" % i, [64, 512], BF16) for i in range(2)]
    o_t = [p.sb("otb%d" % i, [64, 512], F32) for i in range(2)]
    sq_t = p.sb("sqb", [64, 512], F32)
    rs_t = p.sb("rsb", [64, 512], F32)
    rtmp = p.sb("rtmpb", [64, 512], F32)
    on_t = p.sb("onb", [64, 512], F32)
    ob_t = [p.sb("obb%d" % i, [64, 512], BF16) for i in range(2)]
    PB = p.ps("PB")
    PC = PB
    PA = p.ps("PA")
    PO = [p.ps("PO%d" % i) for i in range(NSC)]
    PD = [p.ps("PD%d" % i) for i in range(2)]
    for tv in PD:
        tv.view = tv[0:64, 0:64]
    st = {"t": 0, "e": 0, "ec": 0, "pd": 0, "g": 0, "ot": 0, "ob": 0}
    seen = {}

    nlat = L // 512
    sbs_lat = [("lat", j, LC + 512 * j, 512) for j in range(nlat)]
    ctxsb = ("ctx", 0, 0, LC)
    order = [[ctxsb] + sbs_lat, [ctxsb] + sbs_lat[::-1]]

    def prep(sc, d, h, sb, par):
        _, _, t0, n = sb
        nch = n // CH
        nb = n // 128
        ti = st["t"] % NT_
        st["t"] += 1
        lft, lfc, kc, qt, kt, bt, d1 = LFt_sb[ti], LFc_sb[ti], Kc_sb[ti], qT_sb[ti], kT_sb[ti], bT_sb[ti % 2], d1_sb[ti % 2]
        hs = slice(h * 64, (h + 1) * 64)
        b0 = t0 // 128
        c0 = t0 // CH
        if nat is None:
            p.dma("sp", lft[:, 0:nb, :], LFb[d, h, :, b0:b0 + nb, :], [LFb], [lft])
            p.dma("sp", lfc[:, 0:nch * 64].rearrange("s (c k) -> s c k", k=64), LFc[d, h, :, c0:c0 + nch, :], [LFc], [lfc])
            p.dma("sp", kc[:, 0:nch * 64].rearrange("s (c k) -> s c k", k=64), KBc[d, h, :, c0:c0 + nch, :], [KBc], [kc])
        else:
            ncs = slice((2 * g + h) * 64, (2 * g + h + 1) * 64)
            p.dma("sp", lft[:, 0:nb, :], nat["LF"][d, t0:t0 + n, ncs].rearrange("(b t) k -> t b k", t=128), [], [lft])
            p.dma("sp", lfc[:, 0:nch * 64].rearrange("s (c k) -> s c k", k=64),
                  nat["LF"][d, t0:t0 + n, ncs].rearrange("(c s) k -> s c k", s=CH), [], [lfc])
            p.dma("sp", kc[:, 0:nch * 64].rearrange("s (c k) -> s c k", k=64),
                  nat["KB"][d, t0:t0 + n, ncs].rearrange("(c s) k -> s c k", s=CH), [], [kc])
        p.dma("sp", qt[:, 0:n], qB[d, h, :, t0:t0 + n], [qB], [qt])
        p.dma("sp", kt[:, 0:n], kB[d, h, :, t0:t0 + n], [kB], [kt])
        vc = Vc[sc][par]
        if nat is None:
            p.dma("sp", vc[:, 0:nch * 64].rearrange("s (c k) -> s c k", k=64), vBc[h, :, c0:c0 + nch, :], [vBc], [vc])
        else:
            p.dma("sp", vc[:, 0:nch * 64].rearrange("s (c k) -> s c k", k=64),
                  nat["vB"][t0:t0 + n, ncs].rearrange("(c s) k -> s c k", s=CH), [], [vc])
        yield
        for b in range(nb):
            p.mm(PB[0:64, b * 128:(b + 1) * 128], lft[:, b, :], ucum_sb[:, d, :], True, True, [lft, ucum_sb], [PB])
        p.copy("act", bt[:, 0:n], PB[0:64, 0:n], [PB], [bt])
        yield
        bt3 = bt[:, 0:n].rearrange("p (c s) -> p c s", s=CH)
        d13 = d1[:, 0:n].rearrange("p (c s) -> p c s", s=CH)
        p.tt("dve", d13, bt3, bt3[:, :, 16:17].to_broadcast([64, nch, CH]), ALU.subtract, [bt], [d1])
        yield
        e1 = E_sb[st["e"] % 3]; st["e"] += 1
        p.act(e1[:, 0:n], d1[:, 0:n], AF.Exp, [d1], [e1])
        p.tt("pool", qe[sc][par][:, 0:n], qt[:, 0:n], e1[:, 0:n], ALU.mult, [qt, e1], [qe[sc][par]])
        yield
        e2 = E_sb[st["e"] % 3]; st["e"] += 1
        p.act(e2[:, 0:n], d1[:, 0:n], AF.Exp, [d1], [e2], scale=-1.0)
        p.tt("dve", ke[sc][par][:, 0:n], kt[:, 0:n], e2[:, 0:n], ALU.mult, [kt, e2], [ke[sc][par]])
        yield
        e3 = E_sb[st["e"] % 3]; st["e"] += 1
        p.act(e3[:, 0:n], bt[:, 0:n], AF.Exp, [bt], [e3])
        p.tt("pool", qb[sc][par][:, 0:n], qt[:, 0:n], e3[:, 0:n], ALU.mult, [qt, e3], [qb[sc][par]])
        eidx = CH - 1 if d == 0 else 0
        p.act(ebend[sc][par][:, 0:nch], bt3[:, :, eidx], AF.Exp, [bt], [ebend[sc][par]])
        yield
        for hf in range(nch * 64 // 512):
            p.mm(PC[0:32, :], lst_sb[:, d, :], lfc[:, hf * 512:(hf + 1) * 512], True, True, [lst_sb, lfc], [PC])
            ec = ec_sb[st["ec"] % 2]; st["ec"] += 1
            p.act(ec[:, :], PC[0:32, :], AF.Exp, [PC], [ec])
            p.tt("dve", kend[sc][par][:, hf * 512:(hf + 1) * 512], kc[:, hf * 512:(hf + 1) * 512], ec[:, :], ALU.mult,
                 [kc, ec], [kend[sc][par]])
            yield
        for c in range(nch):
            p.mm(PA[0:32, c * CH:(c + 1) * CH], ke[sc][par][:, c * CH:(c + 1) * CH], qe[sc][par][:, c * CH:(c + 1) * CH], True, True,
                 [ke[sc][par], qe[sc][par]], [PA])
        p.tt("dve", ATm[sc][par][:, 0:n].rearrange("p (c s) -> p c s", s=CH), PA[0:32, 0:n].rearrange("p (c s) -> p c s", s=CH),
             tri_sb[:, d, :].unsqueeze(1).to_broadcast([32, nch, CH]), ALU.mult, [PA, tri_sb], [ATm[sc][par]])

    def chunk(sc, par, c):
        vc = Vc[sc][par]
        p.mm(PO[sc][0:64, c * CH:(c + 1) * CH], vc[:, c * 64:(c + 1) * 64], ATm[sc][par][:, c * CH:(c + 1) * CH], True, False,
             [vc, ATm[sc][par]], [PO[sc]])
        p.mm(PO[sc][0:64, c * CH:(c + 1) * CH], S_bf[sc][:, :], qb[sc][par][:, c * CH:(c + 1) * CH], False, True,
             [S_bf[sc], qb[sc][par]], [PO[sc]])
        if 'c1' in DBG:
            return
        pd = PD[st["pd"] % 2]; st["pd"] += 1
        p.mm(pd.view, kend[sc][par][:, c * 64:(c + 1) * 64], vc[:, c * 64:(c + 1) * 64], True, True, [kend[sc][par], vc], [pd])
        if 'c2' in DBG:
            return
        p.stt("dve", S[sc][:, :], S[sc][:, :], ebend[sc][par][:, c:c + 1], pd.view, ALU.mult, ALU.add,
              [S[sc], ebend[sc][par], pd], [S[sc]])
        if 'c3' in DBG:
            return
        p.copy("pool", S_bf[sc][:, :], S[sc][:, :], [S[sc]], [S_bf[sc]])

    def finalize(sc, h, sb):
        kind, j, t0, n = sb
        if 'nofin' in DBG:
            return
        key = (h, kind, j)
        acc = o_acc[h]
        if key not in seen:
            seen[key] = 1
            p.copy("act", acc[:, t0:t0 + n], PO[sc][0:64, 0:n], [PO[sc]], [acc])
            return
        if kind == "ctx" and not need_ctx:
            return
        o = o_t[st["ot"] % 2]; st["ot"] += 1
        p.tt("dve", o[:, 0:n], PO[sc][0:64, 0:n], acc[:, t0:t0 + n], ALU.add, [PO[sc], acc], [o])
        gt = gT_sb[st["g"] % 2]; st["g"] += 1
        p.dma("sp", gt[:, 0:n], gB[h, :, t0:t0 + n], [gB], [gt])
        p.tt("pool", sq_t[:, 0:n], o[:, 0:n], o[:, 0:n], ALU.mult, [o], [sq_t])
        p.mm(PB[0:64, 0:n], g64_sb[:, :], sq_t[:, 0:n], True, True, [g64_sb, sq_t], [PB])
        p.rsqrt(rs_t[:, 0:n], PB[0:64, 0:n], EPS, rtmp, [PB], [rs_t])
        p.stt("dve", on_t[:, 0:n], o[:, 0:n], og_sb[:, 0:1], rs_t[:, 0:n], ALU.mult, ALU.mult, [o, og_sb, rs_t], [on_t])
        ob = ob_t[st["ob"] % 2]; st["ob"] += 1
        p.tt("pool", ob[:, 0:n], on_t[:, 0:n], gt[:, 0:n], ALU.mult, [on_t, gt], [ob])
        if kind == "ctx":
            p.dma("pool", oB_c[h * 64:(h + 1) * 64, 0:n], ob[:, 0:n], [ob], [])
        else:
            p.dma("pool", oB[h * 64:(h + 1) * 64, t0 - LC:t0 - LC + n], ob[:, 0:n], [ob], [])

    def run():
        nsteps = 1 + nlat

        def make(step):
            par = step % NPB
            scans, gens = [], []
            for d in range(2):
                sb = order[d][step]
                for h in range(2):
                    sc = d * 2 + h
                    scans.append((sc, d, h, sb))
                    gens.append(prep(sc, d, h, sb, par))
            return scans, gens

        def advance(gens, k):
            for _ in range(k):
                while gens:
                    try:
                        next(gens[0])
                        break
                    except StopIteration:
                        gens.pop(0)

        scans, gens = make(0)
        for g_ in gens:
            next(g_)
        advance(gens, 10 ** 6)
        for step in range(nsteps):
            par = step % NPB
            nxt = None
            if step + 1 < nsteps:
                nxt = make(step + 1)
                for g_ in nxt[1]:
                    next(g_)
            nch = scans[0][3][3] // CH
            per = (4 * 11 + nch - 1) // nch
            for ci in range(nch):
                for (sc, d, h, sb) in scans:
                    chunk(sc, par, ci if d == 0 else nch - 1 - ci)
                if nxt is not None:
                    advance(nxt[1], per)
            if nxt is not None:
                advance(nxt[1], 10 ** 6)
            for (sc, d, h, sb) in scans:
                finalize(sc, h, sb)
            if nxt is not None:
                scans = nxt[0]

    return run


def emit_p2_hgrn(p, L, LC, need_ctx, g=None, nat=None):
    p.make_eps(EPS)
    run = p2_hgrn(p, L, LC, need_ctx, g, nat)
    run()


def build_p2_hgrn(L, LC, need_ctx):
    nc = bass.Bass("TRN2", target_bir_lowering=False)
    p = Prog(nc)
    emit_p2_hgrn(p, L, LC, need_ctx)
    p.finish()
    return nc


def hgrn_consts():
    t = np.arange(128)
    same = (t[:, None] // 32) == (t[None, :] // 32)
    ucum = np.stack([same & (t[:, None] <= t[None, :]), same & (t[:, None] >= t[None, :])], 0).astype(np.float32)
    s = np.arange(32)
    lst = np.stack([s[:, None] > s[None, :], s[:, None] < s[None, :]], 0).astype(np.float32)
    tri = np.stack([s[:, None] <= s[None, :], s[:, None] >= s[None, :]], 0).astype(np.float32)
    return ucum, lst, tri


def hgrn_layouts(LF, KB, vB):
    Tt = LF.shape[1]
    LFb = np.ascontiguousarray(LF.reshape(2, Tt // 128, 128, 2, 64).transpose(0, 3, 2, 1, 4))
    LFc = np.ascontiguousarray(LF.reshape(2, Tt // 32, 32, 2, 64).transpose(0, 3, 2, 1, 4))
    KBc = np.ascontiguousarray(KB.reshape(2, Tt // 32, 32, 2, 64).transpose(0, 3, 2, 1, 4))
    vBc = np.ascontiguousarray(vB.reshape(Tt // 32, 32, 2, 64).transpose(2, 1, 0, 3))
    return LFb, LFc, KBc, vBc


def emit_p3a(p, T_list):
    p.make_eps(EPS)
    KC = 8
    w_out = p.dram("w_out", [D, D], F32, "ExternalInput")
    mod3 = p.dram("mod3", [128, 6, KC], F32, "ExternalInput")
    cones = p.dram("cones", [128, 128], F32, "ExternalInput")
    segs = []
    for i, Tn in enumerate(T_list):
        segs.append(dict(
            T=Tn,
            oT=p.dram("oT%d" % i, [D, Tn], BF16, "ExternalInput"),
            xT=p.dram("xT%d" % i, [D, Tn], F32, "ExternalInput"),
            xmT=p.dram("xmT%d" % i, [D, Tn], F32, "ExternalOutput"),
            h2T=p.dram("h2T%d" % i, [D, Tn], BF16, "ExternalOutput")))
    w_sb = [p.sb("wo%d" % k, [128, D], BF16) for k in range(KC)]
    mod_sb = p.sb("mod3", [128, 6, KC], F32)
    ones_sb = p.sb("cones", [128, 128], F32)
    p.dma("sp", mod_sb[:], mod3[:], [mod3], [mod_sb])
    p.dma("sp", ones_sb[:], cones[:], [cones], [ones_sb])
    stg = [p.sb("stg%d" % i, [128, D], F32) for i in range(3)]
    ceng = ["dve", "pool", "act"]
    for k in range(KC):
        s = stg[k % 3]
        p.dma("sp", s[:, :], w_out[k * 128:(k + 1) * 128, :], [w_out], [s])
        p.copy(ceng[k % 3], w_sb[k][:, :], s[:, :], [s], [w_sb[k]])
    TW = 512
    NB = 2
    o_sb = [p.sb("o%d" % i, [128, KC, TW], BF16) for i in range(NB)]
    x_sb = [p.sb("x%d" % i, [128, KC, TW], F32) for i in range(NB)]
    xm_sb = [[p.sb("xm%d_%d" % (i, k), [128, TW], F32) for k in range(KC)] for i in range(NB)]
    sq_sb = [p.sb("sq%d" % i, [128, TW], F32) for i in range(2)]
    rstd_sb = p.sb("rstd", [128, TW], F32)
    rtmp = p.sb("rtmp", [128, TW], F32)
    hx_sb = [p.sb("hx%d" % i, [128, TW], F32) for i in range(2)]
    h_sb = [p.sb("h%d" % i, [128, TW], BF16) for i in range(3)]
    pp = [p.ps("pp%d" % i) for i in range(2)]
    pss = p.ps("pss")
    cc = {"t": 0, "h": 0}
    for si, sg in enumerate(segs):
        Tn = sg["T"]
        tw = min(TW, Tn)
        for t in range(Tn // tw):
            bi = cc["t"] % NB
            cc["t"] += 1
            ob, xb, xm = o_sb[bi], x_sb[bi], xm_sb[bi]
            c0 = t * tw
            p.dma("sp", ob[:, :, 0:tw], sg["oT"][:, c0:c0 + tw].rearrange("(k p) n -> p k n", p=128), [sg["oT"]], [ob])
            p.dma("sp", xb[:, :, 0:tw], sg["xT"][:, c0:c0 + tw].rearrange("(k p) n -> p k n", p=128), [sg["xT"]], [xb])
            for m in range(KC):
                P = pp[m % 2]
                for k in range(KC):
                    p.mm(P[:, 0:tw], w_sb[k][:, m * 128:(m + 1) * 128], ob[:, k, 0:tw], k == 0, k == KC - 1, [w_sb[k], ob], [P])
                p.stt("dve", xm[m][:, 0:tw], P[:, 0:tw], mod_sb[:, 3 * si, m:m + 1], xb[:, m, 0:tw], ALU.mult, ALU.add,
                      [P, mod_sb, xb], [xm[m]])
                p.dma("pool", sg["xmT"][m * 128:(m + 1) * 128, c0:c0 + tw], xm[m][:, 0:tw], [xm[m]], [])
            for k in range(KC):
                s = sq_sb[k % 2]
                p.act(s[:, 0:tw], xm[k][:, 0:tw], AF.Square, [xm[k]], [s])
                p.mm(pss[:, 0:tw], ones_sb[:, :], s[:, 0:tw], k == 0, k == KC - 1, [ones_sb, s], [pss])
            p.rsqrt(rstd_sb[:, 0:tw], pss[:, 0:tw], EPS, rtmp, [pss], [rstd_sb])
            for k in range(KC):
                hx = hx_sb[k % 2]
                p.stt("dve", hx[:, 0:tw], xm[k][:, 0:tw], mod_sb[:, 3 * si + 1, k:k + 1], rstd_sb[:, 0:tw], ALU.mult, ALU.mult,
                      [xm[k], mod_sb, rstd_sb], [hx])
                h = h_sb[cc["h"] % 3]
                cc["h"] += 1
                p.act(h[:, 0:tw], hx[:, 0:tw], AF.Identity, [hx, mod_sb], [h], bias=mod_sb[:, 3 * si + 2, k:k + 1])
                p.dma("pool", sg["h2T"][k * 128:(k + 1) * 128, c0:c0 + tw], h[:, 0:tw], [h], [])


def build_p3a(T_list):
    nc = bass.Bass("TRN2", target_bir_lowering=False)
    p = Prog(nc)
    emit_p3a(p, T_list)
    p.finish()
    return nc


LAM_INIT = [0.8 - 0.6 * float(np.exp(-0.3 * l)) for l in range(2)]


def emit_p0(p):
    KC = 8
    NJ = 48
    cvec = p.dram("cvec", [128, KC, 2], F32, "ExternalInput")
    w_mod = p.dram("w_mod", [2, D, 6 * D], F32, "ExternalInput")
    b_mod = p.dram("b_modT", [2, 128, NJ], F32, "ExternalInput")
    ng = p.dram("ngT", [2, 128, 2, KC], F32, "ExternalInput")
    hgl = p.dram("hglT", [128, 2, 4], F32, "ExternalInput")
    hgr = p.dram("hglR", [1, 2, 512], F32, "ExternalInput")
    dlam = p.dram("dlam", [1, 2, 4, 32], F32, "ExternalInput")
    modall = p.dram("modall", [2, 128, 12, KC], F32, "ExternalOutput")
    oml_o = p.dram("oml", [128, 2, 4], F32, "ExternalOutput")
    lbrow_o = p.dram("lbrow", [1, 2, 512], F32, "ExternalOutput")
    lam_o = p.dram("lamb", [128, 2], F32, "ExternalOutput")

    c_sb = p.sb("c", [128, KC, 2], F32)
    sg_sb = p.sb("sg", [128, KC, 2], F32)
    sc_sb = p.sb("sc", [128, KC, 2], F32)
    p.dma("sp", c_sb[:], cvec[:], [cvec], [c_sb])
    p.act(sg_sb[:], c_sb[:], AF.Sigmoid, [c_sb], [sg_sb])
    p.tt("dve", sc_sb[:], c_sb[:], sg_sb[:], ALU.mult, [c_sb, sg_sb], [sc_sb])
    bm_sb = p.sb("bm", [128, 2, NJ], F32)
    ng_sb = p.sb("ng", [128, 2, 2, KC], F32)
    p.dma("sp", bm_sb[:], b_mod[:].rearrange("l p j -> p l j"), [b_mod], [bm_sb])
    p.dma("sp", ng_sb[:], ng[:].rearrange("l p a k -> p l a k"), [ng], [ng_sb])
    wst = [p.sb("wst%d" % i, [128, KC, 512], F32) for i in range(2)]
    pm_ = [p.ps("pm%d" % i) for i in range(2)]
    raw = p.sb("raw", [128, 2, NJ, 2], F32)
    wi = 0
    for l in range(2):
        for piece in range(12):
            w = wst[wi % 2]
            wi += 1
            for k in range(KC):
                p.dma("sp", w[:, k, :], w_mod[l, k * 128:(k + 1) * 128, piece * 512:(piece + 1) * 512], [w_mod], [w])
            for jj in range(4):
                j = piece * 4 + jj
                P = pm_[j % 2]
                for k in range(KC):
                    p.mm(P[:, 0:2], w[:, k, jj * 128:(jj + 1) * 128], sc_sb[:, k, :], k == 0, k == KC - 1, [w, sc_sb], [P])
                p.ts("dve", raw[:, l, j, :], P[:, 0:2], bm_sb[:, l, j:j + 1], None, ALU.add, None, [P, bm_sb], [raw])
    out_sb = p.sb("outm", [128, 2, 12, KC], F32)
    tmp = p.sb("tmpm", [128, KC], F32)
    for l in range(2):
        for v in range(2):
            def grp(g):
                return raw[:, l, g * 8:(g + 1) * 8, v]
            base = 0 if v == 0 else 2
            p.ts("dve", tmp[:, :], grp(1), 1.0, None, ALU.add, None, [raw], [tmp])
            p.tt("dve", out_sb[:, l, base + 0, :], tmp[:, :], ng_sb[:, l, 0, :], ALU.mult, [tmp, ng_sb], [out_sb])
            p.copy("dve", out_sb[:, l, base + 1, :], grp(0), [raw], [out_sb])
            b3 = 4 if v == 0 else 7
            p.copy("dve", out_sb[:, l, b3 + 0, :], grp(2), [raw], [out_sb])
            p.ts("dve", tmp[:, :], grp(4), 1.0, None, ALU.add, None, [raw], [tmp])
            p.tt("dve", out_sb[:, l, b3 + 1, :], tmp[:, :], ng_sb[:, l, 1, :], ALU.mult, [tmp, ng_sb], [out_sb])
            p.copy("dve", out_sb[:, l, b3 + 2, :], grp(3), [raw], [out_sb])
            p.copy("dve", out_sb[:, l, 10 + v, :], grp(5), [raw], [out_sb])
        p.dma("sp", modall[l], out_sb[:, l, :, :], [out_sb], [])

    def lower(src_ap, shape, np_, nm):
        r = p.sb(nm + "r", shape, F32)
        p.dma("sp", r[:], src_ap, [], [r])
        mx = p.sb(nm + "mx", [shape[0], shape[2]], F32)
        e = p.sb(nm + "e", shape, F32)
        den = p.sb(nm + "den", [shape[0], shape[2]], F32)
        pr = p.sb(nm + "p", shape, F32)
        lbt = p.sb(nm + "lb", shape, F32)
        p.tt("dve", mx[:, :], r[:, 0, :], r[:, 1, :], ALU.max, [r], [mx])
        for l in range(2):
            p.tt("dve", e[:, l, :], r[:, l, :], mx[:, :], ALU.subtract, [r, mx], [e])
        p.act(e[:], e[:], AF.Exp, [e], [e])
        p.tt("dve", den[:, :], e[:, 0, :], e[:, 1, :], ALU.add, [e], [den])
        p.recip(den[:, :], den[:, :], [den], [den])
        for l in range(2):
            p.tt("dve", pr[:, l, :], e[:, l, :], den[:, :], ALU.mult, [e, den], [pr])
        p.tt("dve", lbt[:, 0, :], pr[:, 0, :], pr[:, 0, :], ALU.subtract, [pr], [lbt])
        p.tt("dve", lbt[:, 1, :], pr[:, 0, :], pr[:, 1, :], ALU.add, [pr], [lbt])
        p.tt("dve", lbt[:, 1, :], lbt[:, 1, :], pr[:, 0, :], ALU.subtract, [lbt, pr], [lbt])
        return lbt
    lbT = lower(hgl[:], [128, 2, 4], 128, "lp")
    oml_sb = p.sb("omlo", [128, 2, 4], F32)
    p.ts("dve", oml_sb[:], lbT[:], -1.0, 1.0, ALU.mult, ALU.add, [lbT], [oml_sb])
    p.dma("sp", oml_o[:], oml_sb[:], [oml_sb], [])
    lbR = lower(hgr[:], [1, 2, 512], 1, "lr")
    p.dma("sp", lbrow_o[:], lbR[:], [lbR], [])

    dl = p.sb("dl", [1, 2, 4, 32], F32)
    p.dma("sp", dl[:], dlam[:], [dlam], [dl])
    pr2 = p.sb("pr2", [1, 2, 2, 32], F32)
    for l in range(2):
        for a in range(2):
            p.tt("dve", pr2[:, l, a, :], dl[:, l, 2 * a, :], dl[:, l, 2 * a + 1, :], ALU.mult, [dl], [pr2])
    ssum = p.sb("ssum", [1, 4], F32)
    p.op("dve", lambda E: E.reduce_sum(ssum[:, :], pr2[:].rearrange("o l a d -> o (l a) d"), AX.X), [pr2], [ssum])
    p.act(ssum[:, :], ssum[:, :], AF.Exp, [ssum], [ssum])
    lam1 = p.sb("lam1", [1, 2], F32)
    for l in range(2):
        p.tt("dve", lam1[:, l:l + 1], ssum[:, 2 * l:2 * l + 1], ssum[:, 2 * l + 1:2 * l + 2], ALU.subtract, [ssum], [lam1])
        p.ts("dve", lam1[:, l:l + 1], lam1[:, l:l + 1], float(LAM_INIT[l]), None, ALU.add, None, [lam1], [lam1])
    ones1 = p.sb("ones1", [1, 128], F32)
    p.memset("pool", ones1[:, :], 1.0, [ones1])
    P = pm_[0]
    p.mm(P[:, 0:2], ones1[:, :], lam1[:, :], True, True, [ones1, lam1], [P])
    lam_sb = p.sb("lamsb", [128, 2], F32)
    p.copy("dve", lam_sb[:, :], P[:, 0:2], [P], [lam_sb])
    p.dma("sp", lam_o[:], lam_sb[:], [lam_sb], [])


def build_p0():
    nc = bass.Bass("TRN2", target_bir_lowering=False)
    p = Prog(nc)
    emit_p0(p)
    p.finish()
    return nc


_CACHE = {}


def _prog(name, fn, *args):
    key = (name,) + tuple(str(a) for a in args)
    if key not in _CACHE:
        _CACHE[key] = fn(*args)
    return _CACHE[key]


def _run(nc, in_maps):
    res = run_bass_kernel_spmd(nc, in_maps, core_ids=list(range(NCORES)))
    return res.results


def _pp(v):
    return np.ascontiguousarray(np.asarray(v).reshape(8, 128).T)


def kernel_unfused(x, c, ctx, c_ctx, w_mod, b_mod, norm1_g, norm2_g, w_in, win_qnorm_g, win_knorm_g, win_sink,
           hg_lower, hg_onorm_g, diff_qnorm_g, diff_knorm_g, diff_lambda, diff_onorm_g, w_out,
           w_up, conv_w, conv_b, w_down):
    f32 = np.float32
    A = lambda a: np.ascontiguousarray(np.asarray(a, dtype=f32))
    x, c, ctx, c_ctx = A(x), A(c), A(ctx), A(c_ctx)
    w_mod, b_mod, norm1_g, norm2_g, w_in = A(w_mod), A(b_mod), A(norm1_g), A(norm2_g), A(w_in)
    win_qnorm_g, win_knorm_g, win_sink, hg_lower, hg_onorm_g = A(win_qnorm_g), A(win_knorm_g), A(win_sink), A(hg_lower), A(hg_onorm_g)
    diff_qnorm_g, diff_knorm_g, diff_lambda, diff_onorm_g = A(diff_qnorm_g), A(diff_knorm_g), A(diff_lambda), A(diff_onorm_g)
    w_out, w_up, conv_w, conv_b, w_down = A(w_out), A(w_up), A(conv_w), A(conv_b), A(w_down)
    B, L, _ = x.shape
    LC = ctx.shape[1]
    TL = L // 2
    DEPTH = w_in.shape[0]
    cores = [(b, r) for b in range(B) for r in range(2)]
    cat = np.concatenate
    C_ = np.ascontiguousarray

    nc0 = _prog("p0", build_p0)
    b_modT = C_(b_mod.reshape(DEPTH, 48, 128).transpose(0, 2, 1))
    ngT = C_(np.stack([np.stack([_pp(norm1_g[l]), _pp(norm2_g[l])], 1) for l in range(DEPTH)], 0))
    hglT = C_(hg_lower.reshape(DEPTH, 4, 128).transpose(2, 0, 1))
    hglR = C_(hg_lower.reshape(1, DEPTH, 512))
    dl = C_(diff_lambda.reshape(1, DEPTH, 4, 32))
    ims = []
    for (b, r) in cores:
        ims.append(dict(cvec=C_(np.stack([_pp(c[b]), _pp(c_ctx)], -1)), w_mod=w_mod, b_modT=b_modT, ngT=ngT,
                        hglT=hglT, hglR=hglR, dlam=dl))
    r0 = _run(nc0, ims)
    modall = [r0[i]["modall"] for i in range(NCORES)]
    oml = r0[0]["oml"]
    lbrow = r0[0]["lbrow"]
    lamb = r0[0]["lamb"]

    xT = [C_(x[b, r * TL:(r + 1) * TL].T) for (b, r) in cores]
    xcT = [C_(ctx[b].T) for b in range(B)]
    cm, rm = const_mats()
    ropes = [rope_tables(r * TL + np.arange(TL)) for r in range(2)]
    masks = attn_masks()
    ucum, lst, tri = hgrn_consts()
    g64 = np.full((64, 64), 1.0 / 64, f32)

    for l in range(DEPTH):
        need_ctx = l < DEPTH - 1
        nc1 = _prog("p1", build_p1, TL, LC)
        gains = C_(np.stack([np.tile(win_qnorm_g[l], 2), np.tile(win_knorm_g[l], 2),
                             np.tile(diff_qnorm_g[l], 4), np.tile(diff_knorm_g[l], 4)], 1))
        ims = []
        for i, (b, r) in enumerate(cores):
            ims.append(dict(xT=xT[i], xcT=xcT[b], mod=C_(modall[i][l][:, 0:4, :]), w_in=w_in[l], gains=gains,
                            lbT=C_(oml[:, l, :]), lbrow=C_(lbrow[0, l].reshape(2, 256)), ropeT=ropes[r], cmat=cm, rmat=rm))
        r1 = _run(nc1, ims)

        def full(b, nm, axis):
            return cat([r1[2 * b][nm], r1[2 * b + 1][nm]], axis=axis)

        nca = _prog("p2a", build_p2_attn, L, LC, need_ctx, LAM_INIT[l])
        ncb = _prog("p2b", build_p2_hgrn, L, LC, need_ctx)
        ima, imb = [], []
        for i, (b, r) in enumerate(cores):
            P_ = r1[2 * b]
            kC_all = cat([full(b, "kC", 1), P_["kC_c"]], 1)
            vC_all = cat([full(b, "vC", 0), P_["vC_c"]], 0)
            kA_all = cat([full(b, "kA", 1), P_["kA_c"]], 1)
            vA_all = cat([full(b, "vA", 0), P_["vA_c"]], 0)
            scal = np.zeros((128, 8), f32)
            scal[:, 0] = lamb[:, l]
            scal[:, 1] = np.tile(diff_onorm_g[l], 2)
            scal[:, 2:6] = win_sink[l][4 * r:4 * r + 4][None, :]
            ima.append(dict(
                qC=C_(full(b, "qC", 1)[r * 128:(r + 1) * 128]), qC_c=C_(P_["qC_c"][r * 128:(r + 1) * 128]),
                kC=C_(kC_all[r * 128:(r + 1) * 128]), vCp=v_layout(vC_all[:, r * 128:(r + 1) * 128], 2),
                qA=C_(full(b, "qA", 1)[r * 256:(r + 1) * 256]), qA_c=C_(P_["qA_c"][r * 256:(r + 1) * 256]),
                kA=C_(kA_all[r * 64:(r + 1) * 64]), vAp=C_(v_layout(vA_all[:, r * 64:(r + 1) * 64], 1)[:, :, 0, :]),
                scal=scal, masks=masks, g64=g64))
            hs = slice(r * 128, (r + 1) * 128)
            qB = np.stack([cat([P_["qB%d_c" % d], full(b, "qB%d" % d, 1)], 1)[hs].reshape(2, 64, LC + L) for d in range(2)], 0)
            kB = np.stack([cat([P_["kB%d_c" % d], full(b, "kB%d" % d, 1)], 1)[hs].reshape(2, 64, LC + L) for d in range(2)], 0)
            LFa = cat([P_["LF_c"], full(b, "LF", 1)], 1)[:, :, hs]
            KBa = cat([P_["KB_c"], full(b, "KB", 1)], 1)[:, :, hs]
            vBa = cat([P_["vB_c"], full(b, "vB", 0)], 0)[:, hs]
            gBa = cat([P_["gB_c"], full(b, "gB", 1)], 1)[hs].reshape(2, 64, LC + L)
            LFb_, LFc_, KBc_, vBc_ = hgrn_layouts(C_(LFa), C_(KBa), C_(vBa))
            imb.append(dict(qB=C_(qB), kB=C_(kB), LFb=LFb_, LFc=LFc_, KBc=KBc_, vBc=vBc_, gBf=C_(gBa),
                            ogB=C_(hg_onorm_g[l].reshape(64, 1)), ucum=ucum, lst=lst, tri=tri, g64b=g64))
        ra = _run(nca, ima)
        rb = _run(ncb, imb)

        segT = [TL] + ([LC] if need_ctx else [])
        nc3 = _prog("p3a", build_p3a, segT)
        ims = []
        for i, (b, r) in enumerate(cores):
            oT = cat([ra[2 * b]["oA"], ra[2 * b + 1]["oA"], rb[2 * b]["oB"], rb[2 * b + 1]["oB"],
                      ra[2 * b]["oC"], ra[2 * b + 1]["oC"]], 0)
            m = dict(w_out=w_out[l], mod3=C_(modall[i][l][:, 4:10, :]), cones=cm[0],
                     oT0=C_(oT[:, r * TL:(r + 1) * TL]), xT0=xT[i])
            if need_ctx:
                oTc = cat([ra[2 * b]["oA_c"], ra[2 * b + 1]["oA_c"], rb[2 * b]["oB_c"], rb[2 * b + 1]["oB_c"],
                           ra[2 * b]["oC_c"], ra[2 * b + 1]["oC_c"]], 0)
                m.update(oT1=C_(oTc), xT1=xcT[b])
            ims.append(m)
        r3 = _run(nc3, ims)

        ncf = _prog("ffn", build_ffn, segT)
        cw = C_(conv_w[l].reshape(3, 44, 128).transpose(2, 0, 1))
        cb = C_(conv_b[l].reshape(44, 128).T)
        ims = []
        for i, (b, r) in enumerate(cores):
            h2 = r3[i]["h2T0"]
            z = np.zeros((D, 1), h2.dtype)
            left = z if r == 0 else r3[2 * b]["h2T0"][:, -1:]
            right = z if r == 1 else r3[2 * b + 1]["h2T0"][:, 0:1]
            g2 = modall[i][l][:, 10:12, :] if need_ctx else modall[i][l][:, 10:11, :]
            m = dict(w_up=w_up[l], w_down=w_down[l], cw=cw, cb=cb, g2=C_(g2),
                     h2T0=C_(cat([left, h2, right], 1)), xmT0=r3[i]["xmT0"])
            if need_ctx:
                h2c = r3[i]["h2T1"]
                m.update(h2T1=C_(cat([z, h2c, z], 1)), xmT1=r3[i]["xmT1"])
            ims.append(m)
        rf = _run(ncf, ims)
        xT = [rf[i]["outT0"] for i in range(NCORES)]
        if need_ctx:
            xcT = [rf[2 * b]["outT1"] for b in range(B)]

    out = np.empty((B, L, D), f32)
    for i, (b, r) in enumerate(cores):
        out[b, r * TL:(r + 1) * TL] = xT[i].T
    return out


def build_fused(L, LC, depth=2):
    nc = bass.Bass("TRN2", target_bir_lowering=False)
    p = Prog(nc)
    Tt = LC + L
    KC = 8

    def ext(name, shape, dt, kind="ExternalInput"):
        if not hasattr(nc, "k_io"):
            nc.k_io = []
        nc.k_io.append((name, tuple(shape), dt, kind))
        return nc.dram_tensor(name, list(shape), dt, kind=kind).ap()

    def scr(name, shape, dt):
        return nc.dram_tensor(name, list(shape), dt, kind="Internal").ap()

    E = dict(
        xT=ext("xT", [D, L], F32), xcT=ext("xcT", [D, LC], F32),
        cvec=ext("cvec", [128, KC, 2], F32), w_mod=ext("w_mod", [depth, D, 6 * D], F32),
        b_modT=ext("b_modT", [depth, 128, 48], F32), ngT=ext("ngT", [depth, 128, 2, KC], F32),
        hglT=ext("hglT", [128, depth, 4], F32), hglR=ext("hglR", [1, depth, 512], F32), dlam=ext("dlam", [1, depth, 4, 32], F32),
        w_in=ext("w_in", [depth, D, NIN], F32), w_out=ext("w_out", [depth, D, D], F32),
        w_up=ext("w_up", [depth, D, 2 * DFF], F32), w_down=ext("w_down", [depth, DFF, D], F32),
        cw=ext("cw", [depth, 128, 3, 44], F32), cb=ext("cb", [depth, 128, 44], F32),
        gains=ext("gains", [depth, 128, 4], F32), ropeT=ext("ropeT", [4, 128, L], F32),
        cmat=ext("cmat", [3, 128, 128], F32), rmat=ext("rmat", [2, 128, 128], BF16),
        scal=ext("scal", [depth, 2, 128, 8], F32), masks=ext("masks", [2, 128, 128], BF16),
        g64=ext("g64", [64, 64], F32), ogB=ext("ogB", [depth, 64, 1], F32),
        ucum=ext("ucum", [2, 128, 128], F32), lst=ext("lst", [2, 32, 32], F32), tri=ext("tri", [2, 32, 32], F32),
        outT=ext("outT", [D, L], F32, "ExternalOutput"),
    )
    S = dict(
        modall=scr("s_modall", [depth, 128, 12, KC], F32), oml=scr("s_oml", [128, depth, 4], F32),
        lbrow=scr("s_lbrow", [1, depth, 512], F32), lamb=scr("s_lamb", [128, depth], F32),
        qA=scr("s_qA", [512, Tt], BF16), kA=scr("s_kA", [128, Tt], BF16),
        qB=scr("s_qB", [2, 256, Tt], BF16), kB=scr("s_kB", [2, 256, Tt], BF16), gB=scr("s_gB", [256, Tt], BF16),
        qC=scr("s_qC", [256, Tt], BF16), kC=scr("s_kC", [256, Tt], BF16),
        vA=scr("s_vA", [Tt, 128], BF16), vB=scr("s_vB", [Tt, 256], BF16), vC=scr("s_vC", [Tt, 256], BF16),
        LF=scr("s_LF", [2, Tt, 256], F32), KB=scr("s_KB", [2, Tt, 256], BF16),
        oT=scr("s_oT", [D, Tt], BF16),
        xm=scr("s_xm", [D, L], F32), xmc=scr("s_xmc", [D, LC], F32),
        h2=scr("s_h2", [D, L + 2], BF16), h2c=scr("s_h2c", [D, LC + 2], BF16),
        x1=scr("s_x1", [D, L], F32), xc1=scr("s_xc1", [D, LC], F32),
    )

    with p.scope():
        z = p.sb("zero", [128, 2], BF16)
        p.memset("pool", z[:, :], 0.0, [z])
        for t_, n_ in ((S["h2"], L), (S["h2c"], LC)):
            for col in (0, n_ + 1):
                for kk in range(KC):
                    p.dma("sp", t_[kk * 128:(kk + 1) * 128, col:col + 1], z[:, 0:1], [z], [], slow=True)
        p.bind = dict(cvec=E["cvec"], w_mod=E["w_mod"], b_modT=E["b_modT"], ngT=E["ngT"], hglT=E["hglT"], hglR=E["hglR"],
                      dlam=E["dlam"], modall=S["modall"], oml=S["oml"], lbrow=S["lbrow"], lamb=S["lamb"])
        emit_p0(p)

    x_cur, xc_cur = E["xT"], E["xcT"]
    marks = [("P0", 0)]
    nc.k_marks = marks
    for l in range(depth):
        need_ctx = l < depth - 1
        marks.append(("P1_%d" % l, p.cnt["pe"]))
        with p.scope():
            b = dict(xT=x_cur, xcT=xc_cur, mod=S["modall"][l, :, 0:4, :], w_in=E["w_in"][l], gains=E["gains"][l],
                     lbT=S["oml"][:, l, :], lbrow=S["lbrow"][0, l].rearrange("(d k) -> d k", d=2),
                     ropeT=E["ropeT"], cmat=E["cmat"], rmat=E["rmat"])
            fm = dict(qA=S["qA"], kA=S["kA"], qB0=S["qB"][0], kB0=S["kB"][0], qB1=S["qB"][1], kB1=S["kB"][1],
                      gB=S["gB"], qC=S["qC"], kC=S["kC"])
            for nm, ap in fm.items():
                b[nm] = ap[:, LC:Tt]
                b[nm + "_c"] = ap[:, 0:LC]
            for nm in ("vA", "vB", "vC"):
                b[nm] = S[nm][LC:Tt, :]
                b[nm + "_c"] = S[nm][0:LC, :]
            for nm in ("LF", "KB"):
                b[nm] = S[nm][:, LC:Tt, :]
                b[nm + "_c"] = S[nm][:, 0:LC, :]
            p.bind = b
            emit_p1(p, L, LC)
        for g in range(2):
            marks.append(("attn_%d_%d" % (l, g), p.cnt["pe"]))
            with p.scope():
                p.bind = dict(qC=S["qC"][g * 128:(g + 1) * 128, LC:Tt], qC_c=S["qC"][g * 128:(g + 1) * 128, 0:LC],
                              qA=S["qA"][g * 256:(g + 1) * 256, LC:Tt], qA_c=S["qA"][g * 256:(g + 1) * 256, 0:LC],
                              scal=E["scal"][l, g], masks=E["masks"], g64=E["g64"],
                              oA=S["oT"][g * 256:(g + 1) * 256, LC:Tt], oA_c=S["oT"][g * 256:(g + 1) * 256, 0:LC],
                              oC=S["oT"][768 + g * 128:768 + (g + 1) * 128, LC:Tt],
                              oC_c=S["oT"][768 + g * 128:768 + (g + 1) * 128, 0:LC])
                emit_p2_attn(p, L, LC, need_ctx, LAM_INIT[l], True, True, g,
                             dict(kC=S["kC"], vC=S["vC"], kA=S["kA"], vA=S["vA"], lamcol=S["lamb"][:, l:l + 1]))
        for g in range(2):
            marks.append(("hgrn_%d_%d" % (l, g), p.cnt["pe"]))
            with p.scope():
                p.bind = dict(qB=S["qB"][:, g * 128:(g + 1) * 128, :].rearrange("d (h k) t -> d h k t", h=2),
                              kB=S["kB"][:, g * 128:(g + 1) * 128, :].rearrange("d (h k) t -> d h k t", h=2),
                              gBf=S["gB"][g * 128:(g + 1) * 128, :].rearrange("(h k) t -> h k t", h=2),
                              ogB=E["ogB"][l], ucum=E["ucum"], lst=E["lst"], tri=E["tri"], g64b=E["g64"],
                              oB=S["oT"][512 + g * 128:512 + (g + 1) * 128, LC:Tt],
                              oB_c=S["oT"][512 + g * 128:512 + (g + 1) * 128, 0:LC])
                emit_p2_hgrn(p, L, LC, need_ctx, g, dict(LF=S["LF"], KB=S["KB"], vB=S["vB"]))
        segT = [L] + ([LC] if need_ctx else [])
        marks.append(("p3a_%d" % l, p.cnt["pe"]))
        with p.scope():
            p.bind = dict(w_out=E["w_out"][l], mod3=S["modall"][l, :, 4:10, :], cones=E["cmat"][0],
                          oT0=S["oT"][:, LC:Tt], xT0=x_cur, xmT0=S["xm"], h2T0=S["h2"][:, 1:L + 1],
                          oT1=S["oT"][:, 0:LC], xT1=xc_cur, xmT1=S["xmc"], h2T1=S["h2c"][:, 1:LC + 1])
            emit_p3a(p, segT)
        last = l == depth - 1
        marks.append(("ffn_%d" % l, p.cnt["pe"]))
        with p.scope():
            p.bind = dict(h2T0=S["h2"], xmT0=S["xm"], outT0=(E["outT"] if last else S["x1"]),
                          h2T1=S["h2c"], xmT1=S["xmc"], outT1=S["xc1"],
                          g2=S["modall"][l, :, 10:10 + len(segT), :], w_up=E["w_up"][l], w_down=E["w_down"][l],
                          cw=E["cw"][l], cb=E["cb"][l])
            emit_ffn(p, segT)
        x_cur, xc_cur = S["x1"], S["xc1"]
    p.bind = {}
    p.finish()
    nc.k_ninst = sum(len(v) for v in p.ops.values())
    return nc


def fused_inputs(x, c, ctx, c_ctx, w_mod, b_mod, norm1_g, norm2_g, w_in, win_qnorm_g, win_knorm_g, win_sink,
                 hg_lower, hg_onorm_g, diff_qnorm_g, diff_knorm_g, diff_lambda, diff_onorm_g, w_out,
                 w_up, conv_w, conv_b, w_down):
    f32 = np.float32
    C_ = np.ascontiguousarray
    B, L, _ = x.shape
    depth = w_in.shape[0]
    cm, rm = const_mats()
    ucum, lst, tri = hgrn_consts()
    shared = dict(
        w_mod=w_mod, b_modT=C_(b_mod.reshape(depth, 48, 128).transpose(0, 2, 1)),
        ngT=C_(np.stack([np.stack([_pp(norm1_g[l]), _pp(norm2_g[l])], 1) for l in range(depth)], 0)),
        hglT=C_(hg_lower.reshape(depth, 4, 128).transpose(2, 0, 1)), hglR=C_(hg_lower.reshape(1, depth, 512)),
        dlam=C_(diff_lambda.reshape(1, depth, 4, 32)),
        w_in=w_in, w_out=w_out, w_up=w_up, w_down=w_down,
        cw=C_(conv_w.reshape(depth, 3, 44, 128).transpose(0, 3, 1, 2)), cb=C_(conv_b.reshape(depth, 44, 128).transpose(0, 2, 1)),
        gains=C_(np.stack([np.stack([np.tile(win_qnorm_g[l], 2), np.tile(win_knorm_g[l], 2),
                                     np.tile(diff_qnorm_g[l], 4), np.tile(diff_knorm_g[l], 4)], 1) for l in range(depth)], 0)),
        ropeT=rope_tables(np.arange(L)), cmat=cm, rmat=rm, masks=attn_masks(),
        g64=np.full((64, 64), 1.0 / 64, f32), ogB=C_(hg_onorm_g.reshape(depth, 64, 1)), ucum=ucum, lst=lst, tri=tri,
    )
    scal = np.zeros((depth, 2, 128, 8), f32)
    for l in range(depth):
        for g in range(2):
            scal[l, g, :, 1] = np.tile(diff_onorm_g[l], 2)
            scal[l, g, :, 2:6] = win_sink[l][4 * g:4 * g + 4][None, :]
    shared["scal"] = scal
    ims = []
    for core in range(NCORES):
        b = core // 2
        m = dict(shared)
        m.update(xT=C_(x[b].T), xcT=C_(ctx[b].T), cvec=C_(np.stack([_pp(c[b]), _pp(c_ctx)], -1)))
        ims.append(m)
    return ims


def kernel(**inputs):
    f32 = np.float32
    inp = {k: np.ascontiguousarray(np.asarray(v, dtype=f32)) for k, v in inputs.items()}
    B, L, _ = inp["x"].shape
    LC = inp["ctx"].shape[1]
    nc = _prog("fused", build_fused, L, LC, inp["w_in"].shape[0])
    res = _run(nc, fused_inputs(**inp))
    out = np.empty((B, L, D), f32)
    for b in range(B):
        out[b] = res[2 * b]["outT"].T
    return out
```

```python
import contextlib
import numpy as np
import concourse.bass as bass
import concourse.mybir as mybir
from concourse.bass_utils import run_bass_kernel_spmd

F32 = mybir.dt.float32
BF16 = mybir.dt.bfloat16
AF = mybir.ActivationFunctionType
ALU = mybir.AluOpType
AX = mybir.AxisListType

import os
DBG = os.environ.get('KDBG', '')
ENGS = ("pe", "act", "dve", "pool", "sp")
SAME_ENGINE_SYNC = "nosesync" not in DBG


class Buf:
    __slots__ = ("name", "w", "r")

    def __init__(self, name=""):
        self.name = name
        self.w = None
        self.r = {}


class T:
    def __init__(self, t, name=""):
        self.t = t
        self.b = Buf(name)

    def __getitem__(self, idx):
        return self.t[idx]


class Prog:
    def __init__(self, nc, nring=8):
        self.nc = nc
        self.stack = contextlib.ExitStack()
        self.ops = {e: [] for e in ENGS}
        self.cnt = {e: 0 for e in ENGS}
        self.seen = {e: {} for e in ENGS}
        self.dma_tot = {}
        self.ring = {e: 0 for e in ENGS}
        self.nring = nring
        self.sems = {}
        self.nalloc = 0
        self.bind = {}
        self.scopes = []
        for e in ENGS:
            self.sems[("e", e)] = self.stack.enter_context(nc.semaphore("s_" + e))
        for q in ("sp", "act", "pool"):
            for i in range(nring):
                self.sems[("d", q, i)] = self.stack.enter_context(nc.semaphore("d_%s%d" % (q, i)))

    def _stk(self):
        return self.scopes[-1] if self.scopes else self.stack

    def sb(self, name, shape, dt):
        self.nalloc += 1
        return T(self._stk().enter_context(self.nc.sbuf_tensor("%s_%d" % (name, self.nalloc), list(shape), dt)), name)

    def ps(self, name, shape=(128, 512), dt=F32):
        self.nalloc += 1
        return T(self._stk().enter_context(self.nc.psum_tensor("%s_%d" % (name, self.nalloc), list(shape), dt)), name)

    def scratch(self, name, shape, dt):
        return self.nc.dram_tensor(name, list(shape), dt, kind="Internal")

    def dma_fence(self, engines=("pe", "act", "dve", "pool")):
        for e in engines:
            for key, tot in self.dma_tot.items():
                self._wait(e, key, tot)

    def barrier(self):
        for e in ENGS:
            for f in ENGS:
                if f != e and self.cnt[f]:
                    self._wait(e, ("e", f), self.cnt[f])
            for key, tot in self.dma_tot.items():
                self._wait(e, key, tot)

    @contextlib.contextmanager
    def scope(self):
        st = contextlib.ExitStack()
        self.scopes.append(st)
        try:
            yield
        finally:
            self.barrier()
            self.scopes.pop()
            st.close()
            if hasattr(self, "_eps"):
                del self._eps

    def dram(self, name, shape, dt, kind):
        if name in self.bind:
            v = self.bind[name]
            assert tuple(v.shape) == tuple(shape), (name, tuple(v.shape), tuple(shape))
            return T(v, name)
        if not hasattr(self.nc, "k_io"):
            self.nc.k_io = []
        self.nc.k_io.append((name, tuple(shape), dt, kind))
        return T(self.nc.dram_tensor(name, list(shape), dt, kind=kind), name)

    def _wait(self, eng, key, val):
        if key == ("e", eng) and (eng == "pe" or not SAME_ENGINE_SYNC):
            return
        if self.seen[eng].get(key, 0) >= val:
            return
        self.seen[eng][key] = val
        sem = self.sems[key]
        self.ops[eng].append(lambda E, sem=sem, val=val: E.wait_ge(sem, val))

    def _deps(self, eng, reads, writes):
        for t in reads:
            b = t.b
            if b.w is not None:
                self._wait(eng, *b.w)
        for t in writes:
            b = t.b
            if b.w is not None:
                self._wait(eng, *b.w)
            for k, v in b.r.items():
                self._wait(eng, k, v)

    def _mark(self, key, val, reads, writes):
        for t in reads:
            t.b.r[key] = val
        for t in writes:
            t.b.w = (key, val)
            t.b.r = {}

    def op(self, eng, fn, reads=(), writes=()):
        self._deps(eng, reads, writes)
        self.cnt[eng] += 1
        key = ("e", eng)
        sem = self.sems[key]
        self.ops[eng].append(lambda E, fn=fn, sem=sem: fn(E).then_inc(sem, 1))
        self._mark(key, self.cnt[eng], reads, writes)

    def dma(self, q, out_ap, in_ap, reads=(), writes=(), slow=False):
        self._deps(q, reads, writes)
        idx = self.ring[q]
        self.ring[q] = (idx + 1) % self.nring
        key = ("d", q, idx)
        prev = self.dma_tot.get(key, 0)
        if prev:
            self._wait(q, key, prev)
        tot = prev + 16
        self.dma_tot[key] = tot
        sem = self.sems[key]
        kw = dict(allow_slow_non_contiguous=True) if slow else {}
        self.ops[q].append(lambda E, o=out_ap, i=in_ap, sem=sem, kw=kw: E.dma_start(out=o, in_=i, **kw).then_inc(sem, 16))
        self._mark(key, tot, reads, writes)

    def finish(self):
        for key, tot in self.dma_tot.items():
            self._wait(key[1], key, tot)
        nc = self.nc
        ops = self.ops
        with nc.Block() as block:
            @block.tensor
            def _(E):
                for f in ops["pe"]:
                    f(E)

            @block.scalar
            def _(E):
                for f in ops["act"]:
                    f(E)

            @block.vector
            def _(E):
                for f in ops["dve"]:
                    f(E)

            @block.gpsimd
            def _(E):
                for f in ops["pool"]:
                    f(E)

            @block.sync
            def _(E):
                for f in ops["sp"]:
                    f(E)
        self.stack.close()

    def make_eps(self, eps):
        if hasattr(self, "_eps"):
            return
        self._eps = self.sb("epsc", [128, 1], F32)
        self.memset("pool", self._eps[:, :], eps, [self._eps])

    def mm(self, out, lhsT, rhs, start, stop, reads, writes):
        self.op("pe", lambda E: E.matmul(out, lhsT, rhs, start=start, stop=stop), reads, writes)

    def act(self, out, in_, func, reads, writes, bias=None, scale=None, eng="act"):
        kw = {}
        if bias is not None:
            kw["bias"] = bias
        if scale is not None:
            kw["scale"] = scale
        self.op(eng, lambda E: E.activation(out, in_, func, **kw), reads, writes)

    def tt(self, eng, out, in0, in1, op, reads, writes):
        self.op(eng, lambda E: E.tensor_tensor(out, in0, in1, op), reads, writes)

    def ts(self, eng, out, in0, s1, s2, op0, op1, reads, writes):
        if s2 is None:
            self.op(eng, lambda E: E.tensor_scalar(out, in0, s1, 0.0, op0, ALU.add), reads, writes)
        else:
            self.op(eng, lambda E: E.tensor_scalar(out, in0, s1, s2, op0, op1), reads, writes)

    def stt(self, eng, out, in0, scalar, in1, op0, op1, reads, writes):
        eng = "dve"
        self.op(eng, lambda E: E.scalar_tensor_tensor(out, in0, scalar, in1, op0, op1), reads, writes)

    def rsqrt(self, out, in_, eps, tmp, reads, writes):
        n = out.shape[-1]
        np_ = out.shape[0]
        eps_t = self._eps
        self.op("act", lambda E: E.activation(tmp[0:np_, 0:n], in_, AF.Ln, bias=eps_t[0:np_, 0:1]), list(reads) + [eps_t], [tmp])
        self.op("act", lambda E: E.activation(out, tmp[0:np_, 0:n], AF.Exp, scale=-0.5), [tmp], writes)

    def eps_ap(self, eps):
        return self._eps[:, 0:1]

    def recip(self, out, in_, reads, writes):
        self.op("act", lambda E: E.activation(out, in_, AF.Ln), reads, writes)
        self.op("act", lambda E: E.activation(out, out, AF.Exp, scale=-1.0), writes, writes)

    def copy(self, eng, out, in_, reads, writes):
        if eng == "act":
            self.op(eng, lambda E: E.copy(out, in_), reads, writes)
        else:
            self.op(eng, lambda E: E.tensor_copy(out, in_), reads, writes)

    def memset(self, eng, out, val, writes):
        self.op(eng, lambda E: E.memset(out, val), (), writes)


D = 1024
DFF = 2816
NCORES = 8


def emit_ffn(p, T_list, TW=510):
    KC = D // 128
    FC = DFF // 128
    nseg = len(T_list)
    segs = []
    for i, Tn in enumerate(T_list):
        segs.append(dict(T=Tn, h2T=p.dram("h2T%d" % i, [D, Tn + 2], BF16, "ExternalInput"),
                         xmT=p.dram("xmT%d" % i, [D, Tn], F32, "ExternalInput"),
                         outT=p.dram("outT%d" % i, [D, Tn], F32, "ExternalOutput")))
    g2 = p.dram("g2", [128, nseg, KC], F32, "ExternalInput")
    w_up = p.dram("w_up", [D, 2 * DFF], F32, "ExternalInput")
    w_down = p.dram("w_down", [DFF, D], F32, "ExternalInput")
    cw = p.dram("cw", [128, 3, 2 * FC], F32, "ExternalInput")
    cb = p.dram("cb", [128, 2 * FC], F32, "ExternalInput")

    wup_sb = [p.sb("wup%d" % k, [128, 2 * DFF], BF16) for k in range(KC)]
    wdn_sb = [p.sb("wdn%d" % i, [128, D], BF16) for i in range(FC)]
    g2_sb = p.sb("g2", [128, nseg, KC], F32)
    cw_sb = p.sb("cw", [128, 3, 2 * FC], F32)
    cb_sb = p.sb("cb", [128, 2 * FC], F32)
    p.dma("sp", g2_sb[:], g2[:], [g2], [g2_sb])
    p.dma("sp", cw_sb[:], cw[:], [cw], [cw_sb])
    p.dma("sp", cb_sb[:], cb[:], [cb], [cb_sb])
    with p.scope():
        stg = [p.sb("stg%d" % i, [128, 1408], F32) for i in range(3)]
        si = 0
        ceng = ["dve", "pool", "act"]
        for k in range(KC):
            for j in range(4):
                s = stg[si % 3]
                p.dma("sp", s[:, 0:1408], w_up[k * 128:(k + 1) * 128, j * 1408:(j + 1) * 1408], [w_up], [s])
                p.copy(ceng[si % 3], wup_sb[k][:, j * 1408:(j + 1) * 1408], s[:, 0:1408], [s], [wup_sb[k]])
                si += 1
        for i in range(FC):
            s = stg[si % 3]
            p.dma("sp", s[:, 0:D], w_down[i * 128:(i + 1) * 128, :], [w_down], [s])
            p.copy(ceng[si % 3], wdn_sb[i][:, :], s[:, 0:D], [s], [wdn_sb[i]])
            si += 1

    NB = 2
    WB = TW + 2
    h_sb = [p.sb("h%d" % i, [128, KC, WB], BF16) for i in range(NB)]
    xm_sb = [p.sb("xm%d" % i, [128, TW], F32) for i in range(3)]
    gT = [p.sb("gT%d" % i, [128, TW], BF16) for i in range(FC)]
    pa = [p.ps("pa%d" % i) for i in range(2)]
    pv = [p.ps("pv%d" % i) for i in range(2)]
    pd = [p.ps("pd%d" % i) for i in range(2)]
    ya = [p.sb("ya%d" % i, [128, TW], F32) for i in range(2)]
    yv = [p.sb("yv%d" % i, [128, TW], F32) for i in range(2)]
    sa = [p.sb("sa%d" % i, [128, TW], F32) for i in range(2)]
    ot = [p.sb("ot%d" % i, [128, TW], F32) for i in range(2)]
    tiles = [(sgi, t0, min(TW, sg["T"] - t0)) for sgi, sg in enumerate(segs) for t0 in range(0, sg["T"], TW)]

    def load(ti):
        sgi, t0, w = tiles[ti]
        sg = segs[sgi]
        hb = h_sb[ti % NB]
        p.dma("sp", hb[:, :, 0:w + 2], sg["h2T"][:, t0:t0 + w + 2].rearrange("(k p) n -> p k n", p=128), [sg["h2T"]], [hb])

    load(0)
    cc = 0
    xi = 0
    for ti, (sgi, t0, w) in enumerate(tiles):
        sg = segs[sgi]
        if ti + 1 < len(tiles):
            load(ti + 1)
        hb = h_sb[ti % NB]
        for i in range(FC):
            A = pa[cc % 2]
            V = pv[cc % 2]
            for k in range(KC):
                p.mm(A[:, 0:w + 2], wup_sb[k][:, i * 128:(i + 1) * 128], hb[:, k, 0:w + 2], k == 0, k == KC - 1,
                     [wup_sb[k], hb], [A])
            for k in range(KC):
                p.mm(V[:, 0:w + 2], wup_sb[k][:, DFF + i * 128:DFF + (i + 1) * 128], hb[:, k, 0:w + 2], k == 0,
                     k == KC - 1, [wup_sb[k], hb], [V])
            a_t = ya[cc % 2]
            v_t = yv[cc % 2]
            s_t = sa[cc % 2]
            p.act(a_t[:, 0:w], A[:, 0:w], AF.Identity, [A, cw_sb, cb_sb], [a_t], bias=cb_sb[:, i:i + 1],
                  scale=cw_sb[:, 0, i:i + 1])
            p.stt("dve", a_t[:, 0:w], A[:, 1:w + 1], cw_sb[:, 1, i:i + 1], a_t[:, 0:w], ALU.mult, ALU.add,
                  [A, a_t, cw_sb], [a_t])
            p.stt("dve", a_t[:, 0:w], A[:, 2:w + 2], cw_sb[:, 2, i:i + 1], a_t[:, 0:w], ALU.mult, ALU.add,
                  [A, a_t, cw_sb], [a_t])
            p.act(v_t[:, 0:w], V[:, 0:w], AF.Identity, [V, cw_sb, cb_sb], [v_t], bias=cb_sb[:, FC + i:FC + i + 1],
                  scale=cw_sb[:, 0, FC + i:FC + i + 1])
            p.stt("dve", v_t[:, 0:w], V[:, 1:w + 1], cw_sb[:, 1, FC + i:FC + i + 1], v_t[:, 0:w], ALU.mult, ALU.add,
                  [V, v_t, cw_sb], [v_t])
            p.stt("dve", v_t[:, 0:w], V[:, 2:w + 2], cw_sb[:, 2, FC + i:FC + i + 1], v_t[:, 0:w], ALU.mult, ALU.add,
                  [V, v_t, cw_sb], [v_t])
            p.act(s_t[:, 0:w], a_t[:, 0:w], AF.Silu, [a_t], [s_t])
            p.tt("pool", gT[i][:, 0:w], s_t[:, 0:w], v_t[:, 0:w], ALU.mult, [s_t, v_t], [gT[i]])
            cc += 1
        for m in range(KC):
            xb = xm_sb[xi % 3]
            xi += 1
            p.dma("sp", xb[:, 0:w], sg["xmT"][m * 128:(m + 1) * 128, t0:t0 + w], [sg["xmT"]], [xb])
            Dp = pd[m % 2]
            for i in range(FC):
                p.mm(Dp[:, 0:w], wdn_sb[i][:, m * 128:(m + 1) * 128], gT[i][:, 0:w], i == 0, i == FC - 1,
                     [wdn_sb[i], gT[i]], [Dp])
            o = ot[m % 2]
            p.stt("dve", o[:, 0:w], Dp[:, 0:w], g2_sb[:, sgi, m:m + 1], xb[:, 0:w], ALU.mult, ALU.add,
                  [Dp, g2_sb, xb], [o])
            p.dma("pool", sg["outT"][m * 128:(m + 1) * 128, t0:t0 + w], o[:, 0:w], [o], [])


def build_ffn(T_list, TW=510):
    nc = bass.Bass("TRN2", target_bir_lowering=False)
    p = Prog(nc)
    emit_ffn(p, T_list, TW)
    p.finish()
    return nc


EPS = 1e-6
NIN = 3072
FM_CHUNKS = (
    [("qA", 128 * i, "qkA", i) for i in range(4)]
    + [("kA", 512, "qkA", 0)]
    + [("qB0", 768 + 128 * i, "plain", i) for i in range(2)]
    + [("kB0", 1024 + 128 * i, "fgate", i) for i in range(2)]
    + [("qB1", 1280 + 128 * i, "plain", i) for i in range(2)]
    + [("kB1", 1536 + 128 * i, "fgate", i) for i in range(2)]
    + [("gB", 2048 + 128 * i, "silu", i) for i in range(2)]
    + [("qC", 2304 + 128 * i, "qkC", i) for i in range(2)]
    + [("kC", 2560 + 128 * i, "qkC", i) for i in range(2)]
)
TM_GROUPS = [("vA", 640, 128, "plain"), ("vB", 1792, 256, "plain"), ("vC", 2816, 256, "plain"),
             ("f0", 1024, 256, "f"), ("f1", 1536, 256, "f")]
FM_ROWS = {"qA": 512, "kA": 128, "qB0": 256, "kB0": 256, "qB1": 256, "kB1": 256, "gB": 256, "qC": 256, "kC": 256}


def bcast_rows(t, nrow, ncol, off=0):
    h = t.t
    if isinstance(h, bass.AP):
        return bass.AP(h.tensor, h.offset + off, [[0, nrow], [1, ncol]])
    return bass.AP(h, off, [[0, nrow], [1, ncol]])


def p1_declare(p, sfx, Tn):
    o = {}
    for nm, rows in FM_ROWS.items():
        o[nm] = p.dram(nm + sfx, [rows, Tn], BF16, "ExternalOutput")
    o["vA"] = p.dram("vA" + sfx, [Tn, 128], BF16, "ExternalOutput")
    o["vB"] = p.dram("vB" + sfx, [Tn, 256], BF16, "ExternalOutput")
    o["vC"] = p.dram("vC" + sfx, [Tn, 256], BF16, "ExternalOutput")
    o["LF"] = p.dram("LF" + sfx, [2, Tn, 256], F32, "ExternalOutput")
    o["KB"] = p.dram("KB" + sfx, [2, Tn, 256], BF16, "ExternalOutput")
    return o


def emit_p1(p, TL, TC, extra=None):
    KC = 8
    xT = p.dram("xT", [D, TL], F32, "ExternalInput")
    xcT = p.dram("xcT", [D, TC], F32, "ExternalInput")
    mod = p.dram("mod", [128, 4, KC], F32, "ExternalInput")
    w_in = p.dram("w_in", [D, NIN], F32, "ExternalInput")
    gains = p.dram("gains", [128, 4], F32, "ExternalInput")
    lbT = p.dram("lbT", [128, 4], F32, "ExternalInput")
    lbrow = p.dram("lbrow", [2, 256], F32, "ExternalInput")
    ropeT = p.dram("ropeT", [4, 128, TL], F32, "ExternalInput")
    cmat = p.dram("cmat", [3, 128, 128], F32, "ExternalInput")
    rmat = p.dram("rmat", [2, 128, 128], BF16, "ExternalInput")
    outs_l = p1_declare(p, "", TL)
    outs_c = p1_declare(p, "_c", TC)

    w_sb = [p.sb("w%d" % k, [128, NIN], BF16) for k in range(KC)]
    mod_sb = p.sb("mod", [128, 4, KC], F32)
    gains_sb = p.sb("gains", [128, 4], F32)
    oml_sb = p.sb("oml", [128, 4], F32)
    lb_bc = p.sb("lb_bc", [128, 2, 256], F32)
    oml_bc = p.sb("oml_bc", [128, 2, 256], F32)
    cmat_sb = p.sb("cmat", [128, 3, 128], F32)
    rmat_sb = p.sb("rmat", [128, 2, 128], BF16)
    p.dma("sp", mod_sb[:], mod[:], [mod], [mod_sb])
    p.dma("sp", gains_sb[:], gains[:], [gains], [gains_sb])
    p.dma("sp", oml_sb[:], lbT[:], [lbT], [oml_sb])
    p.dma("sp", cmat_sb[:], cmat[:].rearrange("c p n -> p c n"), [cmat], [cmat_sb])
    p.dma("sp", rmat_sb[:], rmat[:].rearrange("c p n -> p c n"), [rmat], [rmat_sb])
    for d in range(2):
        p.dma("sp", lb_bc[:, d, :], bcast_rows(lbrow, 128, 256, d * 256), [lbrow], [lb_bc])
    p.ts("dve", oml_bc[:], lb_bc[:], -1.0, 1.0, ALU.mult, ALU.add, [lb_bc], [oml_bc])

    stg = [p.sb("stg%d" % i, [128, 1536], F32) for i in range(3)]
    ceng = ["dve", "pool", "act"]
    si = 0
    for k in range(KC):
        for j in range(2):
            s = stg[si % 3]
            p.dma("sp", s[:, :], w_in[k * 128:(k + 1) * 128, j * 1536:(j + 1) * 1536], [w_in], [s])
            p.copy(ceng[si % 3], w_sb[k][:, j * 1536:(j + 1) * 1536], s[:, :], [s], [w_sb[k]])
            si += 1

    TW = 512
    NB = 2
    x_sb = [p.sb("x%d" % i, [128, KC, TW], F32) for i in range(NB)]
    rope_sb = [p.sb("rope%d" % i, [128, 4, TW], F32) for i in range(NB)]
    sq_sb = [p.sb("sq%d" % i, [128, TW], F32) for i in range(2)]
    rstd_sb = p.sb("rstd", [128, TW], F32)
    hx_sb = [p.sb("hx%d" % i, [128, TW], F32) for i in range(2)]
    hT = [p.sb("hT%d" % k, [128, TW], BF16) for k in range(KC)]
    pss = p.ps("pss")
    pfm = [p.ps("pfm%d" % i) for i in range(2)]
    pms = p.ps("pms")
    prot = p.ps("prot")
    ptm = [p.ps("ptm%d" % i) for i in range(2)]
    NS = 3
    sqh = [p.sb("sqh%d" % i, [128, TW], F32) for i in range(NS)]
    rs2 = [p.sb("rs2%d" % i, [128, TW], F32) for i in range(NS)]
    qg = [p.sb("qg%d" % i, [128, TW], BF16) for i in range(NS)]
    t1 = [p.sb("t1%d" % i, [128, TW], F32) for i in range(NS)]
    t2 = [p.sb("t2%d" % i, [128, TW], F32) for i in range(NS)]
    ofm = [p.sb("ofm%d" % i, [128, TW], BF16) for i in range(NS)]
    sg = [p.sb("sg%d" % i, [128, TW], F32) for i in range(NS)]
    otm = [p.sb("otm%d" % i, [128, 256], BF16) for i in range(NS)]
    ftm = [p.sb("ftm%d" % i, [128, 256], F32) for i in range(NS)]
    lftm = [p.sb("lftm%d" % i, [128, 256], F32) for i in range(NS)]
    cnt = {"fm": 0, "tm": 0, "s": 0}
    rtmp = p.sb("rtmp", [128, TW], F32)
    rtmp2 = [p.sb("rtmp2%d" % i, [128, TW], F32) for i in range(NS)]
    p.make_eps(EPS)

    def run(src, Tn, outs, mi, rope):
        tw = min(TW, Tn)
        nt = Tn // tw

        def load(t):
            xb = x_sb[t % NB]
            p.dma("sp", xb[:, :, 0:tw], src[:, t * tw:(t + 1) * tw].rearrange("(k p) n -> p k n", p=128), [src], [xb])
            if rope:
                rb = rope_sb[t % NB]
                p.dma("sp", rb[:, :, 0:tw], ropeT[:, :, t * tw:(t + 1) * tw].rearrange("c p n -> p c n"), [ropeT], [rb])

        load(0)
        for t in range(nt):
            if t + 1 < nt:
                load(t + 1)
            xb = x_sb[t % NB]
            rb = rope_sb[t % NB]
            c0 = t * tw
            for k in range(KC):
                s = sq_sb[k % 2]
                p.act(s[:, 0:tw], xb[:, k, 0:tw], AF.Square, [xb], [s])
                p.mm(pss[:, 0:tw], cmat_sb[:, 0, :], s[:, 0:tw], k == 0, k == KC - 1, [cmat_sb, s], [pss])
            p.rsqrt(rstd_sb[:, 0:tw], pss[:, 0:tw], EPS, rtmp, [pss], [rstd_sb])
            for k in range(KC):
                hx = hx_sb[k % 2]
                p.stt("dve", hx[:, 0:tw], xb[:, k, 0:tw], mod_sb[:, mi, k:k + 1], rstd_sb[:, 0:tw], ALU.mult, ALU.mult,
                      [xb, mod_sb, rstd_sb], [hx])
                p.act(hT[k][:, 0:tw], hx[:, 0:tw], AF.Identity, [hx, mod_sb], [hT[k]], bias=mod_sb[:, mi + 1, k:k + 1])
            for (nm, col0, kind, ci) in FM_CHUNKS:
                P = pfm[cnt["fm"] % 2]
                cnt["fm"] += 1
                for k in range(KC):
                    p.mm(P[:, 0:tw], w_sb[k][:, col0:col0 + 128], hT[k][:, 0:tw], k == 0, k == KC - 1, [w_sb[k], hT[k]], [P])
                si_ = cnt["s"] % NS
                cnt["s"] += 1
                o = ofm[si_]
                if kind == "plain":
                    p.copy("act", o[:, 0:tw], P[:, 0:tw], [P], [o])
                elif kind == "fgate":
                    dirn = 0 if nm == "kB0" else 1
                    p.act(sg[si_][:, 0:tw], P[:, 0:tw], AF.Sigmoid, [P], [sg[si_]], scale=-1.0)
                    p.ts("pool", o[:, 0:tw], sg[si_][:, 0:tw], oml_sb[:, dirn * 2 + ci:dirn * 2 + ci + 1], None, ALU.mult, None,
                         [sg[si_], oml_sb], [o])
                elif kind == "silu":
                    p.act(sg[si_][:, 0:tw], P[:, 0:tw], AF.Sigmoid, [P], [sg[si_]])
                    p.tt("dve", o[:, 0:tw], P[:, 0:tw], sg[si_][:, 0:tw], ALU.mult, [P, sg[si_]], [o])
                else:
                    isA = kind == "qkA"
                    gi = (0 if nm == "qA" else 1) if isA else (2 if nm == "qC" else 3)
                    gm = 1 if isA else 2
                    p.act(sqh[si_][:, 0:tw], P[:, 0:tw], AF.Square, [P], [sqh[si_]])
                    p.mm(pms[:, 0:tw], cmat_sb[:, gm, :], sqh[si_][:, 0:tw], True, True, [cmat_sb, sqh[si_]], [pms])
                    p.rsqrt(rs2[si_][:, 0:tw], pms[:, 0:tw], EPS, rtmp2[si_], [pms], [rs2[si_]])
                    if not rope:
                        p.stt("dve", o[:, 0:tw], P[:, 0:tw], gains_sb[:, gi:gi + 1], rs2[si_][:, 0:tw], ALU.mult, ALU.mult,
                              [P, gains_sb, rs2[si_]], [o])
                    else:
                        q_ = qg[si_]
                        p.stt("dve", q_[:, 0:tw], P[:, 0:tw], gains_sb[:, gi:gi + 1], rs2[si_][:, 0:tw], ALU.mult, ALU.mult,
                              [P, gains_sb, rs2[si_]], [q_])
                        ri = 0 if isA else 1
                        p.mm(prot[:, 0:tw], rmat_sb[:, ri, :], q_[:, 0:tw], True, True, [rmat_sb, q_], [prot])
                        p.tt("pool", t1[si_][:, 0:tw], q_[:, 0:tw], rb[:, 2 * ri, 0:tw], ALU.mult, [q_, rb], [t1[si_]])
                        p.tt("dve", t2[si_][:, 0:tw], prot[:, 0:tw], rb[:, 2 * ri + 1, 0:tw], ALU.mult, [prot, rb], [t2[si_]])
                        p.tt("pool", o[:, 0:tw], t1[si_][:, 0:tw], t2[si_][:, 0:tw], ALU.add, [t1[si_], t2[si_]], [o])
                p.dma("pool", outs[nm][ci * 128:(ci + 1) * 128, c0:c0 + tw], o[:, 0:tw], [o], [])
            for s4 in range(tw // 128):
                r0 = c0 + s4 * 128
                for (nm, col0, ncols, kind) in TM_GROUPS:
                    P = ptm[cnt["tm"] % 2]
                    cnt["tm"] += 1
                    for k in range(KC):
                        p.mm(P[:, 0:ncols], hT[k][:, s4 * 128:(s4 + 1) * 128], w_sb[k][:, col0:col0 + ncols], k == 0, k == KC - 1,
                             [w_sb[k], hT[k]], [P])
                    si_ = cnt["s"] % NS
                    cnt["s"] += 1
                    if kind == "plain":
                        o = otm[si_]
                        p.copy("act", o[:, 0:ncols], P[:, 0:ncols], [P], [o])
                        p.dma("pool", outs[nm][r0:r0 + 128, :], o[:, 0:ncols], [o], [])
                    else:
                        dirn = 0 if nm == "f0" else 1
                        f_ = ftm[si_]
                        p.act(f_[:, :], P[:, 0:256], AF.Sigmoid, [P], [f_])
                        p.tt("dve", f_[:, :], f_[:, :], oml_bc[:, dirn, :], ALU.mult, [f_, oml_bc], [f_])
                        p.tt("pool", f_[:, :], f_[:, :], lb_bc[:, dirn, :], ALU.add, [f_, lb_bc], [f_])
                        lf = lftm[si_]
                        p.act(lf[:, :], f_[:, :], AF.Ln, [f_], [lf])
                        o = otm[si_]
                        p.ts("pool", o[:, :], f_[:, :], -1.0, 1.0, ALU.mult, ALU.add, [f_], [o])
                        p.dma("pool", outs["LF"][dirn, r0:r0 + 128, :], lf[:, :], [lf], [])
                        p.dma("pool", outs["KB"][dirn, r0:r0 + 128, :], o[:, :], [o], [])

    run(xT, TL, outs_l, 0, True)
    run(xcT, TC, outs_c, 2, False)


def build_p1(TL, TC):
    nc = bass.Bass("TRN2", target_bir_lowering=False)
    p = Prog(nc)
    emit_p1(p, TL, TC)
    p.finish()
    return nc


ROPE_BASE = 10000.0
GRID_W = 64


def rope_tables(pos):
    pos = np.asarray(pos)
    row = (pos // GRID_W).astype(np.float32)
    col = (pos % GRID_W).astype(np.float32)
    out = []
    for dim in (64, 32):
        axis_dim = dim // 2
        n = axis_dim // 2
        inv = np.power(np.float32(ROPE_BASE), (-np.arange(n, dtype=np.float32) * np.float32(2.0) / np.float32(axis_dim))).astype(np.float32)
        d = np.arange(128) % dim
        half = d // axis_dim
        i = d % n
        ang = np.where(half[:, None] == 0, row[None, :], col[None, :]).astype(np.float32) * inv[i][:, None]
        out.append(np.cos(ang).astype(np.float32))
        out.append(np.sin(ang).astype(np.float32))
    return np.stack(out, 0)


def const_mats():
    cm = np.zeros((3, 128, 128), np.float32)
    cm[0] = 1.0 / 1024
    for g in range(2):
        cm[1, g * 64:(g + 1) * 64, g * 64:(g + 1) * 64] = 1.0 / 64
    for g in range(4):
        cm[2, g * 32:(g + 1) * 32, g * 32:(g + 1) * 32] = 1.0 / 32
    rm = np.zeros((2, 128, 128), np.float32)
    for ri, axis_dim in enumerate((32, 16)):
        hf = axis_dim // 2
        for m in range(128):
            if (m % axis_dim) < hf:
                rm[ri, m + hf, m] = -1.0
            else:
                rm[ri, m - hf, m] = 1.0
    import ml_dtypes
    return cm, rm.astype(ml_dtypes.bfloat16)


def p2_attn(p, L, LC, need_ctx, lam_init, g=None, nat=None):
    NCH = (L + LC) // 128
    NLB = L // 128
    qC = p.dram("qC", [128, L], BF16, "ExternalInput")
    qC_c = p.dram("qC_c", [128, LC], BF16, "ExternalInput")
    if nat is None:
        kC = p.dram("kC", [128, L + LC], BF16, "ExternalInput")
        vC = p.dram("vCp", [128, NCH, 2, 128], BF16, "ExternalInput")
        kA = p.dram("kA", [64, L + LC], BF16, "ExternalInput")
        vA = p.dram("vAp", [128, NCH, 128], BF16, "ExternalInput")
    qA = p.dram("qA", [256, L], BF16, "ExternalInput")
    qA_c = p.dram("qA_c", [256, LC], BF16, "ExternalInput")
    scal = p.dram("scal", [128, 8], F32, "ExternalInput")
    masks = p.dram("masks", [2, 128, 128], BF16, "ExternalInput")
    g64 = p.dram("g64", [64, 64], F32, "ExternalInput")
    oC = p.dram("oC", [128, L], BF16, "ExternalOutput")
    oA = p.dram("oA", [256, L], BF16, "ExternalOutput")
    oC_c = p.dram("oC_c", [128, LC], BF16, "ExternalOutput")
    oA_c = p.dram("oA_c", [256, LC], BF16, "ExternalOutput")
    if 'dbg' in DBG:
        dbg = p.dram("dbg", [2, 4, 64, 512], F32, "ExternalOutput")
        dbg2 = p.dram("dbg2", [2, 2, 128, 512], F32, "ExternalOutput")
        dbgz = [p.sb("dbgz%d" % i, [128, 512], F32) for i in range(2)]

    scal_sb = p.sb("scal", [128, 8], F32)
    sc2 = p.sb("sc2", [128, 8], F32)
    mask_sb = p.sb("mask", [128, 2, 128], BF16)
    g64_sb = p.sb("g64", [64, 64], F32)
    p.dma("sp", scal_sb[:], scal[:], [scal], [scal_sb])
    if nat is not None:
        p.dma("sp", scal_sb[:, 0:1], nat["lamcol"], [scal_sb], [scal_sb], slow=True)
    p.dma("sp", mask_sb[:], masks[:].rearrange("c p n -> p c n"), [masks], [mask_sb])
    p.dma("sp", g64_sb[:], g64[:], [g64], [g64_sb])
    p.ts("dve", sc2[:, 0:1], scal_sb[:, 0:1], -1.0, None, ALU.mult, None, [scal_sb], [sc2])
    p.ts("dve", sc2[:, 1:2], scal_sb[:, 1:2], float(1.0 - lam_init), None, ALU.mult, None, [scal_sb], [sc2])
    p.act(sc2[:, 2:6], scal_sb[:, 2:6], AF.Exp, [scal_sb], [sc2])

    kC_sb = p.sb("kC", [64, 2, L + LC], BF16)
    qC_t = [p.sb("qCt%d" % i, [64, 2, 512], BF16) for i in range(2)]
    vC_sb = p.sb("vC", [128, NCH, 2, 128], BF16)
    kA_sb = p.sb("kA", [64, L + LC], BF16)
    qA_t = [p.sb("qAt%d" % i, [64, 4, 512], BF16) for i in range(2)]
    vA_sb = p.sb("vA", [128, NCH, 128], BF16)
    NSPL = 4
    Ls = (L + LC) // NSPL
    if nat is None:
        for i in range(NSPL):
            for h in range(2):
                p.dma("sp", kC_sb[:, h, i * Ls:(i + 1) * Ls], kC[h * 64:(h + 1) * 64, i * Ls:(i + 1) * Ls], [kC], [kC_sb])
            p.dma("sp", kA_sb[:, i * Ls:(i + 1) * Ls], kA[:, i * Ls:(i + 1) * Ls], [kA], [kA_sb])
    else:
        Lp = L // NSPL
        for h in range(2):
            rows = slice(g * 128 + h * 64, g * 128 + (h + 1) * 64)
            for i in range(NSPL):
                p.dma("sp", kC_sb[:, h, i * Lp:(i + 1) * Lp], nat["kC"][rows, LC + i * Lp:LC + (i + 1) * Lp], [], [])
            p.dma("sp", kC_sb[:, h, L:L + LC], nat["kC"][rows, 0:LC], [], [])
        for i in range(NSPL):
            p.dma("sp", kA_sb[:, i * Lp:(i + 1) * Lp], nat["kA"][g * 64:(g + 1) * 64, LC + i * Lp:LC + (i + 1) * Lp], [], [])
        p.dma("sp", kA_sb[:, L:L + LC], nat["kA"][g * 64:(g + 1) * 64, 0:LC], [], [])
    def load_qC(t):
        buf = qC_t[t % 2]
        for h in range(2):
            p.dma("sp", buf[:, h, :], qC[h * 64:(h + 1) * 64, t * 512:(t + 1) * 512], [qC], [buf])
        return buf

    def load_qA(t):
        buf = qA_t[t % 2]
        for hh in range(4):
            p.dma("sp", buf[:, hh, :], qA[hh * 64:(hh + 1) * 64, t * 512:(t + 1) * 512], [qA], [buf])
        return buf
    CS = NCH // 2
    if nat is None:
        for c0 in range(0, NCH, CS):
            p.dma("sp", vC_sb[:, c0:c0 + CS, :, :], vC[:, c0:c0 + CS, :, :], [vC], [vC_sb])
            p.dma("sp", vA_sb[:, c0:c0 + CS, :], vA[:, c0:c0 + CS, :], [vA], [vA_sb])
    else:
        p.memset("pool", vC_sb[:, :, :, 64:128], 1.0, [vC_sb])
        p.memset("pool", vA_sb[:, :, 64:128], 1.0, [vA_sb])
        for j in range(0, NCH, 2):
            tk = LC + j * 128 if j < NLB else (j - NLB) * 128
            q_ = "sp" if (j // 2) % 2 == 0 else "pool"
            for h in range(2):
                cs = (2 * g + h) * 64
                p.dma(q_, vC_sb[:, j:j + 2, h, 0:64], nat["vC"][tk:tk + 256, cs:cs + 64].rearrange("(c s) v -> s c v", s=128),
                      [], [])
            p.dma(q_, vA_sb[:, j:j + 2, 0:64], nat["vA"][tk:tk + 256, g * 64:(g + 1) * 64].rearrange("(c s) v -> s c v", s=128),
                  [], [])
        p.dma_fence()
    qCc_sb = p.sb("qCc", [64, 2, LC], BF16)
    qAc_sb = p.sb("qAc", [64, 4, LC], BF16)
    if need_ctx:
        for h in range(2):
            p.dma("sp", qCc_sb[:, h, :], qC_c[h * 64:(h + 1) * 64, :], [qC_c], [qCc_sb])
        for hh in range(4):
            p.dma("sp", qAc_sb[:, hh, :], qA_c[hh * 64:(hh + 1) * 64, :], [qA_c], [qAc_sb])

    NPS = 3
    ps = [p.ps("ps%d" % i, (128, 1024)) for i in range(NPS)]
    po = [p.ps("po%d" % i) for i in range(2)]
    NE = 4
    E = [p.sb("E%d" % i, [128, 1024], BF16) for i in range(NE)]
    rz = [p.sb("rz%d" % i, [64, 512], F32) for i in range(2)]
    on = [p.sb("on%d" % i, [64, 512], F32) for i in range(2)]
    o_t = p.sb("o_t", [64, 512], F32)
    sq_t = p.sb("sq_t", [64, 512], F32)
    rs_t = p.sb("rs_t", [64, 512], F32)
    rtmp = p.sb("rtmpa", [64, 512], F32)
    ob = [p.sb("ob%d" % i, [64, 512], BF16) for i in range(2)]
    st = {"e": 0, "ps": 0, "po": 0, "ob": 0, "qm": 0}
    cm_sb = p.sb("cmsel", [64, 2], F32)
    p.memset("pool", cm_sb[:, :], 0.0, [cm_sb])
    p.memset("pool", cm_sb[0:32, 0:1], 1.0, [cm_sb])
    p.memset("pool", cm_sb[32:64, 1:2], 1.0, [cm_sb])
    qm_t = [[p.sb("qm%d_%d" % (i, c), [64, 2, 512], BF16) for c in range(2)] for i in range(2)]
    SC_C = 32 ** -0.5
    SC_A = 64 ** -0.5

    def diff_tile(q_sb, q0, N, chunks, out_dram, o0):
        qm = qm_t[st["qm"] % 2]
        st["qm"] += 1
        for c in range(2):
            p.ts("pool" if c == 0 else "dve", qm[c][:, :, 0:N], q_sb[:, :, q0:q0 + N], cm_sb[:, c:c + 1], None, ALU.mult, None,
                 [q_sb, cm_sb], [qm[c]])
        for h in ((1, 0, 1) if 'swap' in DBG else range(2)):
            POs = [po[c] for c in range(2)]
            for c in range(2):
                r0 = 32 * c
                PO = POs[c]
                npair = len(chunks) // 2

                def pv(pi, e):
                    for u in range(2):
                        j = chunks[2 * pi + u]
                        p.mm(PO[:, 0:N], vC_sb[:, j, h, :], e[:, u * 512:u * 512 + N], pi == 0 and u == 0,
                             pi == npair - 1 and u == 1, [vC_sb, e], [PO])

                pend = []
                for pi in range(npair):
                    PS = ps[st["ps"] % NPS]
                    st["ps"] += 1
                    e = E[st["e"] % NE]
                    st["e"] += 1
                    for u in range(2):
                        j = chunks[2 * pi + u]
                        p.mm(PS[:, u * 512:u * 512 + N], kC_sb[:, h, j * 128:(j + 1) * 128], qm[c][:, h, 0:N],
                             True, True, [kC_sb, qm[c]], [PS])
                    if len(pend) == 2:
                        pv(*pend.pop(0))
                    p.act(e[:, :].rearrange("p (u n) -> p u n", u=2)[:, :, 0:N], PS[:, :].rearrange("p (u n) -> p u n", u=2)[:, :, 0:N],
                          AF.Exp, [PS], [e], scale=SC_C)
                    pend.append((pi, e))
                for pe_ in pend:
                    pv(*pe_)
            for c in range(2):
                p.recip(rz[c][:, 0:N], POs[c][64:128, 0:N], [POs[c]], [rz[c]])
                p.tt("dve", on[c][:, 0:N], POs[c][0:64, 0:N], rz[c][:, 0:N], ALU.mult, [POs[c], rz[c]], [on[c]])
            if 'dbg' in DBG and out_dram is oC_c:
                for c in range(2):
                    p.copy("dve", dbgz[c][:, 0:N], POs[c][:, 0:N], [POs[c]], [dbgz[c]])
                    p.dma("sp", dbg2[h, c, :, 0:N], dbgz[c][:, 0:N], [dbgz[c]], [])
                for c in range(2):
                    p.dma("sp", dbg[h, c, :, 0:N], rz[c][:, 0:N], [rz[c]], [])
                    p.dma("sp", dbg[h, 2 + c, :, 0:N], on[c][:, 0:N], [on[c]], [])
            p.stt("pool", o_t[:, 0:N], on[1][:, 0:N], sc2[0:64, 0:1], on[0][:, 0:N], ALU.mult, ALU.add,
                  [on[0], on[1], sc2], [o_t])
            p.tt("pool", sq_t[:, 0:N], o_t[:, 0:N], o_t[:, 0:N], ALU.mult, [o_t], [sq_t])
            pm = POs[0]
            p.mm(pm[0:64, 0:N], g64_sb[:, :], sq_t[:, 0:N], True, True, [g64_sb, sq_t], [pm])
            p.rsqrt(rs_t[:, 0:N], pm[0:64, 0:N], EPS, rtmp, [pm], [rs_t])
            o_b = ob[st["ob"] % 2]
            st["ob"] += 1
            p.stt("dve", o_b[:, 0:N], o_t[:, 0:N], sc2[0:64, 1:2], rs_t[:, 0:N], ALU.mult, ALU.mult, [o_t, sc2, rs_t], [o_b])
            p.dma("pool", out_dram[h * 64:(h + 1) * 64, o0:o0 + N], o_b[:, 0:N], [o_b], [])

    den = [p.sb("den%d" % i, [64, 512], F32) for i in range(2)]
    sinkL = p.sb("sinkL", [1, 128], BF16)
    esrow = p.sb("esrow", [1, 512], BF16)
    p.memset("pool", sinkL[:, 0:64], 0.0, [sinkL])
    p.memset("pool", sinkL[:, 64:128], 1.0, [sinkL])
    for h in range(4):
        p.copy("dve", esrow[0:1, h * 128:(h + 1) * 128], sc2[0:1, 2 + h:3 + h].to_broadcast([1, 128]), [sc2], [esrow])

    def win_pv(PO, first, last, kc, e, fin):
        p.mm(PO[:, :], vA_sb[:, kc, :], e[:, 0:512], first, False, [vA_sb, e], [PO])
        if not last:
            return
        p.mm(PO[:, :], sinkL[:, :], esrow[:, :], False, True, [sinkL, esrow], [PO])
        out_dram, o0 = fin
        dn = den[st["ob"] % 2]
        p.recip(dn[:, :], PO[64:128, :], [PO], [dn])
        o_b = ob[st["ob"] % 2]
        st["ob"] += 1
        p.tt("dve", o_b[:, :], PO[0:64, :], dn[:, :], ALU.mult, [PO, dn], [o_b])
        p.dma("pool", out_dram[:, o0:o0 + 128].rearrange("(h v) q -> v h q", v=64),
              o_b[:, :].rearrange("v (h q) -> v h q", h=4), [o_b], [])

    def win_blocks(blocks):
        pend = None
        for (q_sb, q0, chunks, out_dram, o0) in blocks:
            PO = po[st["po"] % 2]
            st["po"] += 1
            n = len(chunks)
            for ji, (kc, mt) in enumerate(chunks):
                PS = ps[st["ps"] % NPS]
                st["ps"] += 1
                e = E[st["e"] % NE]
                st["e"] += 1
                for h in range(4):
                    p.mm(PS[:, h * 128:(h + 1) * 128], kA_sb[:, kc * 128:(kc + 1) * 128],
                         q_sb[:, h, q0:q0 + 128], True, True, [kA_sb, q_sb], [PS])
                if pend is not None:
                    win_pv(*pend)
                p.act(e[:, 0:512], PS[:, 0:512], AF.Exp, [PS], [e], scale=SC_A)
                if mt is not None:
                    e3 = e[:, 0:512].rearrange("p (h q) -> p h q", h=4)
                    m3 = mask_sb[:, mt, :].unsqueeze(1).to_broadcast([128, 4, 128])
                    p.tt("dve", e3, e3, m3, ALU.mult, [e, mask_sb], [e])
                pend = (PO, ji == 0, ji == n - 1, kc, e, (out_dram, o0))
        if pend is not None:
            win_pv(*pend)

    ctx_ch = list(range(NLB, NCH))
    all_ch = list(range(NCH))
    return dict(diff_tile=diff_tile, win_blocks=win_blocks, ctx_ch=ctx_ch, all_ch=all_ch, NLB=NLB, NCH=NCH,
                load_qC=load_qC, load_qA=load_qA, qCc_sb=qCc_sb, qAc_sb=qAc_sb, oC=oC, oA=oA, oC_c=oC_c, oA_c=oA_c)


def emit_p2_attn(p, L, LC, need_ctx, lam_init, do_A=True, do_C=True, g=None, nat=None):
    p.make_eps(EPS)
    a = p2_attn(p, L, LC, need_ctx, lam_init, g, nat)
    NLB = a["NLB"]
    blocks = []
    if do_A:
        qbuf = None
        for j in range(NLB):
            if j % 4 == 0:
                qbuf = a["load_qA"](j // 4)
            ch = []
            if j > 0:
                ch.append((j - 1, 0))
            ch.append((j, None))
            if j + 1 < NLB:
                ch.append((j + 1, 1))
            ch += [(c, None) for c in a["ctx_ch"]]
            blocks.append((qbuf, (j % 4) * 128, ch, a["oA"], j * 128))
            if j % 4 == 3 or j == NLB - 1:
                a["win_blocks"](blocks)
                blocks = []
        if need_ctx:
            for j in range(LC // 128):
                blocks.append((a["qAc_sb"], j * 128, [(c, None) for c in a["ctx_ch"]], a["oA_c"], j * 128))
            a["win_blocks"](blocks)
    TQ = 512
    for t in range(L // TQ if (do_C and 'ctxonly' not in DBG) else 0):
        a["diff_tile"](a["load_qC"](t), 0, TQ, a["all_ch"], a["oC"], t * TQ)
    if need_ctx and do_C:
        a["diff_tile"](a["qCc_sb"], 0, LC, a["ctx_ch"], a["oC_c"], 0)


def build_p2_attn(L, LC, need_ctx, lam_init, do_A=True, do_C=True):
    nc = bass.Bass("TRN2", target_bir_lowering=False)
    p = Prog(nc)
    emit_p2_attn(p, L, LC, need_ctx, lam_init, do_A, do_C)
    p.finish()
    return nc


def attn_masks():
    import ml_dtypes
    i = np.arange(128)[:, None]
    j = np.arange(128)[None, :]
    return np.stack([(j <= i), (i <= j)], 0).astype(np.float32).astype(ml_dtypes.bfloat16)


def v_layout(v, heads):
    import ml_dtypes
    S = v.shape[0]
    out = np.ones((128, S // 128, heads, 128), ml_dtypes.bfloat16)
    out[:, :, :, 0:64] = v.reshape(S // 128, 128, heads, 64).transpose(1, 0, 2, 3)
    return out


def p2_hgrn(p, L, LC, need_ctx, g=None, nat=None):
    Tt = LC + L
    CH = 32
    qB = p.dram("qB", [2, 2, 64, Tt], BF16, "ExternalInput")
    kB = p.dram("kB", [2, 2, 64, Tt], BF16, "ExternalInput")
    if nat is None:
        LFb = p.dram("LFb", [2, 2, 128, Tt // 128, 64], F32, "ExternalInput")
        LFc = p.dram("LFc", [2, 2, 32, Tt // 32, 64], F32, "ExternalInput")
        KBc = p.dram("KBc", [2, 2, 32, Tt // 32, 64], BF16, "ExternalInput")
        vBc = p.dram("vBc", [2, 32, Tt // 32, 64], BF16, "ExternalInput")
    gB = p.dram("gBf", [2, 64, Tt], BF16, "ExternalInput")
    ogB = p.dram("ogB", [64, 1], F32, "ExternalInput")
    ucum = p.dram("ucum", [2, 128, 128], F32, "ExternalInput")
    lst = p.dram("lst", [2, 32, 32], F32, "ExternalInput")
    tri = p.dram("tri", [2, 32, 32], F32, "ExternalInput")
    g64 = p.dram("g64b", [64, 64], F32, "ExternalInput")
    oB = p.dram("oB", [128, L], BF16, "ExternalOutput")
    oB_c = p.dram("oB_c", [128, LC], BF16, "ExternalOutput")

    og_sb = p.sb("ogB", [64, 1], F32)
    ucum_sb = p.sb("ucum", [128, 2, 128], F32)
    lst_sb = p.sb("lst", [32, 2, 32], F32)
    tri_sb = p.sb("tri", [32, 2, 32], F32)
    g64_sb = p.sb("g64b", [64, 64], F32)
    p.dma("sp", og_sb[:], ogB[:], [ogB], [og_sb])
    p.dma("sp", ucum_sb[:], ucum[:].rearrange("c p n -> p c n"), [ucum], [ucum_sb])
    p.dma("sp", lst_sb[:], lst[:].rearrange("c p n -> p c n"), [lst], [lst_sb])
    p.dma("sp", tri_sb[:], tri[:].rearrange("c p n -> p c n"), [tri], [tri_sb])
    p.dma("sp", g64_sb[:], g64[:], [g64], [g64_sb])

    o_acc = [p.sb("oacc%d" % h, [64, Tt], F32) for h in range(2)]
    NSC = 4
    S = [p.sb("S%d" % i, [64, 64], F32) for i in range(NSC)]
    S_bf = [p.sb("Sbf%d" % i, [64, 64], BF16) for i in range(NSC)]
    for i in range(NSC):
        p.memset("pool", S[i][:, :], 0.0, [S[i]])
        p.memset("pool", S_bf[i][:, :], 0.0, [S_bf[i]])
    NPB = 2
    def mk(nm, shape, dt):
        return [[p.sb("%s%d_%d" % (nm, i, j), shape, dt) for j in range(NPB)] for i in range(NSC)]
    qe = mk("qe", [64, 512], BF16)
    ke = mk("ke", [64, 512], BF16)
    qb = mk("qb", [64, 512], BF16)
    kend = mk("kend", [32, 1024], BF16)
    ebend = mk("ebend", [64, 16], F32)
    Vc = mk("Vc", [32, 1024], BF16)
    ATm = mk("ATm", [32, 512], BF16)
    NT_ = 4
    LFt_sb = [p.sb("LFt%d" % i, [128, 4, 64], F32) for i in range(NT_)]
    LFc_sb = [p.sb("LFc%d" % i, [32, 1024], F32) for i in range(NT_)]
    Kc_sb = [p.sb("Kc%d" % i, [32, 1024], BF16) for i in range(NT_)]
    qT_sb = [p.sb("qTb%d" % i, [64, 512], BF16) for i in range(NT_)]
    kT_sb = [p.sb("kTb%d" % i, [64, 512], BF16) for i in range(NT_)]
    bT_sb = [p.sb("bT%d" % i, [64, 512], F32) for i in range(2)]
    d1_sb = [p.sb("d1%d" % i, [64, 512], F32) for i in range(2)]
    E_sb = [p.sb("Eb%d" % i, [64, 512], F32) for i in range(3)]
    ec_sb = [p.sb("ec%d" % i, [32, 512], F32) for i in range(2)]
    gT_sb = [p.sb("gTb%d" % i, [64, 512], BF16) for i in range(2)]
    o_t = [p.sb("otb%d" % i, [64, 512], F32) for i in range(2)]
    sq_t = p.sb("sqb", [64, 512], F32)
    rs_t = p.sb("rsb", [64, 512], F32)
    rtmp = p.sb("rtmpb", [64, 512], F32)
    on_t = p.sb("onb", [64, 512], F32)
    ob_t = [p.sb("obb%d" % i, [64, 512], BF16) for i in range(2)]
    PB = p.ps("PB")
    PC = PB
    PA = p.ps("PA")
    PO = [p.ps("PO%d" % i) for i in range(NSC)]
    PD = [p.ps("PD%d" % i) for i in range(2)]
    for tv in PD:
        tv.view = tv[0:64, 0:64]
    st = {"t": 0, "e": 0, "ec": 0, "pd": 0, "g": 0, "ot": 0, "ob": 0}
    seen = {}

    nlat = L // 512
    sbs_lat = [("lat", j, LC + 512 * j, 512) for j in range(nlat)]
    ctxsb = ("ctx", 0, 0, LC)
    order = [[ctxsb] + sbs_lat, [ctxsb] + sbs_lat[::-1]]

    def prep(sc, d, h, sb, par):
        _, _, t0, n = sb
        nch = n // CH
        nb = n // 128
        ti = st["t"] % NT_
        st["t"] += 1
        lft, lfc, kc, qt, kt, bt, d1 = LFt_sb[ti], LFc_sb[ti], Kc_sb[ti], qT_sb[ti], kT_sb[ti], bT_sb[ti % 2], d1_sb[ti % 2]
        hs = slice(h * 64, (h + 1) * 64)
        b0 = t0 // 128
        c0 = t0 // CH
        if nat is None:
            p.dma("sp", lft[:, 0:nb, :], LFb[d, h, :, b0:b0 + nb, :], [LFb], [lft])
            p.dma("sp", lfc[:, 0:nch * 64].rearrange("s (c k) -> s c k", k=64), LFc[d, h, :, c0:c0 + nch, :], [LFc], [lfc])
            p.dma("sp", kc[:, 0:nch * 64].rearrange("s (c k) -> s c k", k=64), KBc[d, h, :, c0:c0 + nch, :], [KBc], [kc])
        else:
            ncs = slice((2 * g + h) * 64, (2 * g + h + 1) * 64)
            p.dma("sp", lft[:, 0:nb, :], nat["LF"][d, t0:t0 + n, ncs].rearrange("(b t) k -> t b k", t=128), [], [lft])
            p.dma("sp", lfc[:, 0:nch * 64].rearrange("s (c k) -> s c k", k=64),
                  nat["LF"][d, t0:t0 + n, ncs].rearrange("(c s) k -> s c k", s=CH), [], [lfc])
            p.dma("sp", kc[:, 0:nch * 64].rearrange("s (c k) -> s c k", k=64),
                  nat["KB"][d, t0:t0 + n, ncs].rearrange("(c s) k -> s c k", s=CH), [], [kc])
        p.dma("sp", qt[:, 0:n], qB[d, h, :, t0:t0 + n], [qB], [qt])
        p.dma("sp", kt[:, 0:n], kB[d, h, :, t0:t0 + n], [kB], [kt])
        vc = Vc[sc][par]
        if nat is None:
            p.dma("sp", vc[:, 0:nch * 64].rearrange("s (c k) -> s c k", k=64), vBc[h, :, c0:c0 + nch, :], [vBc], [vc])
        else:
            p.dma("sp", vc[:, 0:nch * 64].rearrange("s (c k) -> s c k", k=64),
                  nat["vB"][t0:t0 + n, ncs].rearrange("(c s) k -> s c k", s=CH), [], [vc])
        yield
        for b in range(nb):
            p.mm(PB[0:64, b * 128:(b + 1) * 128], lft[:, b, :], ucum_sb[:, d, :], True, True, [lft, ucum_sb], [PB])
        p.copy("act", bt[:, 0:n], PB[0:64, 0:n], [PB], [bt])
        yield
        bt3 = bt[:, 0:n].rearrange("p (c s) -> p c s", s=CH)
        d13 = d1[:, 0:n].rearrange("p (c s) -> p c s", s=CH)
        p.tt("dve", d13, bt3, bt3[:, :, 16:17].to_broadcast([64, nch, CH]), ALU.subtract, [bt], [d1])
        yield
        e1 = E_sb[st["e"] % 3]; st["e"] += 1
        p.act(e1[:, 0:n], d1[:, 0:n], AF.Exp, [d1], [e1])
        p.tt("pool", qe[sc][par][:, 0:n], qt[:, 0:n], e1[:, 0:n], ALU.mult, [qt, e1], [qe[sc][par]])
        yield
        e2 = E_sb[st["e"] % 3]; st["e"] += 1
        p.act(e2[:, 0:n], d1[:, 0:n], AF.Exp, [d1], [e2], scale=-1.0)
        p.tt("dve", ke[sc][par][:, 0:n], kt[:, 0:n], e2[:, 0:n], ALU.mult, [kt, e2], [ke[sc][par]])
        yield
        e3 = E_sb[st["e"] % 3]; st["e"] += 1
        p.act(e3[:, 0:n], bt[:, 0:n], AF.Exp, [bt], [e3])
        p.tt("pool", qb[sc][par][:, 0:n], qt[:, 0:n], e3[:, 0:n], ALU.mult, [qt, e3], [qb[sc][par]])
        eidx = CH - 1 if d == 0 else 0
        p.act(ebend[sc][par][:, 0:nch], bt3[:, :, eidx], AF.Exp, [bt], [ebend[sc][par]])
        yield
        for hf in range(nch * 64 // 512):
            p.mm(PC[0:32, :], lst_sb[:, d, :], lfc[:, hf * 512:(hf + 1) * 512], True, True, [lst_sb, lfc], [PC])
            ec = ec_sb[st["ec"] % 2]; st["ec"] += 1
            p.act(ec[:, :], PC[0:32, :], AF.Exp, [PC], [ec])
            p.tt("dve", kend[sc][par][:, hf * 512:(hf + 1) * 512], kc[:, hf * 512:(hf + 1) * 512], ec[:, :], ALU.mult,
                 [kc, ec], [kend[sc][par]])
            yield
        for c in range(nch):
            p.mm(PA[0:32, c * CH:(c + 1) * CH], ke[sc][par][:, c * CH:(c + 1) * CH], qe[sc][par][:, c * CH:(c + 1) * CH], True, True,
                 [ke[sc][par], qe[sc][par]], [PA])
        p.tt("dve", ATm[sc][par][:, 0:n].rearrange("p (c s) -> p c s", s=CH), PA[0:32, 0:n].rearrange("p (c s) -> p c s", s=CH),
             tri_sb[:, d, :].unsqueeze(1).to_broadcast([32, nch, CH]), ALU.mult, [PA, tri_sb], [ATm[sc][par]])

    def chunk(sc, par, c):
        vc = Vc[sc][par]
        p.mm(PO[sc][0:64, c * CH:(c + 1) * CH], vc[:, c * 64:(c + 1) * 64], ATm[sc][par][:, c * CH:(c + 1) * CH], True, False,
             [vc, ATm[sc][par]], [PO[sc]])
        p.mm(PO[sc][0:64, c * CH:(c + 1) * CH], S_bf[sc][:, :], qb[sc][par][:, c * CH:(c + 1) * CH], False, True,
             [S_bf[sc], qb[sc][par]], [PO[sc]])
        if 'c1' in DBG:
            return
        pd = PD[st["pd"] % 2]; st["pd"] += 1
        p.mm(pd.view, kend[sc][par][:, c * 64:(c + 1) * 64], vc[:, c * 64:(c + 1) * 64], True, True, [kend[sc][par], vc], [pd])
        if 'c2' in DBG:
            return
        p.stt("dve", S[sc][:, :], S[sc][:, :], ebend[sc][par][:, c:c + 1], pd.view, ALU.mult, ALU.add,
              [S[sc], ebend[sc][par], pd], [S[sc]])
        if 'c3' in DBG:
            return
        p.copy("pool", S_bf[sc][:, :], S[sc][:, :], [S[sc]], [S_bf[sc]])

    def finalize(sc, h, sb):
        kind, j, t0, n = sb
        if 'nofin' in DBG:
            return
        key = (h, kind, j)
        acc = o_acc[h]
        if key not in seen:
            seen[key] = 1
            p.copy("act", acc[:, t0:t0 + n], PO[sc][0:64, 0:n], [PO[sc]], [acc])
            return
        if kind == "ctx" and not need_ctx:
            return
        o = o_t[st["ot"] % 2]; st["ot"] += 1
        p.tt("dve", o[:, 0:n], PO[sc][0:64, 0:n], acc[:, t0:t0 + n], ALU.add, [PO[sc], acc], [o])
        gt = gT_sb[st["g"] % 2]; st["g"] += 1
        p.dma("sp", gt[:, 0:n], gB[h, :, t0:t0 + n], [gB], [gt])
        p.tt("pool", sq_t[:, 0:n], o[:, 0:n], o[:, 0:n], ALU.mult, [o], [sq_t])
        p.mm(PB[0:64, 0:n], g64_sb[:, :], sq_t[:, 0:n], True, True, [g64_sb, sq_t], [PB])
        p.rsqrt(rs_t[:, 0:n], PB[0:64, 0:n], EPS, rtmp, [PB], [rs_t])
        p.stt("dve", on_t[:, 0:n], o[:, 0:n], og_sb[:, 0:1], rs_t[:, 0:n], ALU.mult, ALU.mult, [o, og_sb, rs_t], [on_t])
        ob = ob_t[st["ob"] % 2]; st["ob"] += 1
        p.tt("pool", ob[:, 0:n], on_t[:, 0:n], gt[:, 0:n], ALU.mult, [on_t, gt], [ob])
        if kind == "ctx":
            p.dma("pool", oB_c[h * 64:(h + 1) * 64, 0:n], ob[:, 0:n], [ob], [])
        else:
            p.dma("pool", oB[h * 64:(h + 1) * 64, t0 - LC:t0 - LC + n], ob[:, 0:n], [ob], [])

    def run():
        nsteps = 1 + nlat

        def make(step):
            par = step % NPB
            scans, gens = [], []
            for d in range(2):
                sb = order[d][step]
                for h in range(2):
                    sc = d * 2 + h
                    scans.append((sc, d, h, sb))
                    gens.append(prep(sc, d, h, sb, par))
            return scans, gens

        def advance(gens, k):
            for _ in range(k):
                while gens:
                    try:
                        next(gens[0])
                        break
                    except StopIteration:
                        gens.pop(0)

        scans, gens = make(0)
        for g_ in gens:
            next(g_)
        advance(gens, 10 ** 6)
        for step in range(nsteps):
            par = step % NPB
            nxt = None
            if step + 1 < nsteps:
                nxt = make(step + 1)
                for g_ in nxt[1]:
                    next(g_)
            nch = scans[0][3][3] // CH
            per = (4 * 11 + nch - 1) // nch
            for ci in range(nch):
                for (sc, d, h, sb) in scans:
                    chunk(sc, par, ci if d == 0 else nch - 1 - ci)
                if nxt is not None:
                    advance(nxt[1], per)
            if nxt is not None:
                advance(nxt[1], 10 ** 6)
            for (sc, d, h, sb) in scans:
                finalize(sc, h, sb)
            if nxt is not None:
                scans = nxt[0]

    return run


def emit_p2_hgrn(p, L, LC, need_ctx, g=None, nat=None):
    p.make_eps(EPS)
    run = p2_hgrn(p, L, LC, need_ctx, g, nat)
    run()


def build_p2_hgrn(L, LC, need_ctx):
    nc = bass.Bass("TRN2", target_bir_lowering=False)
    p = Prog(nc)
    emit_p2_hgrn(p, L, LC, need_ctx)
    p.finish()
    return nc


def hgrn_consts():
    t = np.arange(128)
    same = (t[:, None] // 32) == (t[None, :] // 32)
    ucum = np.stack([same & (t[:, None] <= t[None, :]), same & (t[:, None] >= t[None, :])], 0).astype(np.float32)
    s = np.arange(32)
    lst = np.stack([s[:, None] > s[None, :], s[:, None] < s[None, :]], 0).astype(np.float32)
    tri = np.stack([s[:, None] <= s[None, :], s[:, None] >= s[None, :]], 0).astype(np.float32)
    return ucum, lst, tri


def hgrn_layouts(LF, KB, vB):
    Tt = LF.shape[1]
    LFb = np.ascontiguousarray(LF.reshape(2, Tt // 128, 128, 2, 64).transpose(0, 3, 2, 1, 4))
    LFc = np.ascontiguousarray(LF.reshape(2, Tt // 32, 32, 2, 64).transpose(0, 3, 2, 1, 4))
    KBc = np.ascontiguousarray(KB.reshape(2, Tt // 32, 32, 2, 64).transpose(0, 3, 2, 1, 4))
    vBc = np.ascontiguousarray(vB.reshape(Tt // 32, 32, 2, 64).transpose(2, 1, 0, 3))
    return LFb, LFc, KBc, vBc


def emit_p3a(p, T_list):
    p.make_eps(EPS)
    KC = 8
    w_out = p.dram("w_out", [D, D], F32, "ExternalInput")
    mod3 = p.dram("mod3", [128, 6, KC], F32, "ExternalInput")
    cones = p.dram("cones", [128, 128], F32, "ExternalInput")
    segs = []
    for i, Tn in enumerate(T_list):
        segs.append(dict(
            T=Tn,
            oT=p.dram("oT%d" % i, [D, Tn], BF16, "ExternalInput"),
            xT=p.dram("xT%d" % i, [D, Tn], F32, "ExternalInput"),
            xmT=p.dram("xmT%d" % i, [D, Tn], F32, "ExternalOutput"),
            h2T=p.dram("h2T%d" % i, [D, Tn], BF16, "ExternalOutput")))
    w_sb = [p.sb("wo%d" % k, [128, D], BF16) for k in range(KC)]
    mod_sb = p.sb("mod3", [128, 6, KC], F32)
    ones_sb = p.sb("cones", [128, 128], F32)
    p.dma("sp", mod_sb[:], mod3[:], [mod3], [mod_sb])
    p.dma("sp", ones_sb[:], cones[:], [cones], [ones_sb])
    stg = [p.sb("stg%d" % i, [128, D], F32) for i in range(3)]
    ceng = ["dve", "pool", "act"]
    for k in range(KC):
        s = stg[k % 3]
        p.dma("sp", s[:, :], w_out[k * 128:(k + 1) * 128, :], [w_out], [s])
        p.copy(ceng[k % 3], w_sb[k][:, :], s[:, :], [s], [w_sb[k]])
    TW = 512
    NB = 2
    o_sb = [p.sb("o%d" % i, [128, KC, TW], BF16) for i in range(NB)]
    x_sb = [p.sb("x%d" % i, [128, KC, TW], F32) for i in range(NB)]
    xm_sb = [[p.sb("xm%d_%d" % (i, k), [128, TW], F32) for k in range(KC)] for i in range(NB)]
    sq_sb = [p.sb("sq%d" % i, [128, TW], F32) for i in range(2)]
    rstd_sb = p.sb("rstd", [128, TW], F32)
    rtmp = p.sb("rtmp", [128, TW], F32)
    hx_sb = [p.sb("hx%d" % i, [128, TW], F32) for i in range(2)]
    h_sb = [p.sb("h%d" % i, [128, TW], BF16) for i in range(3)]
    pp = [p.ps("pp%d" % i) for i in range(2)]
    pss = p.ps("pss")
    cc = {"t": 0, "h": 0}
    for si, sg in enumerate(segs):
        Tn = sg["T"]
        tw = min(TW, Tn)
        for t in range(Tn // tw):
            bi = cc["t"] % NB
            cc["t"] += 1
            ob, xb, xm = o_sb[bi], x_sb[bi], xm_sb[bi]
            c0 = t * tw
            p.dma("sp", ob[:, :, 0:tw], sg["oT"][:, c0:c0 + tw].rearrange("(k p) n -> p k n", p=128), [sg["oT"]], [ob])
            p.dma("sp", xb[:, :, 0:tw], sg["xT"][:, c0:c0 + tw].rearrange("(k p) n -> p k n", p=128), [sg["xT"]], [xb])
            for m in range(KC):
                P = pp[m % 2]
                for k in range(KC):
                    p.mm(P[:, 0:tw], w_sb[k][:, m * 128:(m + 1) * 128], ob[:, k, 0:tw], k == 0, k == KC - 1, [w_sb[k], ob], [P])
                p.stt("dve", xm[m][:, 0:tw], P[:, 0:tw], mod_sb[:, 3 * si, m:m + 1], xb[:, m, 0:tw], ALU.mult, ALU.add,
                      [P, mod_sb, xb], [xm[m]])
                p.dma("pool", sg["xmT"][m * 128:(m + 1) * 128, c0:c0 + tw], xm[m][:, 0:tw], [xm[m]], [])
            for k in range(KC):
                s = sq_sb[k % 2]
                p.act(s[:, 0:tw], xm[k][:, 0:tw], AF.Square, [xm[k]], [s])
                p.mm(pss[:, 0:tw], ones_sb[:, :], s[:, 0:tw], k == 0, k == KC - 1, [ones_sb, s], [pss])
            p.rsqrt(rstd_sb[:, 0:tw], pss[:, 0:tw], EPS, rtmp, [pss], [rstd_sb])
            for k in range(KC):
                hx = hx_sb[k % 2]
                p.stt("dve", hx[:, 0:tw], xm[k][:, 0:tw], mod_sb[:, 3 * si + 1, k:k + 1], rstd_sb[:, 0:tw], ALU.mult, ALU.mult,
                      [xm[k], mod_sb, rstd_sb], [hx])
                h = h_sb[cc["h"] % 3]
                cc["h"] += 1
                p.act(h[:, 0:tw], hx[:, 0:tw], AF.Identity, [hx, mod_sb], [h], bias=mod_sb[:, 3 * si + 2, k:k + 1])
                p.dma("pool", sg["h2T"][k * 128:(k + 1) * 128, c0:c0 + tw], h[:, 0:tw], [h], [])


def build_p3a(T_list):
    nc = bass.Bass("TRN2", target_bir_lowering=False)
    p = Prog(nc)
    emit_p3a(p, T_list)
    p.finish()
    return nc


LAM_INIT = [0.8 - 0.6 * float(np.exp(-0.3 * l)) for l in range(2)]


def emit_p0(p):
    KC = 8
    NJ = 48
    cvec = p.dram("cvec", [128, KC, 2], F32, "ExternalInput")
    w_mod = p.dram("w_mod", [2, D, 6 * D], F32, "ExternalInput")
    b_mod = p.dram("b_modT", [2, 128, NJ], F32, "ExternalInput")
    ng = p.dram("ngT", [2, 128, 2, KC], F32, "ExternalInput")
    hgl = p.dram("hglT", [128, 2, 4], F32, "ExternalInput")
    hgr = p.dram("hglR", [1, 2, 512], F32, "ExternalInput")
    dlam = p.dram("dlam", [1, 2, 4, 32], F32, "ExternalInput")
    modall = p.dram("modall", [2, 128, 12, KC], F32, "ExternalOutput")
    oml_o = p.dram("oml", [128, 2, 4], F32, "ExternalOutput")
    lbrow_o = p.dram("lbrow", [1, 2, 512], F32, "ExternalOutput")
    lam_o = p.dram("lamb", [128, 2], F32, "ExternalOutput")

    c_sb = p.sb("c", [128, KC, 2], F32)
    sg_sb = p.sb("sg", [128, KC, 2], F32)
    sc_sb = p.sb("sc", [128, KC, 2], F32)
    p.dma("sp", c_sb[:], cvec[:], [cvec], [c_sb])
    p.act(sg_sb[:], c_sb[:], AF.Sigmoid, [c_sb], [sg_sb])
    p.tt("dve", sc_sb[:], c_sb[:], sg_sb[:], ALU.mult, [c_sb, sg_sb], [sc_sb])
    bm_sb = p.sb("bm", [128, 2, NJ], F32)
    ng_sb = p.sb("ng", [128, 2, 2, KC], F32)
    p.dma("sp", bm_sb[:], b_mod[:].rearrange("l p j -> p l j"), [b_mod], [bm_sb])
    p.dma("sp", ng_sb[:], ng[:].rearrange("l p a k -> p l a k"), [ng], [ng_sb])
    wst = [p.sb("wst%d" % i, [128, KC, 512], F32) for i in range(2)]
    pm_ = [p.ps("pm%d" % i) for i in range(2)]
    raw = p.sb("raw", [128, 2, NJ, 2], F32)
    wi = 0
    for l in range(2):
        for piece in range(12):
            w = wst[wi % 2]
            wi += 1
            for k in range(KC):
                p.dma("sp", w[:, k, :], w_mod[l, k * 128:(k + 1) * 128, piece * 512:(piece + 1) * 512], [w_mod], [w])
            for jj in range(4):
                j = piece * 4 + jj
                P = pm_[j % 2]
                for k in range(KC):
                    p.mm(P[:, 0:2], w[:, k, jj * 128:(jj + 1) * 128], sc_sb[:, k, :], k == 0, k == KC - 1, [w, sc_sb], [P])
                p.ts("dve", raw[:, l, j, :], P[:, 0:2], bm_sb[:, l, j:j + 1], None, ALU.add, None, [P, bm_sb], [raw])
    out_sb = p.sb("outm", [128, 2, 12, KC], F32)
    tmp = p.sb("tmpm", [128, KC], F32)
    for l in range(2):
        for v in range(2):
            def grp(g):
                return raw[:, l, g * 8:(g + 1) * 8, v]
            base = 0 if v == 0 else 2
            p.ts("dve", tmp[:, :], grp(1), 1.0, None, ALU.add, None, [raw], [tmp])
            p.tt("dve", out_sb[:, l, base + 0, :], tmp[:, :], ng_sb[:, l, 0, :], ALU.mult, [tmp, ng_sb], [out_sb])
            p.copy("dve", out_sb[:, l, base + 1, :], grp(0), [raw], [out_sb])
            b3 = 4 if v == 0 else 7
            p.copy("dve", out_sb[:, l, b3 + 0, :], grp(2), [raw], [out_sb])
            p.ts("dve", tmp[:, :], grp(4), 1.0, None, ALU.add, None, [raw], [tmp])
            p.tt("dve", out_sb[:, l, b3 + 1, :], tmp[:, :], ng_sb[:, l, 1, :], ALU.mult, [tmp, ng_sb], [out_sb])
            p.copy("dve", out_sb[:, l, b3 + 2, :], grp(3), [raw], [out_sb])
            p.copy("dve", out_sb[:, l, 10 + v, :], grp(5), [raw], [out_sb])
        p.dma("sp", modall[l], out_sb[:, l, :, :], [out_sb], [])

    def lower(src_ap, shape, np_, nm):
        r = p.sb(nm + "r", shape, F32)
        p.dma("sp", r[:], src_ap, [], [r])
        mx = p.sb(nm + "mx", [shape[0], shape[2]], F32)
        e = p.sb(nm + "e", shape, F32)
        den = p.sb(nm + "den", [shape[0], shape[2]], F32)
        pr = p.sb(nm + "p", shape, F32)
        lbt = p.sb(nm + "lb", shape, F32)
        p.tt("dve", mx[:, :], r[:, 0, :], r[:, 1, :], ALU.max, [r], [mx])
        for l in range(2):
            p.tt("dve", e[:, l, :], r[:, l, :], mx[:, :], ALU.subtract, [r, mx], [e])
        p.act(e[:], e[:], AF.Exp, [e], [e])
        p.tt("dve", den[:, :], e[:, 0, :], e[:, 1, :], ALU.add, [e], [den])
        p.recip(den[:, :], den[:, :], [den], [den])
        for l in range(2):
            p.tt("dve", pr[:, l, :], e[:, l, :], den[:, :], ALU.mult, [e, den], [pr])
        p.tt("dve", lbt[:, 0, :], pr[:, 0, :], pr[:, 0, :], ALU.subtract, [pr], [lbt])
        p.tt("dve", lbt[:, 1, :], pr[:, 0, :], pr[:, 1, :], ALU.add, [pr], [lbt])
        p.tt("dve", lbt[:, 1, :], lbt[:, 1, :], pr[:, 0, :], ALU.subtract, [lbt, pr], [lbt])
        return lbt
    lbT = lower(hgl[:], [128, 2, 4], 128, "lp")
    oml_sb = p.sb("omlo", [128, 2, 4], F32)
    p.ts("dve", oml_sb[:], lbT[:], -1.0, 1.0, ALU.mult, ALU.add, [lbT], [oml_sb])
    p.dma("sp", oml_o[:], oml_sb[:], [oml_sb], [])
    lbR = lower(hgr[:], [1, 2, 512], 1, "lr")
    p.dma("sp", lbrow_o[:], lbR[:], [lbR], [])

    dl = p.sb("dl", [1, 2, 4, 32], F32)
    p.dma("sp", dl[:], dlam[:], [dlam], [dl])
    pr2 = p.sb("pr2", [1, 2, 2, 32], F32)
    for l in range(2):
        for a in range(2):
            p.tt("dve", pr2[:, l, a, :], dl[:, l, 2 * a, :], dl[:, l, 2 * a + 1, :], ALU.mult, [dl], [pr2])
    ssum = p.sb("ssum", [1, 4], F32)
    p.op("dve", lambda E: E.reduce_sum(ssum[:, :], pr2[:].rearrange("o l a d -> o (l a) d"), AX.X), [pr2], [ssum])
    p.act(ssum[:, :], ssum[:, :], AF.Exp, [ssum], [ssum])
    lam1 = p.sb("lam1", [1, 2], F32)
    for l in range(2):
        p.tt("dve", lam1[:, l:l + 1], ssum[:, 2 * l:2 * l + 1], ssum[:, 2 * l + 1:2 * l + 2], ALU.subtract, [ssum], [lam1])
        p.ts("dve", lam1[:, l:l + 1], lam1[:, l:l + 1], float(LAM_INIT[l]), None, ALU.add, None, [lam1], [lam1])
    ones1 = p.sb("ones1", [1, 128], F32)
    p.memset("pool", ones1[:, :], 1.0, [ones1])
    P = pm_[0]
    p.mm(P[:, 0:2], ones1[:, :], lam1[:, :], True, True, [ones1, lam1], [P])
    lam_sb = p.sb("lamsb", [128, 2], F32)
    p.copy("dve", lam_sb[:, :], P[:, 0:2], [P], [lam_sb])
    p.dma("sp", lam_o[:], lam_sb[:], [lam_sb], [])


def build_p0():
    nc = bass.Bass("TRN2", target_bir_lowering=False)
    p = Prog(nc)
    emit_p0(p)
    p.finish()
    return nc


_CACHE = {}


def _prog(name, fn, *args):
    key = (name,) + tuple(str(a) for a in args)
    if key not in _CACHE:
        _CACHE[key] = fn(*args)
    return _CACHE[key]


def _run(nc, in_maps):
    res = run_bass_kernel_spmd(nc, in_maps, core_ids=list(range(NCORES)))
    return res.results


def _pp(v):
    return np.ascontiguousarray(np.asarray(v).reshape(8, 128).T)


def kernel_unfused(x, c, ctx, c_ctx, w_mod, b_mod, norm1_g, norm2_g, w_in, win_qnorm_g, win_knorm_g, win_sink,
           hg_lower, hg_onorm_g, diff_qnorm_g, diff_knorm_g, diff_lambda, diff_onorm_g, w_out,
           w_up, conv_w, conv_b, w_down):
    f32 = np.float32
    A = lambda a: np.ascontiguousarray(np.asarray(a, dtype=f32))
    x, c, ctx, c_ctx = A(x), A(c), A(ctx), A(c_ctx)
    w_mod, b_mod, norm1_g, norm2_g, w_in = A(w_mod), A(b_mod), A(norm1_g), A(norm2_g), A(w_in)
    win_qnorm_g, win_knorm_g, win_sink, hg_lower, hg_onorm_g = A(win_qnorm_g), A(win_knorm_g), A(win_sink), A(hg_lower), A(hg_onorm_g)
    diff_qnorm_g, diff_knorm_g, diff_lambda, diff_onorm_g = A(diff_qnorm_g), A(diff_knorm_g), A(diff_lambda), A(diff_onorm_g)
    w_out, w_up, conv_w, conv_b, w_down = A(w_out), A(w_up), A(conv_w), A(conv_b), A(w_down)
    B, L, _ = x.shape
    LC = ctx.shape[1]
    TL = L // 2
    DEPTH = w_in.shape[0]
    cores = [(b, r) for b in range(B) for r in range(2)]
    cat = np.concatenate
    C_ = np.ascontiguousarray

    nc0 = _prog("p0", build_p0)
    b_modT = C_(b_mod.reshape(DEPTH, 48, 128).transpose(0, 2, 1))
    ngT = C_(np.stack([np.stack([_pp(norm1_g[l]), _pp(norm2_g[l])], 1) for l in range(DEPTH)], 0))
    hglT = C_(hg_lower.reshape(DEPTH, 4, 128).transpose(2, 0, 1))
    hglR = C_(hg_lower.reshape(1, DEPTH, 512))
    dl = C_(diff_lambda.reshape(1, DEPTH, 4, 32))
    ims = []
    for (b, r) in cores:
        ims.append(dict(cvec=C_(np.stack([_pp(c[b]), _pp(c_ctx)], -1)), w_mod=w_mod, b_modT=b_modT, ngT=ngT,
                        hglT=hglT, hglR=hglR, dlam=dl))
    r0 = _run(nc0, ims)
    modall = [r0[i]["modall"] for i in range(NCORES)]
    oml = r0[0]["oml"]
    lbrow = r0[0]["lbrow"]
    lamb = r0[0]["lamb"]

    xT = [C_(x[b, r * TL:(r + 1) * TL].T) for (b, r) in cores]
    xcT = [C_(ctx[b].T) for b in range(B)]
    cm, rm = const_mats()
    ropes = [rope_tables(r * TL + np.arange(TL)) for r in range(2)]
    masks = attn_masks()
    ucum, lst, tri = hgrn_consts()
    g64 = np.full((64, 64), 1.0 / 64, f32)

    for l in range(DEPTH):
        need_ctx = l < DEPTH - 1
        nc1 = _prog("p1", build_p1, TL, LC)
        gains = C_(np.stack([np.tile(win_qnorm_g[l], 2), np.tile(win_knorm_g[l], 2),
                             np.tile(diff_qnorm_g[l], 4), np.tile(diff_knorm_g[l], 4)], 1))
        ims = []
        for i, (b, r) in enumerate(cores):
            ims.append(dict(xT=xT[i], xcT=xcT[b], mod=C_(modall[i][l][:, 0:4, :]), w_in=w_in[l], gains=gains,
                            lbT=C_(oml[:, l, :]), lbrow=C_(lbrow[0, l].reshape(2, 256)), ropeT=ropes[r], cmat=cm, rmat=rm))
        r1 = _run(nc1, ims)

        def full(b, nm, axis):
            return cat([r1[2 * b][nm], r1[2 * b + 1][nm]], axis=axis)

        nca = _prog("p2a", build_p2_attn, L, LC, need_ctx, LAM_INIT[l])
        ncb = _prog("p2b", build_p2_hgrn, L, LC, need_ctx)
        ima, imb = [], []
        for i, (b, r) in enumerate(cores):
            P_ = r1[2 * b]
            kC_all = cat([full(b, "kC", 1), P_["kC_c"]], 1)
            vC_all = cat([full(b, "vC", 0), P_["vC_c"]], 0)
            kA_all = cat([full(b, "kA", 1), P_["kA_c"]], 1)
            vA_all = cat([full(b, "vA", 0), P_["vA_c"]], 0)
            scal = np.zeros((128, 8), f32)
            scal[:, 0] = lamb[:, l]
            scal[:, 1] = np.tile(diff_onorm_g[l], 2)
            scal[:, 2:6] = win_sink[l][4 * r:4 * r + 4][None, :]
            ima.append(dict(
                qC=C_(full(b, "qC", 1)[r * 128:(r + 1) * 128]), qC_c=C_(P_["qC_c"][r * 128:(r + 1) * 128]),
                kC=C_(kC_all[r * 128:(r + 1) * 128]), vCp=v_layout(vC_all[:, r * 128:(r + 1) * 128], 2),
                qA=C_(full(b, "qA", 1)[r * 256:(r + 1) * 256]), qA_c=C_(P_["qA_c"][r * 256:(r + 1) * 256]),
                kA=C_(kA_all[r * 64:(r + 1) * 64]), vAp=C_(v_layout(vA_all[:, r * 64:(r + 1) * 64], 1)[:, :, 0, :]),
                scal=scal, masks=masks, g64=g64))
            hs = slice(r * 128, (r + 1) * 128)
            qB = np.stack([cat([P_["qB%d_c" % d], full(b, "qB%d" % d, 1)], 1)[hs].reshape(2, 64, LC + L) for d in range(2)], 0)
            kB = np.stack([cat([P_["kB%d_c" % d], full(b, "kB%d" % d, 1)], 1)[hs].reshape(2, 64, LC + L) for d in range(2)], 0)
            LFa = cat([P_["LF_c"], full(b, "LF", 1)], 1)[:, :, hs]
            KBa = cat([P_["KB_c"], full(b, "KB", 1)], 1)[:, :, hs]
            vBa = cat([P_["vB_c"], full(b, "vB", 0)], 0)[:, hs]
            gBa = cat([P_["gB_c"], full(b, "gB", 1)], 1)[hs].reshape(2, 64, LC + L)
            LFb_, LFc_, KBc_, vBc_ = hgrn_layouts(C_(LFa), C_(KBa), C_(vBa))
            imb.append(dict(qB=C_(qB), kB=C_(kB), LFb=LFb_, LFc=LFc_, KBc=KBc_, vBc=vBc_, gBf=C_(gBa),
                            ogB=C_(hg_onorm_g[l].reshape(64, 1)), ucum=ucum, lst=lst, tri=tri, g64b=g64))
        ra = _run(nca, ima)
        rb = _run(ncb, imb)

        segT = [TL] + ([LC] if need_ctx else [])
        nc3 = _prog("p3a", build_p3a, segT)
        ims = []
        for i, (b, r) in enumerate(cores):
            oT = cat([ra[2 * b]["oA"], ra[2 * b + 1]["oA"], rb[2 * b]["oB"], rb[2 * b + 1]["oB"],
                      ra[2 * b]["oC"], ra[2 * b + 1]["oC"]], 0)
            m = dict(w_out=w_out[l], mod3=C_(modall[i][l][:, 4:10, :]), cones=cm[0],
                     oT0=C_(oT[:, r * TL:(r + 1) * TL]), xT0=xT[i])
            if need_ctx:
                oTc = cat([ra[2 * b]["oA_c"], ra[2 * b + 1]["oA_c"], rb[2 * b]["oB_c"], rb[2 * b + 1]["oB_c"],
                           ra[2 * b]["oC_c"], ra[2 * b + 1]["oC_c"]], 0)
                m.update(oT1=C_(oTc), xT1=xcT[b])
            ims.append(m)
        r3 = _run(nc3, ims)

        ncf = _prog("ffn", build_ffn, segT)
        cw = C_(conv_w[l].reshape(3, 44, 128).transpose(2, 0, 1))
        cb = C_(conv_b[l].reshape(44, 128).T)
        ims = []
        for i, (b, r) in enumerate(cores):
            h2 = r3[i]["h2T0"]
            z = np.zeros((D, 1), h2.dtype)
            left = z if r == 0 else r3[2 * b]["h2T0"][:, -1:]
            right = z if r == 1 else r3[2 * b + 1]["h2T0"][:, 0:1]
            g2 = modall[i][l][:, 10:12, :] if need_ctx else modall[i][l][:, 10:11, :]
            m = dict(w_up=w_up[l], w_down=w_down[l], cw=cw, cb=cb, g2=C_(g2),
                     h2T0=C_(cat([left, h2, right], 1)), xmT0=r3[i]["xmT0"])
            if need_ctx:
                h2c = r3[i]["h2T1"]
                m.update(h2T1=C_(cat([z, h2c, z], 1)), xmT1=r3[i]["xmT1"])
            ims.append(m)
        rf = _run(ncf, ims)
        xT = [rf[i]["outT0"] for i in range(NCORES)]
        if need_ctx:
            xcT = [rf[2 * b]["outT1"] for b in range(B)]

    out = np.empty((B, L, D), f32)
    for i, (b, r) in enumerate(cores):
        out[b, r * TL:(r + 1) * TL] = xT[i].T
    return out


def build_fused(L, LC, depth=2):
    nc = bass.Bass("TRN2", target_bir_lowering=False)
    p = Prog(nc)
    Tt = LC + L
    KC = 8

    def ext(name, shape, dt, kind="ExternalInput"):
        if not hasattr(nc, "k_io"):
            nc.k_io = []
        nc.k_io.append((name, tuple(shape), dt, kind))
        return nc.dram_tensor(name, list(shape), dt, kind=kind).ap()

    def scr(name, shape, dt):
        return nc.dram_tensor(name, list(shape), dt, kind="Internal").ap()

    E = dict(
        xT=ext("xT", [D, L], F32), xcT=ext("xcT", [D, LC], F32),
        cvec=ext("cvec", [128, KC, 2], F32), w_mod=ext("w_mod", [depth, D, 6 * D], F32),
        b_modT=ext("b_modT", [depth, 128, 48], F32), ngT=ext("ngT", [depth, 128, 2, KC], F32),
        hglT=ext("hglT", [128, depth, 4], F32), hglR=ext("hglR", [1, depth, 512], F32), dlam=ext("dlam", [1, depth, 4, 32], F32),
        w_in=ext("w_in", [depth, D, NIN], F32), w_out=ext("w_out", [depth, D, D], F32),
        w_up=ext("w_up", [depth, D, 2 * DFF], F32), w_down=ext("w_down", [depth, DFF, D], F32),
        cw=ext("cw", [depth, 128, 3, 44], F32), cb=ext("cb", [depth, 128, 44], F32),
        gains=ext("gains", [depth, 128, 4], F32), ropeT=ext("ropeT", [4, 128, L], F32),
        cmat=ext("cmat", [3, 128, 128], F32), rmat=ext("rmat", [2, 128, 128], BF16),
        scal=ext("scal", [depth, 2, 128, 8], F32), masks=ext("masks", [2, 128, 128], BF16),
        g64=ext("g64", [64, 64], F32), ogB=ext("ogB", [depth, 64, 1], F32),
        ucum=ext("ucum", [2, 128, 128], F32), lst=ext("lst", [2, 32, 32], F32), tri=ext("tri", [2, 32, 32], F32),
        outT=ext("outT", [D, L], F32, "ExternalOutput"),
    )
    S = dict(
        modall=scr("s_modall", [depth, 128, 12, KC], F32), oml=scr("s_oml", [128, depth, 4], F32),
        lbrow=scr("s_lbrow", [1, depth, 512], F32), lamb=scr("s_lamb", [128, depth], F32),
        qA=scr("s_qA", [512, Tt], BF16), kA=scr("s_kA", [128, Tt], BF16),
        qB=scr("s_qB", [2, 256, Tt], BF16), kB=scr("s_kB", [2, 256, Tt], BF16), gB=scr("s_gB", [256, Tt], BF16),
        qC=scr("s_qC", [256, Tt], BF16), kC=scr("s_kC", [256, Tt], BF16),
        vA=scr("s_vA", [Tt, 128], BF16), vB=scr("s_vB", [Tt, 256], BF16), vC=scr("s_vC", [Tt, 256], BF16),
        LF=scr("s_LF", [2, Tt, 256], F32), KB=scr("s_KB", [2, Tt, 256], BF16),
        oT=scr("s_oT", [D, Tt], BF16),
        xm=scr("s_xm", [D, L], F32), xmc=scr("s_xmc", [D, LC], F32),
        h2=scr("s_h2", [D, L + 2], BF16), h2c=scr("s_h2c", [D, LC + 2], BF16),
        x1=scr("s_x1", [D, L], F32), xc1=scr("s_xc1", [D, LC], F32),
    )

    with p.scope():
        z = p.sb("zero", [128, 2], BF16)
        p.memset("pool", z[:, :], 0.0, [z])
        for t_, n_ in ((S["h2"], L), (S["h2c"], LC)):
            for col in (0, n_ + 1):
                for kk in range(KC):
                    p.dma("sp", t_[kk * 128:(kk + 1) * 128, col:col + 1], z[:, 0:1], [z], [], slow=True)
        p.bind = dict(cvec=E["cvec"], w_mod=E["w_mod"], b_modT=E["b_modT"], ngT=E["ngT"], hglT=E["hglT"], hglR=E["hglR"],
                      dlam=E["dlam"], modall=S["modall"], oml=S["oml"], lbrow=S["lbrow"], lamb=S["lamb"])
        emit_p0(p)

    x_cur, xc_cur = E["xT"], E["xcT"]
    marks = [("P0", 0)]
    nc.k_marks = marks
    for l in range(depth):
        need_ctx = l < depth - 1
        marks.append(("P1_%d" % l, p.cnt["pe"]))
        with p.scope():
            b = dict(xT=x_cur, xcT=xc_cur, mod=S["modall"][l, :, 0:4, :], w_in=E["w_in"][l], gains=E["gains"][l],
                     lbT=S["oml"][:, l, :], lbrow=S["lbrow"][0, l].rearrange("(d k) -> d k", d=2),
                     ropeT=E["ropeT"], cmat=E["cmat"], rmat=E["rmat"])
            fm = dict(qA=S["qA"], kA=S["kA"], qB0=S["qB"][0], kB0=S["kB"][0], qB1=S["qB"][1], kB1=S["kB"][1],
                      gB=S["gB"], qC=S["qC"], kC=S["kC"])
            for nm, ap in fm.items():
                b[nm] = ap[:, LC:Tt]
                b[nm + "_c"] = ap[:, 0:LC]
            for nm in ("vA", "vB", "vC"):
                b[nm] = S[nm][LC:Tt, :]
                b[nm + "_c"] = S[nm][0:LC, :]
            for nm in ("LF", "KB"):
                b[nm] = S[nm][:, LC:Tt, :]
                b[nm + "_c"] = S[nm][:, 0:LC, :]
            p.bind = b
            emit_p1(p, L, LC)
        for g in range(2):
            marks.append(("attn_%d_%d" % (l, g), p.cnt["pe"]))
            with p.scope():
                p.bind = dict(qC=S["qC"][g * 128:(g + 1) * 128, LC:Tt], qC_c=S["qC"][g * 128:(g + 1) * 128, 0:LC],
                              qA=S["qA"][g * 256:(g + 1) * 256, LC:Tt], qA_c=S["qA"][g * 256:(g + 1) * 256, 0:LC],
                              scal=E["scal"][l, g], masks=E["masks"], g64=E["g64"],
                              oA=S["oT"][g * 256:(g + 1) * 256, LC:Tt], oA_c=S["oT"][g * 256:(g + 1) * 256, 0:LC],
                              oC=S["oT"][768 + g * 128:768 + (g + 1) * 128, LC:Tt],
                              oC_c=S["oT"][768 + g * 128:768 + (g + 1) * 128, 0:LC])
                emit_p2_attn(p, L, LC, need_ctx, LAM_INIT[l], True, True, g,
                             dict(kC=S["kC"], vC=S["vC"], kA=S["kA"], vA=S["vA"], lamcol=S["lamb"][:, l:l + 1]))
        for g in range(2):
            marks.append(("hgrn_%d_%d" % (l, g), p.cnt["pe"]))
            with p.scope():
                p.bind = dict(qB=S["qB"][:, g * 128:(g + 1) * 128, :].rearrange("d (h k) t -> d h k t", h=2),
                              kB=S["kB"][:, g * 128:(g + 1) * 128, :].rearrange("d (h k) t -> d h k t", h=2),
                              gBf=S["gB"][g * 128:(g + 1) * 128, :].rearrange("(h k) t -> h k t", h=2),
                              ogB=E["ogB"][l], ucum=E["ucum"], lst=E["lst"], tri=E["tri"], g64b=E["g64"],
                              oB=S["oT"][512 + g * 128:512 + (g + 1) * 128, LC:Tt],
                              oB_c=S["oT"][512 + g * 128:512 + (g + 1) * 128, 0:LC])
                emit_p2_hgrn(p, L, LC, need_ctx, g, dict(LF=S["LF"], KB=S["KB"], vB=S["vB"]))
        segT = [L] + ([LC] if need_ctx else [])
        marks.append(("p3a_%d" % l, p.cnt["pe"]))
        with p.scope():
            p.bind = dict(w_out=E["w_out"][l], mod3=S["modall"][l, :, 4:10, :], cones=E["cmat"][0],
                          oT0=S["oT"][:, LC:Tt], xT0=x_cur, xmT0=S["xm"], h2T0=S["h2"][:, 1:L + 1],
                          oT1=S["oT"][:, 0:LC], xT1=xc_cur, xmT1=S["xmc"], h2T1=S["h2c"][:, 1:LC + 1])
            emit_p3a(p, segT)
        last = l == depth - 1
        marks.append(("ffn_%d" % l, p.cnt["pe"]))
        with p.scope():
            p.bind = dict(h2T0=S["h2"], xmT0=S["xm"], outT0=(E["outT"] if last else S["x1"]),
                          h2T1=S["h2c"], xmT1=S["xmc"], outT1=S["xc1"],
                          g2=S["modall"][l, :, 10:10 + len(segT), :], w_up=E["w_up"][l], w_down=E["w_down"][l],
                          cw=E["cw"][l], cb=E["cb"][l])
            emit_ffn(p, segT)
        x_cur, xc_cur = S["x1"], S["xc1"]
    p.bind = {}
    p.finish()
    nc.k_ninst = sum(len(v) for v in p.ops.values())
    return nc


def fused_inputs(x, c, ctx, c_ctx, w_mod, b_mod, norm1_g, norm2_g, w_in, win_qnorm_g, win_knorm_g, win_sink,
                 hg_lower, hg_onorm_g, diff_qnorm_g, diff_knorm_g, diff_lambda, diff_onorm_g, w_out,
                 w_up, conv_w, conv_b, w_down):
    f32 = np.float32
    C_ = np.ascontiguousarray
    B, L, _ = x.shape
    depth = w_in.shape[0]
    cm, rm = const_mats()
    ucum, lst, tri = hgrn_consts()
    shared = dict(
        w_mod=w_mod, b_modT=C_(b_mod.reshape(depth, 48, 128).transpose(0, 2, 1)),
        ngT=C_(np.stack([np.stack([_pp(norm1_g[l]), _pp(norm2_g[l])], 1) for l in range(depth)], 0)),
        hglT=C_(hg_lower.reshape(depth, 4, 128).transpose(2, 0, 1)), hglR=C_(hg_lower.reshape(1, depth, 512)),
        dlam=C_(diff_lambda.reshape(1, depth, 4, 32)),
        w_in=w_in, w_out=w_out, w_up=w_up, w_down=w_down,
        cw=C_(conv_w.reshape(depth, 3, 44, 128).transpose(0, 3, 1, 2)), cb=C_(conv_b.reshape(depth, 44, 128).transpose(0, 2, 1)),
        gains=C_(np.stack([np.stack([np.tile(win_qnorm_g[l], 2), np.tile(win_knorm_g[l], 2),
                                     np.tile(diff_qnorm_g[l], 4), np.tile(diff_knorm_g[l], 4)], 1) for l in range(depth)], 0)),
        ropeT=rope_tables(np.arange(L)), cmat=cm, rmat=rm, masks=attn_masks(),
        g64=np.full((64, 64), 1.0 / 64, f32), ogB=C_(hg_onorm_g.reshape(depth, 64, 1)), ucum=ucum, lst=lst, tri=tri,
    )
    scal = np.zeros((depth, 2, 128, 8), f32)
    for l in range(depth):
        for g in range(2):
            scal[l, g, :, 1] = np.tile(diff_onorm_g[l], 2)
            scal[l, g, :, 2:6] = win_sink[l][4 * g:4 * g + 4][None, :]
    shared["scal"] = scal
    ims = []
    for core in range(NCORES):
        b = core // 2
        m = dict(shared)
        m.update(xT=C_(x[b].T), xcT=C_(ctx[b].T), cvec=C_(np.stack([_pp(c[b]), _pp(c_ctx)], -1)))
        ims.append(m)
    return ims


def kernel(**inputs):
    f32 = np.float32
    inp = {k: np.ascontiguousarray(np.asarray(v, dtype=f32)) for k, v in inputs.items()}
    B, L, _ = inp["x"].shape
    LC = inp["ctx"].shape[1]
    nc = _prog("fused", build_fused, L, LC, inp["w_in"].shape[0])
    res = _run(nc, fused_inputs(**inp))
    out = np.empty((B, L, D), f32)
    for b in range(B):
        out[b] = res[2 * b]["outT"].T
    return out
```

```python
import contextlib
import numpy as np
import concourse.bass as bass
import concourse.mybir as mybir
from concourse.bass_utils import run_bass_kernel_spmd

F32 = mybir.dt.float32
BF16 = mybir.dt.bfloat16
AF = mybir.ActivationFunctionType
ALU = mybir.AluOpType
AX = mybir.AxisListType

import os
DBG = os.environ.get('KDBG', '')
ENGS = ("pe", "act", "dve", "pool", "sp")
SAME_ENGINE_SYNC = "nosesync" not in DBG


class Buf:
    __slots__ = ("name", "w", "r")

    def __init__(self, name=""):
        self.name = name
        self.w = None
        self.r = {}


class T:
    def __init__(self, t, name=""):
        self.t = t
        self.b = Buf(name)

    def __getitem__(self, idx):
        return self.t[idx]


class Prog:
    def __init__(self, nc, nring=8):
        self.nc = nc
        self.stack = contextlib.ExitStack()
        self.ops = {e: [] for e in ENGS}
        self.cnt = {e: 0 for e in ENGS}
        self.seen = {e: {} for e in ENGS}
        self.dma_tot = {}
        self.ring = {e: 0 for e in ENGS}
        self.nring = nring
        self.sems = {}
        self.nalloc = 0
        self.bind = {}
        self.scopes = []
        for e in ENGS:
            self.sems[("e", e)] = self.stack.enter_context(nc.semaphore("s_" + e))
        for q in ("sp", "act", "pool"):
            for i in range(nring):
                self.sems[("d", q, i)] = self.stack.enter_context(nc.semaphore("d_%s%d" % (q, i)))

    def _stk(self):
        return self.scopes[-1] if self.scopes else self.stack

    def sb(self, name, shape, dt):
        self.nalloc += 1
        return T(self._stk().enter_context(self.nc.sbuf_tensor("%s_%d" % (name, self.nalloc), list(shape), dt)), name)

    def ps(self, name, shape=(128, 512), dt=F32):
        self.nalloc += 1
        return T(self._stk().enter_context(self.nc.psum_tensor("%s_%d" % (name, self.nalloc), list(shape), dt)), name)

    def scratch(self, name, shape, dt):
        return self.nc.dram_tensor(name, list(shape), dt, kind="Internal")

    def dma_fence(self, engines=("pe", "act", "dve", "pool")):
        for e in engines:
            for key, tot in self.dma_tot.items():
                self._wait(e, key, tot)

    def barrier(self):
        for e in ENGS:
            for f in ENGS:
                if f != e and self.cnt[f]:
                    self._wait(e, ("e", f), self.cnt[f])
            for key, tot in self.dma_tot.items():
                self._wait(e, key, tot)

    @contextlib.contextmanager
    def scope(self):
        st = contextlib.ExitStack()
        self.scopes.append(st)
        try:
            yield
        finally:
            self.barrier()
            self.scopes.pop()
            st.close()
            if hasattr(self, "_eps"):
                del self._eps

    def dram(self, name, shape, dt, kind):
        if name in self.bind:
            v = self.bind[name]
            assert tuple(v.shape) == tuple(shape), (name, tuple(v.shape), tuple(shape))
            return T(v, name)
        if not hasattr(self.nc, "k_io"):
            self.nc.k_io = []
        self.nc.k_io.append((name, tuple(shape), dt, kind))
        return T(self.nc.dram_tensor(name, list(shape), dt, kind=kind), name)

    def _wait(self, eng, key, val):
        if key == ("e", eng) and (eng == "pe" or not SAME_ENGINE_SYNC):
            return
        if self.seen[eng].get(key, 0) >= val:
            return
        self.seen[eng][key] = val
        sem = self.sems[key]
        self.ops[eng].append(lambda E, sem=sem, val=val: E.wait_ge(sem, val))

    def _deps(self, eng, reads, writes):
        for t in reads:
            b = t.b
            if b.w is not None:
                self._wait(eng, *b.w)
        for t in writes:
            b = t.b
            if b.w is not None:
                self._wait(eng, *b.w)
            for k, v in b.r.items():
                self._wait(eng, k, v)

    def _mark(self, key, val, reads, writes):
        for t in reads:
            t.b.r[key] = val
        for t in writes:
            t.b.w = (key, val)
            t.b.r = {}

    def op(self, eng, fn, reads=(), writes=()):
        self._deps(eng, reads, writes)
        self.cnt[eng] += 1
        key = ("e", eng)
        sem = self.sems[key]
        self.ops[eng].append(lambda E, fn=fn, sem=sem: fn(E).then_inc(sem, 1))
        self._mark(key, self.cnt[eng], reads, writes)

    def dma(self, q, out_ap, in_ap, reads=(), writes=(), slow=False):
        self._deps(q, reads, writes)
        idx = self.ring[q]
        self.ring[q] = (idx + 1) % self.nring
        key = ("d", q, idx)
        prev = self.dma_tot.get(key, 0)
        if prev:
            self._wait(q, key, prev)
        tot = prev + 16
        self.dma_tot[key] = tot
        sem = self.sems[key]
        kw = dict(allow_slow_non_contiguous=True) if slow else {}
        self.ops[q].append(lambda E, o=out_ap, i=in_ap, sem=sem, kw=kw: E.dma_start(out=o, in_=i, **kw).then_inc(sem, 16))
        self._mark(key, tot, reads, writes)

    def finish(self):
        for key, tot in self.dma_tot.items():
            self._wait(key[1], key, tot)
        nc = self.nc
        ops = self.ops
        with nc.Block() as block:
            @block.tensor
            def _(E):
                for f in ops["pe"]:
                    f(E)

            @block.scalar
            def _(E):
                for f in ops["act"]:
                    f(E)

            @block.vector
            def _(E):
                for f in ops["dve"]:
                    f(E)

            @block.gpsimd
            def _(E):
                for f in ops["pool"]:
                    f(E)

            @block.sync
            def _(E):
                for f in ops["sp"]:
                    f(E)
        self.stack.close()

    def make_eps(self, eps):
        if hasattr(self, "_eps"):
            return
        self._eps = self.sb("epsc", [128, 1], F32)
        self.memset("pool", self._eps[:, :], eps, [self._eps])

    def mm(self, out, lhsT, rhs, start, stop, reads, writes):
        self.op("pe", lambda E: E.matmul(out, lhsT, rhs, start=start, stop=stop), reads, writes)

    def act(self, out, in_, func, reads, writes, bias=None, scale=None, eng="act"):
        kw = {}
        if bias is not None:
            kw["bias"] = bias
        if scale is not None:
            kw["scale"] = scale
        self.op(eng, lambda E: E.activation(out, in_, func, **kw), reads, writes)

    def tt(self, eng, out, in0, in1, op, reads, writes):
        self.op(eng, lambda E: E.tensor_tensor(out, in0, in1, op), reads, writes)

    def ts(self, eng, out, in0, s1, s2, op0, op1, reads, writes):
        if s2 is None:
            self.op(eng, lambda E: E.tensor_scalar(out, in0, s1, 0.0, op0, ALU.add), reads, writes)
        else:
            self.op(eng, lambda E: E.tensor_scalar(out, in0, s1, s2, op0, op1), reads, writes)

    def stt(self, eng, out, in0, scalar, in1, op0, op1, reads, writes):
        eng = "dve"
        self.op(eng, lambda E: E.scalar_tensor_tensor(out, in0, scalar, in1, op0, op1), reads, writes)

    def rsqrt(self, out, in_, eps, tmp, reads, writes):
        n = out.shape[-1]
        np_ = out.shape[0]
        eps_t = self._eps
        self.op("act", lambda E: E.activation(tmp[0:np_, 0:n], in_, AF.Ln, bias=eps_t[0:np_, 0:1]), list(reads) + [eps_t], [tmp])
        self.op("act", lambda E: E.activation(out, tmp[0:np_, 0:n], AF.Exp, scale=-0.5), [tmp], writes)

    def eps_ap(self, eps):
        return self._eps[:, 0:1]

    def recip(self, out, in_, reads, writes):
        self.op("act", lambda E: E.activation(out, in_, AF.Ln), reads, writes)
        self.op("act", lambda E: E.activation(out, out, AF.Exp, scale=-1.0), writes, writes)

    def copy(self, eng, out, in_, reads, writes):
        if eng == "act":
            self.op(eng, lambda E: E.copy(out, in_), reads, writes)
        else:
            self.op(eng, lambda E: E.tensor_copy(out, in_), reads, writes)

    def memset(self, eng, out, val, writes):
        self.op(eng, lambda E: E.memset(out, val), (), writes)


D = 1024
DFF = 2816
NCORES = 8


def emit_ffn(p, T_list, TW=510):
    KC = D // 128
    FC = DFF // 128
    nseg = len(T_list)
    segs = []
    for i, Tn in enumerate(T_list):
        segs.append(dict(T=Tn, h2T=p.dram("h2T%d" % i, [D, Tn + 2], BF16, "ExternalInput"),
                         xmT=p.dram("xmT%d" % i, [D, Tn], F32, "ExternalInput"),
                         outT=p.dram("outT%d" % i, [D, Tn], F32, "ExternalOutput")))
    g2 = p.dram("g2", [128, nseg, KC], F32, "ExternalInput")
    w_up = p.dram("w_up", [D, 2 * DFF], F32, "ExternalInput")
    w_down = p.dram("w_down", [DFF, D], F32, "ExternalInput")
    cw = p.dram("cw", [128, 3, 2 * FC], F32, "ExternalInput")
    cb = p.dram("cb", [128, 2 * FC], F32, "ExternalInput")

    wup_sb = [p.sb("wup%d" % k, [128, 2 * DFF], BF16) for k in range(KC)]
    wdn_sb = [p.sb("wdn%d" % i, [128, D], BF16) for i in range(FC)]
    g2_sb = p.sb("g2", [128, nseg, KC], F32)
    cw_sb = p.sb("cw", [128, 3, 2 * FC], F32)
    cb_sb = p.sb("cb", [128, 2 * FC], F32)
    p.dma("sp", g2_sb[:], g2[:], [g2], [g2_sb])
    p.dma("sp", cw_sb[:], cw[:], [cw], [cw_sb])
    p.dma("sp", cb_sb[:], cb[:], [cb], [cb_sb])
    with p.scope():
        stg = [p.sb("stg%d" % i, [128, 1408], F32) for i in range(3)]
        si = 0
        ceng = ["dve", "pool", "act"]
        for k in range(KC):
            for j in range(4):
                s = stg[si % 3]
                p.dma("sp", s[:, 0:1408], w_up[k * 128:(k + 1) * 128, j * 1408:(j + 1) * 1408], [w_up], [s])
                p.copy(ceng[si % 3], wup_sb[k][:, j * 1408:(j + 1) * 1408], s[:, 0:1408], [s], [wup_sb[k]])
                si += 1
        for i in range(FC):
            s = stg[si % 3]
            p.dma("sp", s[:, 0:D], w_down[i * 128:(i + 1) * 128, :], [w_down], [s])
            p.copy(ceng[si % 3], wdn_sb[i][:, :], s[:, 0:D], [s], [wdn_sb[i]])
            si += 1

    NB = 2
    WB = TW + 2
    h_sb = [p.sb("h%d" % i, [128, KC, WB], BF16) for i in range(NB)]
    xm_sb = [p.sb("xm%d" % i, [128, TW], F32) for i in range(3)]
    gT = [p.sb("gT%d" % i, [128, TW], BF16) for i in range(FC)]
    pa = [p.ps("pa%d" % i) for i in range(2)]
    pv = [p.ps("pv%d" % i) for i in range(2)]
    pd = [p.ps("pd%d" % i) for i in range(2)]
    ya = [p.sb("ya%d" % i, [128, TW], F32) for i in range(2)]
    yv = [p.sb("yv%d" % i, [128, TW], F32) for i in range(2)]
    sa = [p.sb("sa%d" % i, [128, TW], F32) for i in range(2)]
    ot = [p.sb("ot%d" % i, [128, TW], F32) for i in range(2)]
    tiles = [(sgi, t0, min(TW, sg["T"] - t0)) for sgi, sg in enumerate(segs) for t0 in range(0, sg["T"], TW)]

    def load(ti):
        sgi, t0, w = tiles[ti]
        sg = segs[sgi]
        hb = h_sb[ti % NB]
        p.dma("sp", hb[:, :, 0:w + 2], sg["h2T"][:, t0:t0 + w + 2].rearrange("(k p) n -> p k n", p=128), [sg["h2T"]], [hb])

    load(0)
    cc = 0
    xi = 0
    for ti, (sgi, t0, w) in enumerate(tiles):
        sg = segs[sgi]
        if ti + 1 < len(tiles):
            load(ti + 1)
        hb = h_sb[ti % NB]
        for i in range(FC):
            A = pa[cc % 2]
            V = pv[cc % 2]
            for k in range(KC):
                p.mm(A[:, 0:w + 2], wup_sb[k][:, i * 128:(i + 1) * 128], hb[:, k, 0:w + 2], k == 0, k == KC - 1,
                     [wup_sb[k], hb], [A])
            for k in range(KC):
                p.mm(V[:, 0:w + 2], wup_sb[k][:, DFF + i * 128:DFF + (i + 1) * 128], hb[:, k, 0:w + 2], k == 0,
                     k == KC - 1, [wup_sb[k], hb], [V])
            a_t = ya[cc % 2]
            v_t = yv[cc % 2]
            s_t = sa[cc % 2]
            p.act(a_t[:, 0:w], A[:, 0:w], AF.Identity, [A, cw_sb, cb_sb], [a_t], bias=cb_sb[:, i:i + 1],
                  scale=cw_sb[:, 0, i:i + 1])
            p.stt("dve", a_t[:, 0:w], A[:, 1:w + 1], cw_sb[:, 1, i:i + 1], a_t[:, 0:w], ALU.mult, ALU.add,
                  [A, a_t, cw_sb], [a_t])
            p.stt("dve", a_t[:, 0:w], A[:, 2:w + 2], cw_sb[:, 2, i:i + 1], a_t[:, 0:w], ALU.mult, ALU.add,
                  [A, a_t, cw_sb], [a_t])
            p.act(v_t[:, 0:w], V[:, 0:w], AF.Identity, [V, cw_sb, cb_sb], [v_t], bias=cb_sb[:, FC + i:FC + i + 1],
                  scale=cw_sb[:, 0, FC + i:FC + i + 1])
            p.stt("dve", v_t[:, 0:w], V[:, 1:w + 1], cw_sb[:, 1, FC + i:FC + i + 1], v_t[:, 0:w], ALU.mult, ALU.add,
                  [V, v_t, cw_sb], [v_t])
            p.stt("dve", v_t[:, 0:w], V[:, 2:w + 2], cw_sb[:, 2, FC + i:FC + i + 1], v_t[:, 0:w], ALU.mult, ALU.add,
                  [V, v_t, cw_sb], [v_t])
            p.act(s_t[:, 0:w], a_t[:, 0:w], AF.Silu, [a_t], [s_t])
            p.tt("pool", gT[i][:, 0:w], s_t[:, 0:w], v_t[:, 0:w], ALU.mult, [s_t, v_t], [gT[i]])
            cc += 1
        for m in range(KC):
            xb = xm_sb[xi % 3]
            xi += 1
            p.dma("sp", xb[:, 0:w], sg["xmT"][m * 128:(m + 1) * 128, t0:t0 + w], [sg["xmT"]], [xb])
            Dp = pd[m % 2]
            for i in range(FC):
                p.mm(Dp[:, 0:w], wdn_sb[i][:, m * 128:(m + 1) * 128], gT[i][:, 0:w], i == 0, i == FC - 1,
                     [wdn_sb[i], gT[i]], [Dp])
            o = ot[m % 2]
            p.stt("dve", o[:, 0:w], Dp[:, 0:w], g2_sb[:, sgi, m:m + 1], xb[:, 0:w], ALU.mult, ALU.add,
                  [Dp, g2_sb, xb], [o])
            p.dma("pool", sg["outT"][m * 128:(m + 1) * 128, t0:t0 + w], o[:, 0:w], [o], [])


def build_ffn(T_list, TW=510):
    nc = bass.Bass("TRN2", target_bir_lowering=False)
    p = Prog(nc)
    emit_ffn(p, T_list, TW)
    p.finish()
    return nc


EPS = 1e-6
NIN = 3072
FM_CHUNKS = (
    [("qA", 128 * i, "qkA", i) for i in range(4)]
    + [("kA", 512, "qkA", 0)]
    + [("qB0", 768 + 128 * i, "plain", i) for i in range(2)]
    + [("kB0", 1024 + 128 * i, "fgate", i) for i in range(2)]
    + [("qB1", 1280 + 128 * i, "plain", i) for i in range(2)]
    + [("kB1", 1536 + 128 * i, "fgate", i) for i in range(2)]
    + [("gB", 2048 + 128 * i, "silu", i) for i in range(2)]
    + [("qC", 2304 + 128 * i, "qkC", i) for i in range(2)]
    + [("kC", 2560 + 128 * i, "qkC", i) for i in range(2)]
)
TM_GROUPS = [("vA", 640, 128, "plain"), ("vB", 1792, 256, "plain"), ("vC", 2816, 256, "plain"),
             ("f0", 1024, 256, "f"), ("f1", 1536, 256, "f")]
FM_ROWS = {"qA": 512, "kA": 128, "qB0": 256, "kB0": 256, "qB1": 256, "kB1": 256, "gB": 256, "qC": 256, "kC": 256}


def bcast_rows(t, nrow, ncol, off=0):
    h = t.t
    if isinstance(h, bass.AP):
        return bass.AP(h.tensor, h.offset + off, [[0, nrow], [1, ncol]])
    return bass.AP(h, off, [[0, nrow], [1, ncol]])


def p1_declare(p, sfx, Tn):
    o = {}
    for nm, rows in FM_ROWS.items():
        o[nm] = p.dram(nm + sfx, [rows, Tn], BF16, "ExternalOutput")
    o["vA"] = p.dram("vA" + sfx, [Tn, 128], BF16, "ExternalOutput")
    o["vB"] = p.dram("vB" + sfx, [Tn, 256], BF16, "ExternalOutput")
    o["vC"] = p.dram("vC" + sfx, [Tn, 256], BF16, "ExternalOutput")
    o["LF"] = p.dram("LF" + sfx, [2, Tn, 256], F32, "ExternalOutput")
    o["KB"] = p.dram("KB" + sfx, [2, Tn, 256], BF16, "ExternalOutput")
    return o


def emit_p1(p, TL, TC, extra=None):
    KC = 8
    xT = p.dram("xT", [D, TL], F32, "ExternalInput")
    xcT = p.dram("xcT", [D, TC], F32, "ExternalInput")
    mod = p.dram("mod", [128, 4, KC], F32, "ExternalInput")
    w_in = p.dram("w_in", [D, NIN], F32, "ExternalInput")
    gains = p.dram("gains", [128, 4], F32, "ExternalInput")
    lbT = p.dram("lbT", [128, 4], F32, "ExternalInput")
    lbrow = p.dram("lbrow", [2, 256], F32, "ExternalInput")
    ropeT = p.dram("ropeT", [4, 128, TL], F32, "ExternalInput")
    cmat = p.dram("cmat", [3, 128, 128], F32, "ExternalInput")
    rmat = p.dram("rmat", [2, 128, 128], BF16, "ExternalInput")
    outs_l = p1_declare(p, "", TL)
    outs_c = p1_declare(p, "_c", TC)

    w_sb = [p.sb("w%d" % k, [128, NIN], BF16) for k in range(KC)]
    mod_sb = p.sb("mod", [128, 4, KC], F32)
    gains_sb = p.sb("gains", [128, 4], F32)
    oml_sb = p.sb("oml", [128, 4], F32)
    lb_bc = p.sb("lb_bc", [128, 2, 256], F32)
    oml_bc = p.sb("oml_bc", [128, 2, 256], F32)
    cmat_sb = p.sb("cmat", [128, 3, 128], F32)
    rmat_sb = p.sb("rmat", [128, 2, 128], BF16)
    p.dma("sp", mod_sb[:], mod[:], [mod], [mod_sb])
    p.dma("sp", gains_sb[:], gains[:], [gains], [gains_sb])
    p.dma("sp", oml_sb[:], lbT[:], [lbT], [oml_sb])
    p.dma("sp", cmat_sb[:], cmat[:].rearrange("c p n -> p c n"), [cmat], [cmat_sb])
    p.dma("sp", rmat_sb[:], rmat[:].rearrange("c p n -> p c n"), [rmat], [rmat_sb])
    for d in range(2):
        p.dma("sp", lb_bc[:, d, :], bcast_rows(lbrow, 128, 256, d * 256), [lbrow], [lb_bc])
    p.ts("dve", oml_bc[:], lb_bc[:], -1.0, 1.0, ALU.mult, ALU.add, [lb_bc], [oml_bc])

    stg = [p.sb("stg%d" % i, [128, 1536], F32) for i in range(3)]
    ceng = ["dve", "pool", "act"]
    si = 0
    for k in range(KC):
        for j in range(2):
            s = stg[si % 3]
            p.dma("sp", s[:, :], w_in[k * 128:(k + 1) * 128, j * 1536:(j + 1) * 1536], [w_in], [s])
            p.copy(ceng[si % 3], w_sb[k][:, j * 1536:(j + 1) * 1536], s[:, :], [s], [w_sb[k]])
            si += 1

    TW = 512
    NB = 2
    x_sb = [p.sb("x%d" % i, [128, KC, TW], F32) for i in range(NB)]
    rope_sb = [p.sb("rope%d" % i, [128, 4, TW], F32) for i in range(NB)]
    sq_sb = [p.sb("sq%d" % i, [128, TW], F32) for i in range(2)]
    rstd_sb = p.sb("rstd", [128, TW], F32)
    hx_sb = [p.sb("hx%d" % i, [128, TW], F32) for i in range(2)]
    hT = [p.sb("hT%d" % k, [128, TW], BF16) for k in range(KC)]
    pss = p.ps("pss")
    pfm = [p.ps("pfm%d" % i) for i in range(2)]
    pms = p.ps("pms")
    prot = p.ps("prot")
    ptm = [p.ps("ptm%d" % i) for i in range(2)]
    NS = 3
    sqh = [p.sb("sqh%d" % i, [128, TW], F32) for i in range(NS)]
    rs2 = [p.sb("rs2%d" % i, [128, TW], F32) for i in range(NS)]
    qg = [p.sb("qg%d" % i, [128, TW], BF16) for i in range(NS)]
    t1 = [p.sb("t1%d" % i, [128, TW], F32) for i in range(NS)]
    t2 = [p.sb("t2%d" % i, [128, TW], F32) for i in range(NS)]
    ofm = [p.sb("ofm%d" % i, [128, TW], BF16) for i in range(NS)]
    sg = [p.sb("sg%d" % i, [128, TW], F32) for i in range(NS)]
    otm = [p.sb("otm%d" % i, [128, 256], BF16) for i in range(NS)]
    ftm = [p.sb("ftm%d" % i, [128, 256], F32) for i in range(NS)]
    lftm = [p.sb("lftm%d" % i, [128, 256], F32) for i in range(NS)]
    cnt = {"fm": 0, "tm": 0, "s": 0}
    rtmp = p.sb("rtmp", [128, TW], F32)
    rtmp2 = [p.sb("rtmp2%d" % i, [128, TW], F32) for i in range(NS)]
    p.make_eps(EPS)

    def run(src, Tn, outs, mi, rope):
        tw = min(TW, Tn)
        nt = Tn // tw

        def load(t):
            xb = x_sb[t % NB]
            p.dma("sp", xb[:, :, 0:tw], src[:, t * tw:(t + 1) * tw].rearrange("(k p) n -> p k n", p=128), [src], [xb])
            if rope:
                rb = rope_sb[t % NB]
                p.dma("sp", rb[:, :, 0:tw], ropeT[:, :, t * tw:(t + 1) * tw].rearrange("c p n -> p c n"), [ropeT], [rb])

        load(0)
        for t in range(nt):
            if t + 1 < nt:
                load(t + 1)
            xb = x_sb[t % NB]
            rb = rope_sb[t % NB]
            c0 = t * tw
            for k in range(KC):
                s = sq_sb[k % 2]
                p.act(s[:, 0:tw], xb[:, k, 0:tw], AF.Square, [xb], [s])
                p.mm(pss[:, 0:tw], cmat_sb[:, 0, :], s[:, 0:tw], k == 0, k == KC - 1, [cmat_sb, s], [pss])
            p.rsqrt(rstd_sb[:, 0:tw], pss[:, 0:tw], EPS, rtmp, [pss], [rstd_sb])
            for k in range(KC):
                hx = hx_sb[k % 2]
                p.stt("dve", hx[:, 0:tw], xb[:, k, 0:tw], mod_sb[:, mi, k:k + 1], rstd_sb[:, 0:tw], ALU.mult, ALU.mult,
                      [xb, mod_sb, rstd_sb], [hx])
                p.act(hT[k][:, 0:tw], hx[:, 0:tw], AF.Identity, [hx, mod_sb], [hT[k]], bias=mod_sb[:, mi + 1, k:k + 1])
            def fm_main(nm, col0, kind, ci):
                P = pfm[cnt["fm"] % 2]
                cnt["fm"] += 1
                for k in range(KC):
                    p.mm(P[:, 0:tw], w_sb[k][:, col0:col0 + 128], hT[k][:, 0:tw], k == 0, k == KC - 1, [w_sb[k], hT[k]], [P])
                return P

            def fm_post(P, nm, col0, kind, ci):
                si_ = cnt["s"] % NS
                cnt["s"] += 1
                o = ofm[si_]
                if kind == "plain":
                    p.copy("act", o[:, 0:tw], P[:, 0:tw], [P], [o])
                elif kind == "fgate":
                    dirn = 0 if nm == "kB0" else 1
                    p.act(sg[si_][:, 0:tw], P[:, 0:tw], AF.Sigmoid, [P], [sg[si_]], scale=-1.0)
                    p.ts("pool", o[:, 0:tw], sg[si_][:, 0:tw], oml_sb[:, dirn * 2 + ci:dirn * 2 + ci + 1], None, ALU.mult, None,
                         [sg[si_], oml_sb], [o])
                elif kind == "silu":
                    p.act(sg[si_][:, 0:tw], P[:, 0:tw], AF.Sigmoid, [P], [sg[si_]])
                    p.tt("dve", o[:, 0:tw], P[:, 0:tw], sg[si_][:, 0:tw], ALU.mult, [P, sg[si_]], [o])
                else:
                    isA = kind == "qkA"
                    gi = (0 if nm == "qA" else 1) if isA else (2 if nm == "qC" else 3)
                    gm = 1 if isA else 2
                    p.act(sqh[si_][:, 0:tw], P[:, 0:tw], AF.Square, [P], [sqh[si_]])
                    p.mm(pms[:, 0:tw], cmat_sb[:, gm, :], sqh[si_][:, 0:tw], True, True, [cmat_sb, sqh[si_]], [pms])
                    p.rsqrt(rs2[si_][:, 0:tw], pms[:, 0:tw], EPS, rtmp2[si_], [pms], [rs2[si_]])
                    if not rope:
                        p.stt("dve", o[:, 0:tw], P[:, 0:tw], gains_sb[:, gi:gi + 1], rs2[si_][:, 0:tw], ALU.mult, ALU.mult,
                              [P, gains_sb, rs2[si_]], [o])
                    else:
                        q_ = qg[si_]
                        p.stt("dve", q_[:, 0:tw], P[:, 0:tw], gains_sb[:, gi:gi + 1], rs2[si_][:, 0:tw], ALU.mult, ALU.mult,
                              [P, gains_sb, rs2[si_]], [q_])
                        ri = 0 if isA else 1
                        p.mm(prot[:, 0:tw], rmat_sb[:, ri, :], q_[:, 0:tw], True, True, [rmat_sb, q_], [prot])
                        p.tt("pool", t1[si_][:, 0:tw], q_[:, 0:tw], rb[:, 2 * ri, 0:tw], ALU.mult, [q_, rb], [t1[si_]])
                        p.tt("dve", t2[si_][:, 0:tw], prot[:, 0:tw], rb[:, 2 * ri + 1, 0:tw], ALU.mult, [prot, rb], [t2[si_]])
                        p.tt("pool", o[:, 0:tw], t1[si_][:, 0:tw], t2[si_][:, 0:tw], ALU.add, [t1[si_], t2[si_]], [o])
                p.dma("pool", outs[nm][ci * 128:(ci + 1) * 128, c0:c0 + tw], o[:, 0:tw], [o], [])
            pendfm = None
            for item in FM_CHUNKS:
                P_ = fm_main(*item)
                if pendfm is not None:
                    fm_post(*pendfm)
                pendfm = (P_,) + tuple(item)
            fm_post(*pendfm)
            for s4 in range(tw // 128):
                r0 = c0 + s4 * 128
                for (nm, col0, ncols, kind) in TM_GROUPS:
                    P = ptm[cnt["tm"] % 2]
                    cnt["tm"] += 1
                    for k in range(KC):
                        p.mm(P[:, 0:ncols], hT[k][:, s4 * 128:(s4 + 1) * 128], w_sb[k][:, col0:col0 + ncols], k == 0, k == KC - 1,
                             [w_sb[k], hT[k]], [P])
                    si_ = cnt["s"] % NS
                    cnt["s"] += 1
                    if kind == "plain":
                        o = otm[si_]
                        p.copy("act", o[:, 0:ncols], P[:, 0:ncols], [P], [o])
                        p.dma("pool", outs[nm][r0:r0 + 128, :], o[:, 0:ncols], [o], [])
                    else:
                        dirn = 0 if nm == "f0" else 1
                        f_ = ftm[si_]
                        p.act(f_[:, :], P[:, 0:256], AF.Sigmoid, [P], [f_])
                        p.tt("dve", f_[:, :], f_[:, :], oml_bc[:, dirn, :], ALU.mult, [f_, oml_bc], [f_])
                        p.tt("pool", f_[:, :], f_[:, :], lb_bc[:, dirn, :], ALU.add, [f_, lb_bc], [f_])
                        lf = lftm[si_]
                        p.act(lf[:, :], f_[:, :], AF.Ln, [f_], [lf])
                        o = otm[si_]
                        p.ts("pool", o[:, :], f_[:, :], -1.0, 1.0, ALU.mult, ALU.add, [f_], [o])
                        p.dma("pool", outs["LF"][dirn, r0:r0 + 128, :], lf[:, :], [lf], [])
                        p.dma("pool", outs["KB"][dirn, r0:r0 + 128, :], o[:, :], [o], [])

    run(xT, TL, outs_l, 0, True)
    run(xcT, TC, outs_c, 2, False)


def build_p1(TL, TC):
    nc = bass.Bass("TRN2", target_bir_lowering=False)
    p = Prog(nc)
    emit_p1(p, TL, TC)
    p.finish()
    return nc


ROPE_BASE = 10000.0
GRID_W = 64


def rope_tables(pos):
    pos = np.asarray(pos)
    row = (pos // GRID_W).astype(np.float32)
    col = (pos % GRID_W).astype(np.float32)
    out = []
    for dim in (64, 32):
        axis_dim = dim // 2
        n = axis_dim // 2
        inv = np.power(np.float32(ROPE_BASE), (-np.arange(n, dtype=np.float32) * np.float32(2.0) / np.float32(axis_dim))).astype(np.float32)
        d = np.arange(128) % dim
        half = d // axis_dim
        i = d % n
        ang = np.where(half[:, None] == 0, row[None, :], col[None, :]).astype(np.float32) * inv[i][:, None]
        out.append(np.cos(ang).astype(np.float32))
        out.append(np.sin(ang).astype(np.float32))
    return np.stack(out, 0)


def const_mats():
    cm = np.zeros((3, 128, 128), np.float32)
    cm[0] = 1.0 / 1024
    for g in range(2):
        cm[1, g * 64:(g + 1) * 64, g * 64:(g + 1) * 64] = 1.0 / 64
    for g in range(4):
        cm[2, g * 32:(g + 1) * 32, g * 32:(g + 1) * 32] = 1.0 / 32
    rm = np.zeros((2, 128, 128), np.float32)
    for ri, axis_dim in enumerate((32, 16)):
        hf = axis_dim // 2
        for m in range(128):
            if (m % axis_dim) < hf:
                rm[ri, m + hf, m] = -1.0
            else:
                rm[ri, m - hf, m] = 1.0
    import ml_dtypes
    return cm, rm.astype(ml_dtypes.bfloat16)


def p2_attn(p, L, LC, need_ctx, lam_init, g=None, nat=None):
    NCH = (L + LC) // 128
    NLB = L // 128
    qC = p.dram("qC", [128, L], BF16, "ExternalInput")
    qC_c = p.dram("qC_c", [128, LC], BF16, "ExternalInput")
    if nat is None:
        kC = p.dram("kC", [128, L + LC], BF16, "ExternalInput")
        vC = p.dram("vCp", [128, NCH, 2, 128], BF16, "ExternalInput")
        kA = p.dram("kA", [64, L + LC], BF16, "ExternalInput")
        vA = p.dram("vAp", [128, NCH, 128], BF16, "ExternalInput")
    qA = p.dram("qA", [256, L], BF16, "ExternalInput")
    qA_c = p.dram("qA_c", [256, LC], BF16, "ExternalInput")
    scal = p.dram("scal", [128, 8], F32, "ExternalInput")
    masks = p.dram("masks", [2, 128, 128], BF16, "ExternalInput")
    g64 = p.dram("g64", [64, 64], F32, "ExternalInput")
    oC = p.dram("oC", [128, L], BF16, "ExternalOutput")
    oA = p.dram("oA", [256, L], BF16, "ExternalOutput")
    oC_c = p.dram("oC_c", [128, LC], BF16, "ExternalOutput")
    oA_c = p.dram("oA_c", [256, LC], BF16, "ExternalOutput")
    if 'dbg' in DBG:
        dbg = p.dram("dbg", [2, 4, 64, 512], F32, "ExternalOutput")
        dbg2 = p.dram("dbg2", [2, 2, 128, 512], F32, "ExternalOutput")
        dbgz = [p.sb("dbgz%d" % i, [128, 512], F32) for i in range(2)]

    scal_sb = p.sb("scal", [128, 8], F32)
    sc2 = p.sb("sc2", [128, 8], F32)
    mask_sb = p.sb("mask", [128, 2, 128], BF16)
    g64_sb = p.sb("g64", [64, 64], F32)
    p.dma("sp", scal_sb[:], scal[:], [scal], [scal_sb])
    if nat is not None:
        p.dma("sp", scal_sb[:, 0:1], nat["lamcol"], [scal_sb], [scal_sb], slow=True)
    p.dma("sp", mask_sb[:], masks[:].rearrange("c p n -> p c n"), [masks], [mask_sb])
    p.dma("sp", g64_sb[:], g64[:], [g64], [g64_sb])
    p.ts("dve", sc2[:, 0:1], scal_sb[:, 0:1], -1.0, None, ALU.mult, None, [scal_sb], [sc2])
    p.ts("dve", sc2[:, 1:2], scal_sb[:, 1:2], float(1.0 - lam_init), None, ALU.mult, None, [scal_sb], [sc2])
    p.act(sc2[:, 2:6], scal_sb[:, 2:6], AF.Exp, [scal_sb], [sc2])

    kC_sb = p.sb("kC", [64, 2, L + LC], BF16)
    qC_t = [p.sb("qCt%d" % i, [64, 2, 512], BF16) for i in range(2)]
    vC_sb = p.sb("vC", [128, NCH, 2, 128], BF16)
    kA_sb = p.sb("kA", [64, L + LC], BF16)
    qA_t = [p.sb("qAt%d" % i, [64, 4, 512], BF16) for i in range(2)]
    vA_sb = p.sb("vA", [128, NCH, 128], BF16)
    NSPL = 4
    Ls = (L + LC) // NSPL
    if nat is None:
        for i in range(NSPL):
            for h in range(2):
                p.dma("sp", kC_sb[:, h, i * Ls:(i + 1) * Ls], kC[h * 64:(h + 1) * 64, i * Ls:(i + 1) * Ls], [kC], [kC_sb])
            p.dma("sp", kA_sb[:, i * Ls:(i + 1) * Ls], kA[:, i * Ls:(i + 1) * Ls], [kA], [kA_sb])
    else:
        Lp = L // NSPL
        for h in range(2):
            rows = slice(g * 128 + h * 64, g * 128 + (h + 1) * 64)
            for i in range(NSPL):
                p.dma("sp", kC_sb[:, h, i * Lp:(i + 1) * Lp], nat["kC"][rows, LC + i * Lp:LC + (i + 1) * Lp], [], [])
            p.dma("sp", kC_sb[:, h, L:L + LC], nat["kC"][rows, 0:LC], [], [])
        for i in range(NSPL):
            p.dma("sp", kA_sb[:, i * Lp:(i + 1) * Lp], nat["kA"][g * 64:(g + 1) * 64, LC + i * Lp:LC + (i + 1) * Lp], [], [])
        p.dma("sp", kA_sb[:, L:L + LC], nat["kA"][g * 64:(g + 1) * 64, 0:LC], [], [])
    def load_qC(t):
        buf = qC_t[t % 2]
        for h in range(2):
            p.dma("sp", buf[:, h, :], qC[h * 64:(h + 1) * 64, t * 512:(t + 1) * 512], [qC], [buf])
        return buf

    def load_qA(t):
        buf = qA_t[t % 2]
        for hh in range(4):
            p.dma("sp", buf[:, hh, :], qA[hh * 64:(hh + 1) * 64, t * 512:(t + 1) * 512], [qA], [buf])
        return buf
    CS = NCH // 2
    if nat is None:
        for c0 in range(0, NCH, CS):
            p.dma("sp", vC_sb[:, c0:c0 + CS, :, :], vC[:, c0:c0 + CS, :, :], [vC], [vC_sb])
            p.dma("sp", vA_sb[:, c0:c0 + CS, :], vA[:, c0:c0 + CS, :], [vA], [vA_sb])
    else:
        p.memset("pool", vC_sb[:, :, :, 64:128], 1.0, [vC_sb])
        p.memset("pool", vA_sb[:, :, 64:128], 1.0, [vA_sb])
        for j in range(0, NCH, 2):
            tk = LC + j * 128 if j < NLB else (j - NLB) * 128
            q_ = "sp" if (j // 2) % 2 == 0 else "pool"
            for h in range(2):
                cs = (2 * g + h) * 64
                p.dma(q_, vC_sb[:, j:j + 2, h, 0:64], nat["vC"][tk:tk + 256, cs:cs + 64].rearrange("(c s) v -> s c v", s=128),
                      [], [])
            p.dma(q_, vA_sb[:, j:j + 2, 0:64], nat["vA"][tk:tk + 256, g * 64:(g + 1) * 64].rearrange("(c s) v -> s c v", s=128),
                  [], [])
        p.dma_fence()
    qCc_sb = p.sb("qCc", [64, 2, LC], BF16)
    qAc_sb = p.sb("qAc", [64, 4, LC], BF16)
    if need_ctx:
        for h in range(2):
            p.dma("sp", qCc_sb[:, h, :], qC_c[h * 64:(h + 1) * 64, :], [qC_c], [qCc_sb])
        for hh in range(4):
            p.dma("sp", qAc_sb[:, hh, :], qA_c[hh * 64:(hh + 1) * 64, :], [qA_c], [qAc_sb])

    NPS = 3
    ps = [p.ps("ps%d" % i, (128, 1024)) for i in range(NPS)]
    po = [p.ps("po%d" % i) for i in range(2)]
    NE = 4
    E = [p.sb("E%d" % i, [128, 1024], BF16) for i in range(NE)]
    rz = [p.sb("rz%d" % i, [64, 512], F32) for i in range(2)]
    on = [p.sb("on%d" % i, [64, 512], F32) for i in range(2)]
    o_t = p.sb("o_t", [64, 512], F32)
    sq_t = p.sb("sq_t", [64, 512], F32)
    rs_t = p.sb("rs_t", [64, 512], F32)
    rtmp = p.sb("rtmpa", [64, 512], F32)
    ob = [p.sb("ob%d" % i, [64, 512], BF16) for i in range(2)]
    st = {"e": 0, "ps": 0, "po": 0, "ob": 0, "qm": 0}
    cm_sb = p.sb("cmsel", [64, 2], F32)
    p.memset("pool", cm_sb[:, :], 0.0, [cm_sb])
    p.memset("pool", cm_sb[0:32, 0:1], 1.0, [cm_sb])
    p.memset("pool", cm_sb[32:64, 1:2], 1.0, [cm_sb])
    qm_t = [[p.sb("qm%d_%d" % (i, c), [64, 2, 512], BF16) for c in range(2)] for i in range(2)]
    SC_C = 32 ** -0.5
    SC_A = 64 ** -0.5

    def diff_tile(q_sb, q0, N, chunks, out_dram, o0):
        qm = qm_t[st["qm"] % 2]
        st["qm"] += 1
        for c in range(2):
            p.ts("pool" if c == 0 else "dve", qm[c][:, :, 0:N], q_sb[:, :, q0:q0 + N], cm_sb[:, c:c + 1], None, ALU.mult, None,
                 [q_sb, cm_sb], [qm[c]])
        for h in ((1, 0, 1) if 'swap' in DBG else range(2)):
            POs = [po[c] for c in range(2)]
            for c in range(2):
                r0 = 32 * c
                PO = POs[c]
                npair = len(chunks) // 2

                def pv(pi, e):
                    for u in range(2):
                        j = chunks[2 * pi + u]
                        p.mm(PO[:, 0:N], vC_sb[:, j, h, :], e[:, u * 512:u * 512 + N], pi == 0 and u == 0,
                             pi == npair - 1 and u == 1, [vC_sb, e], [PO])

                pend = []
                for pi in range(npair):
                    PS = ps[st["ps"] % NPS]
                    st["ps"] += 1
                    e = E[st["e"] % NE]
                    st["e"] += 1
                    for u in range(2):
                        j = chunks[2 * pi + u]
                        p.mm(PS[:, u * 512:u * 512 + N], kC_sb[:, h, j * 128:(j + 1) * 128], qm[c][:, h, 0:N],
                             True, True, [kC_sb, qm[c]], [PS])
                    if len(pend) == 2:
                        pv(*pend.pop(0))
                    p.act(e[:, :].rearrange("p (u n) -> p u n", u=2)[:, :, 0:N], PS[:, :].rearrange("p (u n) -> p u n", u=2)[:, :, 0:N],
                          AF.Exp, [PS], [e], scale=SC_C)
                    pend.append((pi, e))
                for pe_ in pend:
                    pv(*pe_)
            for c in range(2):
                p.recip(rz[c][:, 0:N], POs[c][64:128, 0:N], [POs[c]], [rz[c]])
                p.tt("dve", on[c][:, 0:N], POs[c][0:64, 0:N], rz[c][:, 0:N], ALU.mult, [POs[c], rz[c]], [on[c]])
            if 'dbg' in DBG and out_dram is oC_c:
                for c in range(2):
                    p.copy("dve", dbgz[c][:, 0:N], POs[c][:, 0:N], [POs[c]], [dbgz[c]])
                    p.dma("sp", dbg2[h, c, :, 0:N], dbgz[c][:, 0:N], [dbgz[c]], [])
                for c in range(2):
                    p.dma("sp", dbg[h, c, :, 0:N], rz[c][:, 0:N], [rz[c]], [])
                    p.dma("sp", dbg[h, 2 + c, :, 0:N], on[c][:, 0:N], [on[c]], [])
            p.stt("pool", o_t[:, 0:N], on[1][:, 0:N], sc2[0:64, 0:1], on[0][:, 0:N], ALU.mult, ALU.add,
                  [on[0], on[1], sc2], [o_t])
            p.tt("pool", sq_t[:, 0:N], o_t[:, 0:N], o_t[:, 0:N], ALU.mult, [o_t], [sq_t])
            pm = POs[0]
            p.mm(pm[0:64, 0:N], g64_sb[:, :], sq_t[:, 0:N], True, True, [g64_sb, sq_t], [pm])
            p.rsqrt(rs_t[:, 0:N], pm[0:64, 0:N], EPS, rtmp, [pm], [rs_t])
            o_b = ob[st["ob"] % 2]
            st["ob"] += 1
            p.stt("dve", o_b[:, 0:N], o_t[:, 0:N], sc2[0:64, 1:2], rs_t[:, 0:N], ALU.mult, ALU.mult, [o_t, sc2, rs_t], [o_b])
            p.dma("pool", out_dram[h * 64:(h + 1) * 64, o0:o0 + N], o_b[:, 0:N], [o_b], [])

    den = [p.sb("den%d" % i, [64, 512], F32) for i in range(2)]
    sinkL = p.sb("sinkL", [1, 128], BF16)
    esrow = p.sb("esrow", [1, 512], BF16)
    p.memset("pool", sinkL[:, 0:64], 0.0, [sinkL])
    p.memset("pool", sinkL[:, 64:128], 1.0, [sinkL])
    for h in range(4):
        p.copy("dve", esrow[0:1, h * 128:(h + 1) * 128], sc2[0:1, 2 + h:3 + h].to_broadcast([1, 128]), [sc2], [esrow])

    def win_pv(PO, first, last, kc, e, fin):
        p.mm(PO[:, :], vA_sb[:, kc, :], e[:, 0:512], first, False, [vA_sb, e], [PO])
        if not last:
            return
        p.mm(PO[:, :], sinkL[:, :], esrow[:, :], False, True, [sinkL, esrow], [PO])
        out_dram, o0 = fin
        dn = den[st["ob"] % 2]
        p.recip(dn[:, :], PO[64:128, :], [PO], [dn])
        o_b = ob[st["ob"] % 2]
        st["ob"] += 1
        p.tt("dve", o_b[:, :], PO[0:64, :], dn[:, :], ALU.mult, [PO, dn], [o_b])
        p.dma("pool", out_dram[:, o0:o0 + 128].rearrange("(h v) q -> v h q", v=64),
              o_b[:, :].rearrange("v (h q) -> v h q", h=4), [o_b], [])

    def win_blocks(blocks):
        pend = None
        for (q_sb, q0, chunks, out_dram, o0) in blocks:
            PO = po[st["po"] % 2]
            st["po"] += 1
            n = len(chunks)
            for ji, (kc, mt) in enumerate(chunks):
                PS = ps[st["ps"] % NPS]
                st["ps"] += 1
                e = E[st["e"] % NE]
                st["e"] += 1
                for h in range(4):
                    p.mm(PS[:, h * 128:(h + 1) * 128], kA_sb[:, kc * 128:(kc + 1) * 128],
                         q_sb[:, h, q0:q0 + 128], True, True, [kA_sb, q_sb], [PS])
                if pend is not None:
                    win_pv(*pend)
                p.act(e[:, 0:512], PS[:, 0:512], AF.Exp, [PS], [e], scale=SC_A)
                if mt is not None:
                    e3 = e[:, 0:512].rearrange("p (h q) -> p h q", h=4)
                    m3 = mask_sb[:, mt, :].unsqueeze(1).to_broadcast([128, 4, 128])
                    p.tt("dve", e3, e3, m3, ALU.mult, [e, mask_sb], [e])
                pend = (PO, ji == 0, ji == n - 1, kc, e, (out_dram, o0))
        if pend is not None:
            win_pv(*pend)

    ctx_ch = list(range(NLB, NCH))
    all_ch = list(range(NCH))
    return dict(diff_tile=diff_tile, win_blocks=win_blocks, ctx_ch=ctx_ch, all_ch=all_ch, NLB=NLB, NCH=NCH,
                load_qC=load_qC, load_qA=load_qA, qCc_sb=qCc_sb, qAc_sb=qAc_sb, oC=oC, oA=oA, oC_c=oC_c, oA_c=oA_c)


def emit_p2_attn(p, L, LC, need_ctx, lam_init, do_A=True, do_C=True, g=None, nat=None):
    p.make_eps(EPS)
    a = p2_attn(p, L, LC, need_ctx, lam_init, g, nat)
    NLB = a["NLB"]
    blocks = []
    if do_A:
        qbuf = None
        for j in range(NLB):
            if j % 4 == 0:
                qbuf = a["load_qA"](j // 4)
            ch = []
            if j > 0:
                ch.append((j - 1, 0))
            ch.append((j, None))
            if j + 1 < NLB:
                ch.append((j + 1, 1))
            ch += [(c, None) for c in a["ctx_ch"]]
            blocks.append((qbuf, (j % 4) * 128, ch, a["oA"], j * 128))
            if j % 4 == 3 or j == NLB - 1:
                a["win_blocks"](blocks)
                blocks = []
        if need_ctx:
            for j in range(LC // 128):
                blocks.append((a["qAc_sb"], j * 128, [(c, None) for c in a["ctx_ch"]], a["oA_c"], j * 128))
            a["win_blocks"](blocks)
    TQ = 512
    for t in range(L // TQ if (do_C and 'ctxonly' not in DBG) else 0):
        a["diff_tile"](a["load_qC"](t), 0, TQ, a["all_ch"], a["oC"], t * TQ)
    if need_ctx and do_C:
        a["diff_tile"](a["qCc_sb"], 0, LC, a["ctx_ch"], a["oC_c"], 0)


def build_p2_attn(L, LC, need_ctx, lam_init, do_A=True, do_C=True):
    nc = bass.Bass("TRN2", target_bir_lowering=False)
    p = Prog(nc)
    emit_p2_attn(p, L, LC, need_ctx, lam_init, do_A, do_C)
    p.finish()
    return nc


def attn_masks():
    import ml_dtypes
    i = np.arange(128)[:, None]
    j = np.arange(128)[None, :]
    return np.stack([(j <= i), (i <= j)], 0).astype(np.float32).astype(ml_dtypes.bfloat16)


def v_layout(v, heads):
    import ml_dtypes
    S = v.shape[0]
    out = np.ones((128, S // 128, heads, 128), ml_dtypes.bfloat16)
    out[:, :, :, 0:64] = v.reshape(S // 128, 128, heads, 64).transpose(1, 0, 2, 3)
    return out


def p2_hgrn(p, L, LC, need_ctx, g=None, nat=None):
    Tt = LC + L
    CH = 32
    qB = p.dram("qB", [2, 2, 64, Tt], BF16, "ExternalInput")
    kB = p.dram("kB", [2, 2, 64, Tt], BF16, "ExternalInput")
    if nat is None:
        LFb = p.dram("LFb", [2, 2, 128, Tt // 128, 64], F32, "ExternalInput")
        LFc = p.dram("LFc", [2, 2, 32, Tt // 32, 64], F32, "ExternalInput")
        KBc = p.dram("KBc", [2, 2, 32, Tt // 32, 64], BF16, "ExternalInput")
        vBc = p.dram("vBc", [2, 32, Tt // 32, 64], BF16, "ExternalInput")
    gB = p.dram("gBf", [2, 64, Tt], BF16, "ExternalInput")
    ogB = p.dram("ogB", [64, 1], F32, "ExternalInput")
    ucum = p.dram("ucum", [2, 128, 128], F32, "ExternalInput")
    lst = p.dram("lst", [2, 32, 32], F32, "ExternalInput")
    tri = p.dram("tri", [2, 32, 32], F32, "ExternalInput")
    g64 = p.dram("g64b", [64, 64], F32, "ExternalInput")
    oB = p.dram("oB", [128, L], BF16, "ExternalOutput")
    oB_c = p.dram("oB_c", [128, LC], BF16, "ExternalOutput")

    og_sb = p.sb("ogB", [64, 1], F32)
    ucum_sb = p.sb("ucum", [128, 2, 128], F32)
    lst_sb = p.sb("lst", [32, 2, 32], F32)
    tri_sb = p.sb("tri", [32, 2, 32], F32)
    g64_sb = p.sb("g64b", [64, 64], F32)
    p.dma("sp", og_sb[:], ogB[:], [ogB], [og_sb])
    p.dma("sp", ucum_sb[:], ucum[:].rearrange("c p n -> p c n"), [ucum], [ucum_sb])
    p.dma("sp", lst_sb[:], lst[:].rearrange("c p n -> p c n"), [lst], [lst_sb])
    p.dma("sp", tri_sb[:], tri[:].rearrange("c p n -> p c n"), [tri], [tri_sb])
    p.dma("sp", g64_sb[:], g64[:], [g64], [g64_sb])

    o_acc = [p.sb("oacc%d" % h, [64, Tt], F32) for h in range(2)]
    NSC = 4
    S = [p.sb("S%d" % i, [64, 64], F32) for i in range(NSC)]
    S_bf = [p.sb("Sbf%d" % i, [64, 64], BF16) for i in range(NSC)]
    for i in range(NSC):
        p.memset("pool", S[i][:, :], 0.0, [S[i]])
        p.memset("pool", S_bf[i][:, :], 0.0, [S_bf[i]])
    NPB = 2
    def mk(nm, shape, dt):
        return [[p.sb("%s%d_%d" % (nm, i, j), shape, dt) for j in range(NPB)] for i in range(NSC)]
    qe = mk("qe", [64, 512], BF16)
    ke = mk("ke", [64, 512], BF16)
    qb = mk("qb", [64, 512], BF16)
    kend = mk("kend", [32, 1024], BF16)
    ebend = mk("ebend", [64, 16], F32)
    Vc = mk("Vc", [32, 1024], BF16)
    ATm = mk("ATm", [32, 512], BF16)
    NT_ = 4
    LFt_sb = [p.sb("LFt%d" % i, [128, 4, 64], F32) for i in range(NT_)]
    LFc_sb = [p.sb("LFc%d" % i, [32, 1024], F32) for i in range(NT_)]
    Kc_sb = [p.sb("Kc%d" % i, [32, 1024], BF16) for i in range(NT_)]
    qT_sb = [p.sb("qTb%d" % i, [64, 512], BF16) for i in range(NT_)]
    kT_sb = [p.sb("kTb%d" % i, [64, 512], BF16) for i in range(NT_)]
    bT_sb = [p.sb("bT%d" % i, [64, 512], F32) for i in range(2)]
    d1_sb = [p.sb("d1%d" % i, [64, 512], F32) for i in range(2)]
    E_sb = [p.sb("Eb%d" % i, [64, 512], F32) for i in range(3)]
    ec_sb = [p.sb("ec%d" % i, [32, 512], F32) for i in range(2)]
    gT_sb = [p.sb("gTb%d" % i, [64, 512], BF16) for i in range(2)]
    o_t = [p.sb("otb%d" % i, [64, 512], F32) for i in range(2)]
    sq_t = p.sb("sqb", [64, 512], F32)
    rs_t = p.sb("rsb", [64, 512], F32)
    rtmp = p.sb("rtmpb", [64, 512], F32)
    on_t = p.sb("onb", [64, 512], F32)
    ob_t = [p.sb("obb%d" % i, [64, 512], BF16) for i in range(2)]
    PB = p.ps("PB")
    PC = PB
    PA = p.ps("PA")
    PO = [p.ps("PO%d" % i) for i in range(NSC)]
    PD = [p.ps("PD%d" % i) for i in range(2)]
    for tv in PD:
        tv.view = tv[0:64, 0:64]
    st = {"t": 0, "e": 0, "ec": 0, "pd": 0, "g": 0, "ot": 0, "ob": 0}
    seen = {}

    nlat = L // 512
    sbs_lat = [("lat", j, LC + 512 * j, 512) for j in range(nlat)]
    ctxsb = ("ctx", 0, 0, LC)
    order = [[ctxsb] + sbs_lat, [ctxsb] + sbs_lat[::-1]]

    def prep(sc, d, h, sb, par):
        _, _, t0, n = sb
        nch = n // CH
        nb = n // 128
        ti = st["t"] % NT_
        st["t"] += 1
        lft, lfc, kc, qt, kt, bt, d1 = LFt_sb[ti], LFc_sb[ti], Kc_sb[ti], qT_sb[ti], kT_sb[ti], bT_sb[ti % 2], d1_sb[ti % 2]
        hs = slice(h * 64, (h + 1) * 64)
        b0 = t0 // 128
        c0 = t0 // CH
        if nat is None:
            p.dma("sp", lft[:, 0:nb, :], LFb[d, h, :, b0:b0 + nb, :], [LFb], [lft])
            p.dma("sp", lfc[:, 0:nch * 64].rearrange("s (c k) -> s c k", k=64), LFc[d, h, :, c0:c0 + nch, :], [LFc], [lfc])
            p.dma("sp", kc[:, 0:nch * 64].rearrange("s (c k) -> s c k", k=64), KBc[d, h, :, c0:c0 + nch, :], [KBc], [kc])
        else:
            ncs = slice((2 * g + h) * 64, (2 * g + h + 1) * 64)
            p.dma("sp", lft[:, 0:nb, :], nat["LF"][d, t0:t0 + n, ncs].rearrange("(b t) k -> t b k", t=128), [], [lft])
            p.dma("sp", lfc[:, 0:nch * 64].rearrange("s (c k) -> s c k", k=64),
                  nat["LF"][d, t0:t0 + n, ncs].rearrange("(c s) k -> s c k", s=CH), [], [lfc])
            p.dma("sp", kc[:, 0:nch * 64].rearrange("s (c k) -> s c k", k=64),
                  nat["KB"][d, t0:t0 + n, ncs].rearrange("(c s) k -> s c k", s=CH), [], [kc])
        p.dma("sp", qt[:, 0:n], qB[d, h, :, t0:t0 + n], [qB], [qt])
        p.dma("sp", kt[:, 0:n], kB[d, h, :, t0:t0 + n], [kB], [kt])
        vc = Vc[sc][par]
        if nat is None:
            p.dma("sp", vc[:, 0:nch * 64].rearrange("s (c k) -> s c k", k=64), vBc[h, :, c0:c0 + nch, :], [vBc], [vc])
        else:
            p.dma("sp", vc[:, 0:nch * 64].rearrange("s (c k) -> s c k", k=64),
                  nat["vB"][t0:t0 + n, ncs].rearrange("(c s) k -> s c k", s=CH), [], [vc])
        yield
        for b in range(nb):
            p.mm(PB[0:64, b * 128:(b + 1) * 128], lft[:, b, :], ucum_sb[:, d, :], True, True, [lft, ucum_sb], [PB])
        p.copy("act", bt[:, 0:n], PB[0:64, 0:n], [PB], [bt])
        yield
        bt3 = bt[:, 0:n].rearrange("p (c s) -> p c s", s=CH)
        d13 = d1[:, 0:n].rearrange("p (c s) -> p c s", s=CH)
        p.tt("dve", d13, bt3, bt3[:, :, 16:17].to_broadcast([64, nch, CH]), ALU.subtract, [bt], [d1])
        yield
        e1 = E_sb[st["e"] % 3]; st["e"] += 1
        p.act(e1[:, 0:n], d1[:, 0:n], AF.Exp, [d1], [e1])
        p.tt("pool", qe[sc][par][:, 0:n], qt[:, 0:n], e1[:, 0:n], ALU.mult, [qt, e1], [qe[sc][par]])
        yield
        e2 = E_sb[st["e"] % 3]; st["e"] += 1
        p.act(e2[:, 0:n], d1[:, 0:n], AF.Exp, [d1], [e2], scale=-1.0)
        p.tt("dve", ke[sc][par][:, 0:n], kt[:, 0:n], e2[:, 0:n], ALU.mult, [kt, e2], [ke[sc][par]])
        yield
        e3 = E_sb[st["e"] % 3]; st["e"] += 1
        p.act(e3[:, 0:n], bt[:, 0:n], AF.Exp, [bt], [e3])
        p.tt("pool", qb[sc][par][:, 0:n], qt[:, 0:n], e3[:, 0:n], ALU.mult, [qt, e3], [qb[sc][par]])
        eidx = CH - 1 if d == 0 else 0
        p.act(ebend[sc][par][:, 0:nch], bt3[:, :, eidx], AF.Exp, [bt], [ebend[sc][par]])
        yield
        for hf in range(nch * 64 // 512):
            p.mm(PC[0:32, :], lst_sb[:, d, :], lfc[:, hf * 512:(hf + 1) * 512], True, True, [lst_sb, lfc], [PC])
            ec = ec_sb[st["ec"] % 2]; st["ec"] += 1
            p.act(ec[:, :], PC[0:32, :], AF.Exp, [PC], [ec])
            p.tt("dve", kend[sc][par][:, hf * 512:(hf + 1) * 512], kc[:, hf * 512:(hf + 1) * 512], ec[:, :], ALU.mult,
                 [kc, ec], [kend[sc][par]])
            yield
        for c in range(nch):
            p.mm(PA[0:32, c * CH:(c + 1) * CH], ke[sc][par][:, c * CH:(c + 1) * CH], qe[sc][par][:, c * CH:(c + 1) * CH], True, True,
                 [ke[sc][par], qe[sc][par]], [PA])
        p.tt("dve", ATm[sc][par][:, 0:n].rearrange("p (c s) -> p c s", s=CH), PA[0:32, 0:n].rearrange("p (c s) -> p c s", s=CH),
             tri_sb[:, d, :].unsqueeze(1).to_broadcast([32, nch, CH]), ALU.mult, [PA, tri_sb], [ATm[sc][par]])

    def chunk(sc, par, c):
        vc = Vc[sc][par]
        p.mm(PO[sc][0:64, c * CH:(c + 1) * CH], vc[:, c * 64:(c + 1) * 64], ATm[sc][par][:, c * CH:(c + 1) * CH], True, False,
             [vc, ATm[sc][par]], [PO[sc]])
        p.mm(PO[sc][0:64, c * CH:(c + 1) * CH], S_bf[sc][:, :], qb[sc][par][:, c * CH:(c + 1) * CH], False, True,
             [S_bf[sc], qb[sc][par]], [PO[sc]])
        if 'c1' in DBG:
            return
        pd = PD[st["pd"] % 2]; st["pd"] += 1
        p.mm(pd.view, kend[sc][par][:, c * 64:(c + 1) * 64], vc[:, c * 64:(c + 1) * 64], True, True, [kend[sc][par], vc], [pd])
        if 'c2' in DBG:
            return
        p.stt("dve", S[sc][:, :], S[sc][:, :], ebend[sc][par][:, c:c + 1], pd.view, ALU.mult, ALU.add,
              [S[sc], ebend[sc][par], pd], [S[sc]])
        if 'c3' in DBG:
            return
        p.copy("act", S_bf[sc][:, :], S[sc][:, :], [S[sc]], [S_bf[sc]])

    def finalize(sc, h, sb):
        kind, j, t0, n = sb
        if 'nofin' in DBG:
            return
        key = (h, kind, j)
        acc = o_acc[h]
        if key not in seen:
            seen[key] = 1
            p.copy("act", acc[:, t0:t0 + n], PO[sc][0:64, 0:n], [PO[sc]], [acc])
            return
        if kind == "ctx" and not need_ctx:
            return
        o = o_t[st["ot"] % 2]; st["ot"] += 1
        p.tt("dve", o[:, 0:n], PO[sc][0:64, 0:n], acc[:, t0:t0 + n], ALU.add, [PO[sc], acc], [o])
        gt = gT_sb[st["g"] % 2]; st["g"] += 1
        p.dma("sp", gt[:, 0:n], gB[h, :, t0:t0 + n], [gB], [gt])
        p.tt("pool", sq_t[:, 0:n], o[:, 0:n], o[:, 0:n], ALU.mult, [o], [sq_t])
        p.mm(PB[0:64, 0:n], g64_sb[:, :], sq_t[:, 0:n], True, True, [g64_sb, sq_t], [PB])
        p.rsqrt(rs_t[:, 0:n], PB[0:64, 0:n], EPS, rtmp, [PB], [rs_t])
        p.stt("dve", on_t[:, 0:n], o[:, 0:n], og_sb[:, 0:1], rs_t[:, 0:n], ALU.mult, ALU.mult, [o, og_sb, rs_t], [on_t])
        ob = ob_t[st["ob"] % 2]; st["ob"] += 1
        p.tt("pool", ob[:, 0:n], on_t[:, 0:n], gt[:, 0:n], ALU.mult, [on_t, gt], [ob])
        if kind == "ctx":
            p.dma("pool", oB_c[h * 64:(h + 1) * 64, 0:n], ob[:, 0:n], [ob], [])
        else:
            p.dma("pool", oB[h * 64:(h + 1) * 64, t0 - LC:t0 - LC + n], ob[:, 0:n], [ob], [])

    def run():
        nsteps = 1 + nlat

        def make(step):
            par = step % NPB
            scans, gens = [], []
            for d in range(2):
                sb = order[d][step]
                for h in range(2):
                    sc = d * 2 + h
                    scans.append((sc, d, h, sb))
                    gens.append(prep(sc, d, h, sb, par))
            return scans, gens

        def advance(gens, k):
            for _ in range(k):
                while gens:
                    try:
                        next(gens[0])
                        break
                    except StopIteration:
                        gens.pop(0)

        scans, gens = make(0)
        for g_ in gens:
            next(g_)
        advance(gens, 10 ** 6)
        for step in range(nsteps):
            par = step % NPB
            nxt = None
            if step + 1 < nsteps:
                nxt = make(step + 1)
                for g_ in nxt[1]:
                    next(g_)
            nch = scans[0][3][3] // CH
            per = (4 * 11 + nch - 1) // nch
            for ci in range(nch):
                for (sc, d, h, sb) in scans:
                    chunk(sc, par, ci if d == 0 else nch - 1 - ci)
                if nxt is not None:
                    advance(nxt[1], per)
            if nxt is not None:
                advance(nxt[1], 10 ** 6)
            for (sc, d, h, sb) in scans:
                finalize(sc, h, sb)
            if nxt is not None:
                scans = nxt[0]

    return run


def emit_p2_hgrn(p, L, LC, need_ctx, g=None, nat=None):
    p.make_eps(EPS)
    run = p2_hgrn(p, L, LC, need_ctx, g, nat)
    run()


def build_p2_hgrn(L, LC, need_ctx):
    nc = bass.Bass("TRN2", target_bir_lowering=False)
    p = Prog(nc)
    emit_p2_hgrn(p, L, LC, need_ctx)
    p.finish()
    return nc


def hgrn_consts():
    t = np.arange(128)
    same = (t[:, None] // 32) == (t[None, :] // 32)
    ucum = np.stack([same & (t[:, None] <= t[None, :]), same & (t[:, None] >= t[None, :])], 0).astype(np.float32)
    s = np.arange(32)
    lst = np.stack([s[:, None] > s[None, :], s[:, None] < s[None, :]], 0).astype(np.float32)
    tri = np.stack([s[:, None] <= s[None, :], s[:, None] >= s[None, :]], 0).astype(np.float32)
    return ucum, lst, tri


def hgrn_layouts(LF, KB, vB):
    Tt = LF.shape[1]
    LFb = np.ascontiguousarray(LF.reshape(2, Tt // 128, 128, 2, 64).transpose(0, 3, 2, 1, 4))
    LFc = np.ascontiguousarray(LF.reshape(2, Tt // 32, 32, 2, 64).transpose(0, 3, 2, 1, 4))
    KBc = np.ascontiguousarray(KB.reshape(2, Tt // 32, 32, 2, 64).transpose(0, 3, 2, 1, 4))
    vBc = np.ascontiguousarray(vB.reshape(Tt // 32, 32, 2, 64).transpose(2, 1, 0, 3))
    return LFb, LFc, KBc, vBc


def emit_p3a(p, T_list):
    p.make_eps(EPS)
    KC = 8
    w_out = p.dram("w_out", [D, D], F32, "ExternalInput")
    mod3 = p.dram("mod3", [128, 6, KC], F32, "ExternalInput")
    cones = p.dram("cones", [128, 128], F32, "ExternalInput")
    segs = []
    for i, Tn in enumerate(T_list):
        segs.append(dict(
            T=Tn,
            oT=p.dram("oT%d" % i, [D, Tn], BF16, "ExternalInput"),
            xT=p.dram("xT%d" % i, [D, Tn], F32, "ExternalInput"),
            xmT=p.dram("xmT%d" % i, [D, Tn], F32, "ExternalOutput"),
            h2T=p.dram("h2T%d" % i, [D, Tn], BF16, "ExternalOutput")))
    w_sb = [p.sb("wo%d" % k, [128, D], BF16) for k in range(KC)]
    mod_sb = p.sb("mod3", [128, 6, KC], F32)
    ones_sb = p.sb("cones", [128, 128], F32)
    p.dma("sp", mod_sb[:], mod3[:], [mod3], [mod_sb])
    p.dma("sp", ones_sb[:], cones[:], [cones], [ones_sb])
    stg = [p.sb("stg%d" % i, [128, D], F32) for i in range(3)]
    ceng = ["dve", "pool", "act"]
    for k in range(KC):
        s = stg[k % 3]
        p.dma("sp", s[:, :], w_out[k * 128:(k + 1) * 128, :], [w_out], [s])
        p.copy(ceng[k % 3], w_sb[k][:, :], s[:, :], [s], [w_sb[k]])
    TW = 512
    NB = 2
    o_sb = [p.sb("o%d" % i, [128, KC, TW], BF16) for i in range(NB)]
    x_sb = [p.sb("x%d" % i, [128, KC, TW], F32) for i in range(NB)]
    xm_sb = [[p.sb("xm%d_%d" % (i, k), [128, TW], F32) for k in range(KC)] for i in range(NB)]
    sq_sb = [p.sb("sq%d" % i, [128, TW], F32) for i in range(2)]
    rstd_sb = p.sb("rstd", [128, TW], F32)
    rtmp = p.sb("rtmp", [128, TW], F32)
    hx_sb = [p.sb("hx%d" % i, [128, TW], F32) for i in range(2)]
    h_sb = [p.sb("h%d" % i, [128, TW], BF16) for i in range(3)]
    pp = [p.ps("pp%d" % i) for i in range(2)]
    pss = p.ps("pss")
    cc = {"t": 0, "h": 0}
    for si, sg in enumerate(segs):
        Tn = sg["T"]
        tw = min(TW, Tn)
        for t in range(Tn // tw):
            bi = cc["t"] % NB
            cc["t"] += 1
            ob, xb, xm = o_sb[bi], x_sb[bi], xm_sb[bi]
            c0 = t * tw
            p.dma("sp", ob[:, :, 0:tw], sg["oT"][:, c0:c0 + tw].rearrange("(k p) n -> p k n", p=128), [sg["oT"]], [ob])
            p.dma("sp", xb[:, :, 0:tw], sg["xT"][:, c0:c0 + tw].rearrange("(k p) n -> p k n", p=128), [sg["xT"]], [xb])
            for m in range(KC):
                P = pp[m % 2]
                for k in range(KC):
                    p.mm(P[:, 0:tw], w_sb[k][:, m * 128:(m + 1) * 128], ob[:, k, 0:tw], k == 0, k == KC - 1, [w_sb[k], ob], [P])
                p.stt("dve", xm[m][:, 0:tw], P[:, 0:tw], mod_sb[:, 3 * si, m:m + 1], xb[:, m, 0:tw], ALU.mult, ALU.add,
                      [P, mod_sb, xb], [xm[m]])
                p.dma("pool", sg["xmT"][m * 128:(m + 1) * 128, c0:c0 + tw], xm[m][:, 0:tw], [xm[m]], [])
            for k in range(KC):
                s = sq_sb[k % 2]
                p.act(s[:, 0:tw], xm[k][:, 0:tw], AF.Square, [xm[k]], [s])
                p.mm(pss[:, 0:tw], ones_sb[:, :], s[:, 0:tw], k == 0, k == KC - 1, [ones_sb, s], [pss])
            p.rsqrt(rstd_sb[:, 0:tw], pss[:, 0:tw], EPS, rtmp, [pss], [rstd_sb])
            for k in range(KC):
                hx = hx_sb[k % 2]
                p.stt("dve", hx[:, 0:tw], xm[k][:, 0:tw], mod_sb[:, 3 * si + 1, k:k + 1], rstd_sb[:, 0:tw], ALU.mult, ALU.mult,
                      [xm[k], mod_sb, rstd_sb], [hx])
                h = h_sb[cc["h"] % 3]
                cc["h"] += 1
                p.act(h[:, 0:tw], hx[:, 0:tw], AF.Identity, [hx, mod_sb], [h], bias=mod_sb[:, 3 * si + 2, k:k + 1])
                p.dma("pool", sg["h2T"][k * 128:(k + 1) * 128, c0:c0 + tw], h[:, 0:tw], [h], [])


def build_p3a(T_list):
    nc = bass.Bass("TRN2", target_bir_lowering=False)
    p = Prog(nc)
    emit_p3a(p, T_list)
    p.finish()
    return nc


LAM_INIT = [0.8 - 0.6 * float(np.exp(-0.3 * l)) for l in range(2)]


def emit_p0(p):
    KC = 8
    NJ = 48
    cvec = p.dram("cvec", [128, KC, 2], F32, "ExternalInput")
    w_mod = p.dram("w_mod", [2, D, 6 * D], F32, "ExternalInput")
    b_mod = p.dram("b_modT", [2, 128, NJ], F32, "ExternalInput")
    ng = p.dram("ngT", [2, 128, 2, KC], F32, "ExternalInput")
    hgl = p.dram("hglT", [128, 2, 4], F32, "ExternalInput")
    hgr = p.dram("hglR", [1, 2, 512], F32, "ExternalInput")
    dlam = p.dram("dlam", [1, 2, 4, 32], F32, "ExternalInput")
    modall = p.dram("modall", [2, 128, 12, KC], F32, "ExternalOutput")
    oml_o = p.dram("oml", [128, 2, 4], F32, "ExternalOutput")
    lbrow_o = p.dram("lbrow", [1, 2, 512], F32, "ExternalOutput")
    lam_o = p.dram("lamb", [128, 2], F32, "ExternalOutput")

    c_sb = p.sb("c", [128, KC, 2], F32)
    sg_sb = p.sb("sg", [128, KC, 2], F32)
    sc_sb = p.sb("sc", [128, KC, 2], F32)
    p.dma("sp", c_sb[:], cvec[:], [cvec], [c_sb])
    p.act(sg_sb[:], c_sb[:], AF.Sigmoid, [c_sb], [sg_sb])
    p.tt("dve", sc_sb[:], c_sb[:], sg_sb[:], ALU.mult, [c_sb, sg_sb], [sc_sb])
    bm_sb = p.sb("bm", [128, 2, NJ], F32)
    ng_sb = p.sb("ng", [128, 2, 2, KC], F32)
    p.dma("sp", bm_sb[:], b_mod[:].rearrange("l p j -> p l j"), [b_mod], [bm_sb])
    p.dma("sp", ng_sb[:], ng[:].rearrange("l p a k -> p l a k"), [ng], [ng_sb])
    wst = [p.sb("wst%d" % i, [128, KC, 512], F32) for i in range(2)]
    pm_ = [p.ps("pm%d" % i) for i in range(2)]
    raw = p.sb("raw", [128, 2, NJ, 2], F32)
    wi = 0
    for l in range(2):
        for piece in range(12):
            w = wst[wi % 2]
            wi += 1
            for k in range(KC):
                p.dma("sp", w[:, k, :], w_mod[l, k * 128:(k + 1) * 128, piece * 512:(piece + 1) * 512], [w_mod], [w])
            for jj in range(4):
                j = piece * 4 + jj
                P = pm_[j % 2]
                for k in range(KC):
                    p.mm(P[:, 0:2], w[:, k, jj * 128:(jj + 1) * 128], sc_sb[:, k, :], k == 0, k == KC - 1, [w, sc_sb], [P])
                p.ts("dve", raw[:, l, j, :], P[:, 0:2], bm_sb[:, l, j:j + 1], None, ALU.add, None, [P, bm_sb], [raw])
    out_sb = p.sb("outm", [128, 2, 12, KC], F32)
    tmp = p.sb("tmpm", [128, KC], F32)
    for l in range(2):
        for v in range(2):
            def grp(g):
                return raw[:, l, g * 8:(g + 1) * 8, v]
            base = 0 if v == 0 else 2
            p.ts("dve", tmp[:, :], grp(1), 1.0, None, ALU.add, None, [raw], [tmp])
            p.tt("dve", out_sb[:, l, base + 0, :], tmp[:, :], ng_sb[:, l, 0, :], ALU.mult, [tmp, ng_sb], [out_sb])
            p.copy("dve", out_sb[:, l, base + 1, :], grp(0), [raw], [out_sb])
            b3 = 4 if v == 0 else 7
            p.copy("dve", out_sb[:, l, b3 + 0, :], grp(2), [raw], [out_sb])
            p.ts("dve", tmp[:, :], grp(4), 1.0, None, ALU.add, None, [raw], [tmp])
            p.tt("dve", out_sb[:, l, b3 + 1, :], tmp[:, :], ng_sb[:, l, 1, :], ALU.mult, [tmp, ng_sb], [out_sb])
            p.copy("dve", out_sb[:, l, b3 + 2, :], grp(3), [raw], [out_sb])
            p.copy("dve", out_sb[:, l, 10 + v, :], grp(5), [raw], [out_sb])
        p.dma("sp", modall[l], out_sb[:, l, :, :], [out_sb], [])

    def lower(src_ap, shape, np_, nm):
        r = p.sb(nm + "r", shape, F32)
        p.dma("sp", r[:], src_ap, [], [r])
        mx = p.sb(nm + "mx", [shape[0], shape[2]], F32)
        e = p.sb(nm + "e", shape, F32)
        den = p.sb(nm + "den", [shape[0], shape[2]], F32)
        pr = p.sb(nm + "p", shape, F32)
        lbt = p.sb(nm + "lb", shape, F32)
        p.tt("dve", mx[:, :], r[:, 0, :], r[:, 1, :], ALU.max, [r], [mx])
        for l in range(2):
            p.tt("dve", e[:, l, :], r[:, l, :], mx[:, :], ALU.subtract, [r, mx], [e])
        p.act(e[:], e[:], AF.Exp, [e], [e])
        p.tt("dve", den[:, :], e[:, 0, :], e[:, 1, :], ALU.add, [e], [den])
        p.recip(den[:, :], den[:, :], [den], [den])
        for l in range(2):
            p.tt("dve", pr[:, l, :], e[:, l, :], den[:, :], ALU.mult, [e, den], [pr])
        p.tt("dve", lbt[:, 0, :], pr[:, 0, :], pr[:, 0, :], ALU.subtract, [pr], [lbt])
        p.tt("dve", lbt[:, 1, :], pr[:, 0, :], pr[:, 1, :], ALU.add, [pr], [lbt])
        p.tt("dve", lbt[:, 1, :], lbt[:, 1, :], pr[:, 0, :], ALU.subtract, [lbt, pr], [lbt])
        return lbt
    lbT = lower(hgl[:], [128, 2, 4], 128, "lp")
    oml_sb = p.sb("omlo", [128, 2, 4], F32)
    p.ts("dve", oml_sb[:], lbT[:], -1.0, 1.0, ALU.mult, ALU.add, [lbT], [oml_sb])
    p.dma("sp", oml_o[:], oml_sb[:], [oml_sb], [])
    lbR = lower(hgr[:], [1, 2, 512], 1, "lr")
    p.dma("sp", lbrow_o[:], lbR[:], [lbR], [])

    dl = p.sb("dl", [1, 2, 4, 32], F32)
    p.dma("sp", dl[:], dlam[:], [dlam], [dl])
    pr2 = p.sb("pr2", [1, 2, 2, 32], F32)
    for l in range(2):
        for a in range(2):
            p.tt("dve", pr2[:, l, a, :], dl[:, l, 2 * a, :], dl[:, l, 2 * a + 1, :], ALU.mult, [dl], [pr2])
    ssum = p.sb("ssum", [1, 4], F32)
    p.op("dve", lambda E: E.reduce_sum(ssum[:, :], pr2[:].rearrange("o l a d -> o (l a) d"), AX.X), [pr2], [ssum])
    p.act(ssum[:, :], ssum[:, :], AF.Exp, [ssum], [ssum])
    lam1 = p.sb("lam1", [1, 2], F32)
    for l in range(2):
        p.tt("dve", lam1[:, l:l + 1], ssum[:, 2 * l:2 * l + 1], ssum[:, 2 * l + 1:2 * l + 2], ALU.subtract, [ssum], [lam1])
        p.ts("dve", lam1[:, l:l + 1], lam1[:, l:l + 1], float(LAM_INIT[l]), None, ALU.add, None, [lam1], [lam1])
    ones1 = p.sb("ones1", [1, 128], F32)
    p.memset("pool", ones1[:, :], 1.0, [ones1])
    P = pm_[0]
    p.mm(P[:, 0:2], ones1[:, :], lam1[:, :], True, True, [ones1, lam1], [P])
    lam_sb = p.sb("lamsb", [128, 2], F32)
    p.copy("dve", lam_sb[:, :], P[:, 0:2], [P], [lam_sb])
    p.dma("sp", lam_o[:], lam_sb[:], [lam_sb], [])


def build_p0():
    nc = bass.Bass("TRN2", target_bir_lowering=False)
    p = Prog(nc)
    emit_p0(p)
    p.finish()
    return nc


_CACHE = {}


def _prog(name, fn, *args):
    key = (name,) + tuple(str(a) for a in args)
    if key not in _CACHE:
        _CACHE[key] = fn(*args)
    return _CACHE[key]


def _run(nc, in_maps):
    res = run_bass_kernel_spmd(nc, in_maps, core_ids=list(range(NCORES)))
    return res.results


def _pp(v):
    return np.ascontiguousarray(np.asarray(v).reshape(8, 128).T)


def kernel_unfused(x, c, ctx, c_ctx, w_mod, b_mod, norm1_g, norm2_g, w_in, win_qnorm_g, win_knorm_g, win_sink,
           hg_lower, hg_onorm_g, diff_qnorm_g, diff_knorm_g, diff_lambda, diff_onorm_g, w_out,
           w_up, conv_w, conv_b, w_down):
    f32 = np.float32
    A = lambda a: np.ascontiguousarray(np.asarray(a, dtype=f32))
    x, c, ctx, c_ctx = A(x), A(c), A(ctx), A(c_ctx)
    w_mod, b_mod, norm1_g, norm2_g, w_in = A(w_mod), A(b_mod), A(norm1_g), A(norm2_g), A(w_in)
    win_qnorm_g, win_knorm_g, win_sink, hg_lower, hg_onorm_g = A(win_qnorm_g), A(win_knorm_g), A(win_sink), A(hg_lower), A(hg_onorm_g)
    diff_qnorm_g, diff_knorm_g, diff_lambda, diff_onorm_g = A(diff_qnorm_g), A(diff_knorm_g), A(diff_lambda), A(diff_onorm_g)
    w_out, w_up, conv_w, conv_b, w_down = A(w_out), A(w_up), A(conv_w), A(conv_b), A(w_down)
    B, L, _ = x.shape
    LC = ctx.shape[1]
    TL = L // 2
    DEPTH = w_in.shape[0]
    cores = [(b, r) for b in range(B) for r in range(2)]
    cat = np.concatenate
    C_ = np.ascontiguousarray

    nc0 = _prog("p0", build_p0)
    b_modT = C_(b_mod.reshape(DEPTH, 48, 128).transpose(0, 2, 1))
    ngT = C_(np.stack([np.stack([_pp(norm1_g[l]), _pp(norm2_g[l])], 1) for l in range(DEPTH)], 0))
    hglT = C_(hg_lower.reshape(DEPTH, 4, 128).transpose(2, 0, 1))
    hglR = C_(hg_lower.reshape(1, DEPTH, 512))
    dl = C_(diff_lambda.reshape(1, DEPTH, 4, 32))
    ims = []
    for (b, r) in cores:
        ims.append(dict(cvec=C_(np.stack([_pp(c[b]), _pp(c_ctx)], -1)), w_mod=w_mod, b_modT=b_modT, ngT=ngT,
                        hglT=hglT, hglR=hglR, dlam=dl))
    r0 = _run(nc0, ims)
    modall = [r0[i]["modall"] for i in range(NCORES)]
    oml = r0[0]["oml"]
    lbrow = r0[0]["lbrow"]
    lamb = r0[0]["lamb"]

    xT = [C_(x[b, r * TL:(r + 1) * TL].T) for (b, r) in cores]
    xcT = [C_(ctx[b].T) for b in range(B)]
    cm, rm = const_mats()
    ropes = [rope_tables(r * TL + np.arange(TL)) for r in range(2)]
    masks = attn_masks()
    ucum, lst, tri = hgrn_consts()
    g64 = np.full((64, 64), 1.0 / 64, f32)

    for l in range(DEPTH):
        need_ctx = l < DEPTH - 1
        nc1 = _prog("p1", build_p1, TL, LC)
        gains = C_(np.stack([np.tile(win_qnorm_g[l], 2), np.tile(win_knorm_g[l], 2),
                             np.tile(diff_qnorm_g[l], 4), np.tile(diff_knorm_g[l], 4)], 1))
        ims = []
        for i, (b, r) in enumerate(cores):
            ims.append(dict(xT=xT[i], xcT=xcT[b], mod=C_(modall[i][l][:, 0:4, :]), w_in=w_in[l], gains=gains,
                            lbT=C_(oml[:, l, :]), lbrow=C_(lbrow[0, l].reshape(2, 256)), ropeT=ropes[r], cmat=cm, rmat=rm))
        r1 = _run(nc1, ims)

        def full(b, nm, axis):
            return cat([r1[2 * b][nm], r1[2 * b + 1][nm]], axis=axis)

        nca = _prog("p2a", build_p2_attn, L, LC, need_ctx, LAM_INIT[l])
        ncb = _prog("p2b", build_p2_hgrn, L, LC, need_ctx)
        ima, imb = [], []
        for i, (b, r) in enumerate(cores):
            P_ = r1[2 * b]
            kC_all = cat([full(b, "kC", 1), P_["kC_c"]], 1)
            vC_all = cat([full(b, "vC", 0), P_["vC_c"]], 0)
            kA_all = cat([full(b, "kA", 1), P_["kA_c"]], 1)
            vA_all = cat([full(b, "vA", 0), P_["vA_c"]], 0)
            scal = np.zeros((128, 8), f32)
            scal[:, 0] = lamb[:, l]
            scal[:, 1] = np.tile(diff_onorm_g[l], 2)
            scal[:, 2:6] = win_sink[l][4 * r:4 * r + 4][None, :]
            ima.append(dict(
                qC=C_(full(b, "qC", 1)[r * 128:(r + 1) * 128]), qC_c=C_(P_["qC_c"][r * 128:(r + 1) * 128]),
                kC=C_(kC_all[r * 128:(r + 1) * 128]), vCp=v_layout(vC_all[:, r * 128:(r + 1) * 128], 2),
                qA=C_(full(b, "qA", 1)[r * 256:(r + 1) * 256]), qA_c=C_(P_["qA_c"][r * 256:(r + 1) * 256]),
                kA=C_(kA_all[r * 64:(r + 1) * 64]), vAp=C_(v_layout(vA_all[:, r * 64:(r + 1) * 64], 1)[:, :, 0, :]),
                scal=scal, masks=masks, g64=g64))
            hs = slice(r * 128, (r + 1) * 128)
            qB = np.stack([cat([P_["qB%d_c" % d], full(b, "qB%d" % d, 1)], 1)[hs].reshape(2, 64, LC + L) for d in range(2)], 0)
            kB = np.stack([cat([P_["kB%d_c" % d], full(b, "kB%d" % d, 1)], 1)[hs].reshape(2, 64, LC + L) for d in range(2)], 0)
            LFa = cat([P_["LF_c"], full(b, "LF", 1)], 1)[:, :, hs]
            KBa = cat([P_["KB_c"], full(b, "KB", 1)], 1)[:, :, hs]
            vBa = cat([P_["vB_c"], full(b, "vB", 0)], 0)[:, hs]
            gBa = cat([P_["gB_c"], full(b, "gB", 1)], 1)[hs].reshape(2, 64, LC + L)
            LFb_, LFc_, KBc_, vBc_ = hgrn_layouts(C_(LFa), C_(KBa), C_(vBa))
            imb.append(dict(qB=C_(qB), kB=C_(kB), LFb=LFb_, LFc=LFc_, KBc=KBc_, vBc=vBc_, gBf=C_(gBa),
                            ogB=C_(hg_onorm_g[l].reshape(64, 1)), ucum=ucum, lst=lst, tri=tri, g64b=g64))
        ra = _run(nca, ima)
        rb = _run(ncb, imb)

        segT = [TL] + ([LC] if need_ctx else [])
        nc3 = _prog("p3a", build_p3a, segT)
        ims = []
        for i, (b, r) in enumerate(cores):
            oT = cat([ra[2 * b]["oA"], ra[2 * b + 1]["oA"], rb[2 * b]["oB"], rb[2 * b + 1]["oB"],
                      ra[2 * b]["oC"], ra[2 * b + 1]["oC"]], 0)
            m = dict(w_out=w_out[l], mod3=C_(modall[i][l][:, 4:10, :]), cones=cm[0],
                     oT0=C_(oT[:, r * TL:(r + 1) * TL]), xT0=xT[i])
            if need_ctx:
                oTc = cat([ra[2 * b]["oA_c"], ra[2 * b + 1]["oA_c"], rb[2 * b]["oB_c"], rb[2 * b + 1]["oB_c"],
                           ra[2 * b]["oC_c"], ra[2 * b + 1]["oC_c"]], 0)
                m.update(oT1=C_(oTc), xT1=xcT[b])
            ims.append(m)
        r3 = _run(nc3, ims)

        ncf = _prog("ffn", build_ffn, segT)
        cw = C_(conv_w[l].reshape(3, 44, 128).transpose(2, 0, 1))
        cb = C_(conv_b[l].reshape(44, 128).T)
        ims = []
        for i, (b, r) in enumerate(cores):
            h2 = r3[i]["h2T0"]
            z = np.zeros((D, 1), h2.dtype)
            left = z if r == 0 else r3[2 * b]["h2T0"][:, -1:]
            right = z if r == 1 else r3[2 * b + 1]["h2T0"][:, 0:1]
            g2 = modall[i][l][:, 10:12, :] if need_ctx else modall[i][l][:, 10:11, :]
            m = dict(w_up=w_up[l], w_down=w_down[l], cw=cw, cb=cb, g2=C_(g2),
                     h2T0=C_(cat([left, h2, right], 1)), xmT0=r3[i]["xmT0"])
            if need_ctx:
                h2c = r3[i]["h2T1"]
                m.update(h2T1=C_(cat([z, h2c, z], 1)), xmT1=r3[i]["xmT1"])
            ims.append(m)
        rf = _run(ncf, ims)
        xT = [rf[i]["outT0"] for i in range(NCORES)]
        if need_ctx:
            xcT = [rf[2 * b]["outT1"] for b in range(B)]

    out = np.empty((B, L, D), f32)
    for i, (b, r) in enumerate(cores):
        out[b, r * TL:(r + 1) * TL] = xT[i].T
    return out


def build_fused(L, LC, depth=2):
    nc = bass.Bass("TRN2", target_bir_lowering=False)
    p = Prog(nc)
    Tt = LC + L
    KC = 8

    def ext(name, shape, dt, kind="ExternalInput"):
        if not hasattr(nc, "k_io"):
            nc.k_io = []
        nc.k_io.append((name, tuple(shape), dt, kind))
        return nc.dram_tensor(name, list(shape), dt, kind=kind).ap()

    def scr(name, shape, dt):
        return nc.dram_tensor(name, list(shape), dt, kind="Internal").ap()

    E = dict(
        xT=ext("xT", [D, L], F32), xcT=ext("xcT", [D, LC], F32),
        cvec=ext("cvec", [128, KC, 2], F32), w_mod=ext("w_mod", [depth, D, 6 * D], F32),
        b_modT=ext("b_modT", [depth, 128, 48], F32), ngT=ext("ngT", [depth, 128, 2, KC], F32),
        hglT=ext("hglT", [128, depth, 4], F32), hglR=ext("hglR", [1, depth, 512], F32), dlam=ext("dlam", [1, depth, 4, 32], F32),
        w_in=ext("w_in", [depth, D, NIN], F32), w_out=ext("w_out", [depth, D, D], F32),
        w_up=ext("w_up", [depth, D, 2 * DFF], F32), w_down=ext("w_down", [depth, DFF, D], F32),
        cw=ext("cw", [depth, 128, 3, 44], F32), cb=ext("cb", [depth, 128, 44], F32),
        gains=ext("gains", [depth, 128, 4], F32), ropeT=ext("ropeT", [4, 128, L], F32),
        cmat=ext("cmat", [3, 128, 128], F32), rmat=ext("rmat", [2, 128, 128], BF16),
        scal=ext("scal", [depth, 2, 128, 8], F32), masks=ext("masks", [2, 128, 128], BF16),
        g64=ext("g64", [64, 64], F32), ogB=ext("ogB", [depth, 64, 1], F32),
        ucum=ext("ucum", [2, 128, 128], F32), lst=ext("lst", [2, 32, 32], F32), tri=ext("tri", [2, 32, 32], F32),
        outT=ext("outT", [D, L], F32, "ExternalOutput"),
    )
    S = dict(
        modall=scr("s_modall", [depth, 128, 12, KC], F32), oml=scr("s_oml", [128, depth, 4], F32),
        lbrow=scr("s_lbrow", [1, depth, 512], F32), lamb=scr("s_lamb", [128, depth], F32),
        qA=scr("s_qA", [512, Tt], BF16), kA=scr("s_kA", [128, Tt], BF16),
        qB=scr("s_qB", [2, 256, Tt], BF16), kB=scr("s_kB", [2, 256, Tt], BF16), gB=scr("s_gB", [256, Tt], BF16),
        qC=scr("s_qC", [256, Tt], BF16), kC=scr("s_kC", [256, Tt], BF16),
        vA=scr("s_vA", [Tt, 128], BF16), vB=scr("s_vB", [Tt, 256], BF16), vC=scr("s_vC", [Tt, 256], BF16),
        LF=scr("s_LF", [2, Tt, 256], F32), KB=scr("s_KB", [2, Tt, 256], BF16),
        oT=scr("s_oT", [D, Tt], BF16),
        xm=scr("s_xm", [D, L], F32), xmc=scr("s_xmc", [D, LC], F32),
        h2=scr("s_h2", [D, L + 2], BF16), h2c=scr("s_h2c", [D, LC + 2], BF16),
        x1=scr("s_x1", [D, L], F32), xc1=scr("s_xc1", [D, LC], F32),
    )

    with p.scope():
        z = p.sb("zero", [128, 2], BF16)
        p.memset("pool", z[:, :], 0.0, [z])
        for t_, n_ in ((S["h2"], L), (S["h2c"], LC)):
            for col in (0, n_ + 1):
                for kk in range(KC):
                    p.dma("sp", t_[kk * 128:(kk + 1) * 128, col:col + 1], z[:, 0:1], [z], [], slow=True)
        p.bind = dict(cvec=E["cvec"], w_mod=E["w_mod"], b_modT=E["b_modT"], ngT=E["ngT"], hglT=E["hglT"], hglR=E["hglR"],
                      dlam=E["dlam"], modall=S["modall"], oml=S["oml"], lbrow=S["lbrow"], lamb=S["lamb"])
        emit_p0(p)

    x_cur, xc_cur = E["xT"], E["xcT"]
    marks = [("P0", 0)]
    nc.k_marks = marks
    for l in range(depth):
        need_ctx = l < depth - 1
        marks.append(("P1_%d" % l, p.cnt["pe"]))
        with p.scope():
            b = dict(xT=x_cur, xcT=xc_cur, mod=S["modall"][l, :, 0:4, :], w_in=E["w_in"][l], gains=E["gains"][l],
                     lbT=S["oml"][:, l, :], lbrow=S["lbrow"][0, l].rearrange("(d k) -> d k", d=2),
                     ropeT=E["ropeT"], cmat=E["cmat"], rmat=E["rmat"])
            fm = dict(qA=S["qA"], kA=S["kA"], qB0=S["qB"][0], kB0=S["kB"][0], qB1=S["qB"][1], kB1=S["kB"][1],
                      gB=S["gB"], qC=S["qC"], kC=S["kC"])
            for nm, ap in fm.items():
                b[nm] = ap[:, LC:Tt]
                b[nm + "_c"] = ap[:, 0:LC]
            for nm in ("vA", "vB", "vC"):
                b[nm] = S[nm][LC:Tt, :]
                b[nm + "_c"] = S[nm][0:LC, :]
            for nm in ("LF", "KB"):
                b[nm] = S[nm][:, LC:Tt, :]
                b[nm + "_c"] = S[nm][:, 0:LC, :]
            p.bind = b
            emit_p1(p, L, LC)
        for g in range(2):
            marks.append(("attn_%d_%d" % (l, g), p.cnt["pe"]))
            with p.scope():
                p.bind = dict(qC=S["qC"][g * 128:(g + 1) * 128, LC:Tt], qC_c=S["qC"][g * 128:(g + 1) * 128, 0:LC],
                              qA=S["qA"][g * 256:(g + 1) * 256, LC:Tt], qA_c=S["qA"][g * 256:(g + 1) * 256, 0:LC],
                              scal=E["scal"][l, g], masks=E["masks"], g64=E["g64"],
                              oA=S["oT"][g * 256:(g + 1) * 256, LC:Tt], oA_c=S["oT"][g * 256:(g + 1) * 256, 0:LC],
                              oC=S["oT"][768 + g * 128:768 + (g + 1) * 128, LC:Tt],
                              oC_c=S["oT"][768 + g * 128:768 + (g + 1) * 128, 0:LC])
                emit_p2_attn(p, L, LC, need_ctx, LAM_INIT[l], True, True, g,
                             dict(kC=S["kC"], vC=S["vC"], kA=S["kA"], vA=S["vA"], lamcol=S["lamb"][:, l:l + 1]))
        for g in range(2):
            marks.append(("hgrn_%d_%d" % (l, g), p.cnt["pe"]))
            with p.scope():
                p.bind = dict(qB=S["qB"][:, g * 128:(g + 1) * 128, :].rearrange("d (h k) t -> d h k t", h=2),
                              kB=S["kB"][:, g * 128:(g + 1) * 128, :].rearrange("d (h k) t -> d h k t", h=2),
                              gBf=S["gB"][g * 128:(g + 1) * 128, :].rearrange("(h k) t -> h k t", h=2),
                              ogB=E["ogB"][l], ucum=E["ucum"], lst=E["lst"], tri=E["tri"], g64b=E["g64"],
                              oB=S["oT"][512 + g * 128:512 + (g + 1) * 128, LC:Tt],
                              oB_c=S["oT"][512 + g * 128:512 + (g + 1) * 128, 0:LC])
                emit_p2_hgrn(p, L, LC, need_ctx, g, dict(LF=S["LF"], KB=S["KB"], vB=S["vB"]))
        segT = [L] + ([LC] if need_ctx else [])
        marks.append(("p3a_%d" % l, p.cnt["pe"]))
        with p.scope():
            p.bind = dict(w_out=E["w_out"][l], mod3=S["modall"][l, :, 4:10, :], cones=E["cmat"][0],
                          oT0=S["oT"][:, LC:Tt], xT0=x_cur, xmT0=S["xm"], h2T0=S["h2"][:, 1:L + 1],
                          oT1=S["oT"][:, 0:LC], xT1=xc_cur, xmT1=S["xmc"], h2T1=S["h2c"][:, 1:LC + 1])
            emit_p3a(p, segT)
        last = l == depth - 1
        marks.append(("ffn_%d" % l, p.cnt["pe"]))
        with p.scope():
            p.bind = dict(h2T0=S["h2"], xmT0=S["xm"], outT0=(E["outT"] if last else S["x1"]),
                          h2T1=S["h2c"], xmT1=S["xmc"], outT1=S["xc1"],
                          g2=S["modall"][l, :, 10:10 + len(segT), :], w_up=E["w_up"][l], w_down=E["w_down"][l],
                          cw=E["cw"][l], cb=E["cb"][l])
            emit_ffn(p, segT)
        x_cur, xc_cur = S["x1"], S["xc1"]
    p.bind = {}
    p.finish()
    nc.k_ninst = sum(len(v) for v in p.ops.values())
    return nc


def fused_inputs(x, c, ctx, c_ctx, w_mod, b_mod, norm1_g, norm2_g, w_in, win_qnorm_g, win_knorm_g, win_sink,
                 hg_lower, hg_onorm_g, diff_qnorm_g, diff_knorm_g, diff_lambda, diff_onorm_g, w_out,
                 w_up, conv_w, conv_b, w_down):
    f32 = np.float32
    C_ = np.ascontiguousarray
    B, L, _ = x.shape
    depth = w_in.shape[0]
    cm, rm = const_mats()
    ucum, lst, tri = hgrn_consts()
    shared = dict(
        w_mod=w_mod, b_modT=C_(b_mod.reshape(depth, 48, 128).transpose(0, 2, 1)),
        ngT=C_(np.stack([np.stack([_pp(norm1_g[l]), _pp(norm2_g[l])], 1) for l in range(depth)], 0)),
        hglT=C_(hg_lower.reshape(depth, 4, 128).transpose(2, 0, 1)), hglR=C_(hg_lower.reshape(1, depth, 512)),
        dlam=C_(diff_lambda.reshape(1, depth, 4, 32)),
        w_in=w_in, w_out=w_out, w_up=w_up, w_down=w_down,
        cw=C_(conv_w.reshape(depth, 3, 44, 128).transpose(0, 3, 1, 2)), cb=C_(conv_b.reshape(depth, 44, 128).transpose(0, 2, 1)),
        gains=C_(np.stack([np.stack([np.tile(win_qnorm_g[l], 2), np.tile(win_knorm_g[l], 2),
                                     np.tile(diff_qnorm_g[l], 4), np.tile(diff_knorm_g[l], 4)], 1) for l in range(depth)], 0)),
        ropeT=rope_tables(np.arange(L)), cmat=cm, rmat=rm, masks=attn_masks(),
        g64=np.full((64, 64), 1.0 / 64, f32), ogB=C_(hg_onorm_g.reshape(depth, 64, 1)), ucum=ucum, lst=lst, tri=tri,
    )
    scal = np.zeros((depth, 2, 128, 8), f32)
    for l in range(depth):
        for g in range(2):
            scal[l, g, :, 1] = np.tile(diff_onorm_g[l], 2)
            scal[l, g, :, 2:6] = win_sink[l][4 * g:4 * g + 4][None, :]
    shared["scal"] = scal
    ims = []
    for core in range(NCORES):
        b = core // 2
        m = dict(shared)
        m.update(xT=C_(x[b].T), xcT=C_(ctx[b].T), cvec=C_(np.stack([_pp(c[b]), _pp(c_ctx)], -1)))
        ims.append(m)
    return ims


def kernel(**inputs):
    f32 = np.float32
    inp = {k: np.ascontiguousarray(np.asarray(v, dtype=f32)) for k, v in inputs.items()}
    B, L, _ = inp["x"].shape
    LC = inp["ctx"].shape[1]
    nc = _prog("fused", build_fused, L, LC, inp["w_in"].shape[0])
    res = _run(nc, fused_inputs(**inp))
    out = np.empty((B, L, D), f32)
    for b in range(B):
        out[b] = res[2 * b]["outT"].T
    return out
```
